# Optimizing a Trainium2 kernel written in Bass

```python
import math
import jax
import jax.numpy as jnp
from jax import lax
import numpy as np

D_MODEL = 1024
BATCH = 32
SEQ = 256
DEPTH = 2
DEC_BATCH = 2
DEC_SEQ = 4096
PAST_LEN = 512

GRID_W = 64
CHUNK = 64
EPS = 1e-6
H_A = 4
DK_A = 128
DV_A = 128
W_A = H_A * DV_A
QKV_CONV = 3
W_B = 512
GC_B = 16
G_B = W_B // GC_B
P_B = 64
DT_MIN = 1e-3
DT_MAX = 1e-1
H_C = 4
DK_C = 128
DV_C = 128
W_C = H_C * DV_C
ROPE_BASE = 10000.0
ROPE_PAIRS = DK_C // 4
D_FF = 2816
FFN_CONV = 3
N_BRANCH = 3
N_MOD = 6
IN_SIZES = (2 * H_A * DK_A + H_A * DV_A, 2 * H_A, 2 * H_A, W_A, W_B, H_C * DK_C, H_C * DK_C, W_C, W_C, N_BRANCH * D_MODEL)
N_IN = sum(IN_SIZES)

kernel_name = 'hybrid_gdn_s5_retention_prefix_dit'


def rms_normalize(x):
    xf = x.astype(jnp.float32)
    return xf * lax.rsqrt(jnp.mean(xf * xf, axis=-1, keepdims=True) + EPS)


def rmsnorm(x, w):
    return rms_normalize(x) * w


def l2normalize(x):
    xf = x.astype(jnp.float32)
    return xf * lax.rsqrt(jnp.sum(xf * xf, axis=-1, keepdims=True) + EPS)


def flip_seq(t):
    return jnp.flip(t, axis=1)


def centred_dwconv(x, w):
    k = w.shape[0]
    return lax.conv_general_dilated(x, w[:, None, :].astype(x.dtype), window_strides=(1,),
                                    padding=[(k // 2, k // 2)],
                                    dimension_numbers=('NWC', 'WIO', 'NWC'),
                                    feature_group_count=x.shape[-1])


def to_blocks(t):
    b, l, h = t.shape[:3]
    t = t.reshape((b, l // CHUNK, CHUNK, h) + t.shape[3:])
    return jnp.moveaxis(jnp.swapaxes(t, 2, 3), 1, 0)


def from_blocks(o):
    n, b, h, c = o.shape[:4]
    o = jnp.swapaxes(jnp.moveaxis(o, 0, 1), 2, 3)
    return o.reshape((b, n * c, h) + o.shape[4:])


def grid_rope(n_tokens):
    n_rows = n_tokens // GRID_W
    rows = jnp.repeat(jnp.arange(n_rows), GRID_W).astype(jnp.float32)
    cols = jnp.tile(jnp.arange(GRID_W), n_rows).astype(jnp.float32)
    freqs = ROPE_BASE ** (-jnp.arange(ROPE_PAIRS, dtype=jnp.float32) / ROPE_PAIRS)
    ang = jnp.concatenate([rows[:, None] * freqs, cols[:, None] * freqs], axis=-1)
    return jnp.cos(ang)[None, :, None, :], jnp.sin(ang)[None, :, None, :]


def apply_rope(x, cos, sin):
    x1, x2 = x[..., :DK_C // 2], x[..., DK_C // 2:]
    return jnp.concatenate([x1 * cos - x2 * sin, x1 * sin + x2 * cos], axis=-1)


def gated_delta_chunked(q, k, v, g, beta, s0):
    qc, kc, vc = to_blocks(q), to_blocks(k), to_blocks(v)
    gc = jnp.cumsum(to_blocks(g), axis=-1)
    bc = to_blocks(beta)
    idx = jnp.arange(CHUNK)
    lower = idx[:, None] >= idx[None, :]
    strict = idx[:, None] > idx[None, :]
    decay = jnp.exp(jnp.where(lower, gc[..., :, None] - gc[..., None, :], -jnp.inf))
    kb = kc * bc[..., None]
    lmat = jnp.where(strict, jnp.einsum('nbhcd,nbhmd->nbhcm', kb, kc) * decay, 0.0)
    tmat = lmat + jnp.eye(CHUNK, dtype=lmat.dtype)
    u = lax.linalg.triangular_solve(tmat, vc * bc[..., None], left_side=True, lower=True, unit_diagonal=True)
    w = lax.linalg.triangular_solve(tmat, kb * jnp.exp(gc)[..., None], left_side=True, lower=True, unit_diagonal=True)
    qk = jnp.einsum('nbhcd,nbhmd->nbhcm', qc, kc) * decay
    qg = qc * jnp.exp(gc)[..., None]
    kg = kc * jnp.exp(gc[..., -1:] - gc)[..., None]
    glast = jnp.exp(gc[..., -1])

    def step(s, inp):
        qg_i, kg_i, u_i, w_i, qk_i, gl_i = inp
        v_new = u_i - jnp.einsum('bhcd,bhde->bhce', w_i, s)
        o = jnp.einsum('bhcd,bhde->bhce', qg_i, s) + jnp.einsum('bhcm,bhme->bhce', qk_i, v_new)
        s = s * gl_i[..., None, None] + jnp.einsum('bhcd,bhce->bhde', kg_i, v_new)
        return s, o

    s, o = lax.scan(step, s0, (qg, kg, u, w, qk, glast))
    return from_blocks(o), s


def retention_log_decay():
    return jnp.log1p(-jnp.exp2(-5.0 - jnp.arange(H_C, dtype=jnp.float32)))


def retention_chunked(q, k, v, log_gamma, s0):
    qc, kc, vc = to_blocks(q), to_blocks(k), to_blocks(v)
    idx = jnp.arange(CHUNK, dtype=jnp.float32)
    dist = idx[:, None] - idx[None, :]
    lg = log_gamma[:, None, None]
    dmat = jnp.where(dist >= 0, jnp.exp(lg * jnp.maximum(dist, 0.0)), 0.0)
    inner = jnp.einsum('nbhcm,nbhme->nbhce', jnp.einsum('nbhcd,nbhmd->nbhcm', qc, kc) * dmat, vc)
    q_dec = jnp.exp(log_gamma[:, None] * (idx + 1.0))
    k_dec = jnp.exp(log_gamma[:, None] * (CHUNK - 1.0 - idx))
    chunk_dec = jnp.exp(log_gamma * CHUNK)
    qx = qc * q_dec[..., None]
    kx = kc * k_dec[..., None]

    def step(s, inp):
        qx_i, kx_i, v_i = inp
        o = jnp.einsum('bhcd,bhde->bhce', qx_i, s)
        s = s * chunk_dec[:, None, None] + jnp.einsum('bhcd,bhce->bhde', kx_i, v_i)
        return s, o

    s, cross = lax.scan(step, s0, (qx, kx, vc))
    return from_blocks(inner + cross), s


def s5_combine(earlier, later):
    a1r, a1i, b1r, b1i = earlier
    a2r, a2i, b2r, b2i = later
    return (a2r * a1r - a2i * a1i, a2r * a1i + a2i * a1r,
            a2r * b1r - a2i * b1i + b2r, a2r * b1i + a2i * b1r + b2i)


def s5_scan(in_re, in_im, a_re, a_im, x0_re, x0_im):
    in_re = in_re.at[:, 0].add(a_re * x0_re - a_im * x0_im)
    in_im = in_im.at[:, 0].add(a_re * x0_im + a_im * x0_re)
    ar = jnp.broadcast_to(a_re, in_re.shape)
    ai = jnp.broadcast_to(a_im, in_im.shape)
    _, _, xr, xi = lax.associative_scan(s5_combine, (ar, ai, in_re, in_im), axis=1)
    return xr, xi


def s5_branch(u, lp, x0_re, x0_im):
    b, l, _ = u.shape
    uf = u.astype(jnp.float32)
    ug = uf.reshape(b, l, G_B, GC_B)
    bu_re = jnp.einsum('blgc,gpc->blgp', ug, lp['ssm_b_re'])
    bu_im = jnp.einsum('blgc,gpc->blgp', ug, lp['ssm_b_im'])
    y = uf * lp['ssm_d']
    fin_re, fin_im = [], []
    for d in range(2):
        lam_re = lp['ssm_lam_re'][d].astype(jnp.float32)
        lam_im = lp['ssm_lam_im'][d].astype(jnp.float32)
        step = jnp.exp(lp['ssm_log_dt'][d].astype(jnp.float32))[:, None]
        mag = jnp.exp(lam_re * step)
        ang = lam_im * step
        a_re = mag * jnp.cos(ang)
        a_im = mag * jnp.sin(ang)
        den = lam_re * lam_re + lam_im * lam_im
        z_re = ((a_re - 1.0) * lam_re + a_im * lam_im) / den
        z_im = (a_im * lam_re - (a_re - 1.0) * lam_im) / den
        in_re = z_re * bu_re - z_im * bu_im
        in_im = z_re * bu_im + z_im * bu_re
        if d == 1:
            in_re, in_im = flip_seq(in_re), flip_seq(in_im)
        xr, xi = s5_scan(in_re, in_im, a_re, a_im, x0_re[:, d], x0_im[:, d])
        fin_re.append(xr[:, -1])
        fin_im.append(xi[:, -1])
        yd = (jnp.einsum('blgp,gcp->blgc', xr, lp['ssm_c_re'])
              - jnp.einsum('blgp,gcp->blgc', xi, lp['ssm_c_im']))
        if d == 1:
            yd = flip_seq(yd)
        y = y + yd.reshape(b, l, W_B)
    y = jax.nn.gelu(y)
    y = y * jax.nn.sigmoid(y @ lp['w_glu'] + lp['b_glu'])
    return y, jnp.stack(fin_re, axis=1), jnp.stack(fin_im, axis=1)


def token_mixers(h, lp, rope, s_delta, s_re, s_im, s_ret):
    b, l, _ = h.shape
    proj = h @ lp['w_in']
    splits = [int(s) for s in np.cumsum(IN_SIZES)[:-1]]
    qkv_a, alpha_a, beta_a, z_a, u_b, q_c, k_c, v_c, g_c, gates = jnp.split(proj, splits, axis=-1)

    s_delta = s_delta.astype(jnp.float32)
    qkv = jax.nn.silu(centred_dwconv(qkv_a, lp['w_conv_qkv']))
    qa, ka, va = jnp.split(qkv, [H_A * DK_A, 2 * H_A * DK_A], axis=-1)
    qa = l2normalize(qa.reshape(b, l, H_A, DK_A)) * (DK_A ** -0.5)
    ka = l2normalize(ka.reshape(b, l, H_A, DK_A))
    va = va.reshape(b, l, H_A, DV_A).astype(jnp.float32)
    log_a = -jnp.exp(lp['a_log']) * jax.nn.softplus(alpha_a.reshape(b, l, 2, H_A).astype(jnp.float32) + lp['dt_bias'])
    bt = jax.nn.sigmoid(beta_a.reshape(b, l, 2, H_A).astype(jnp.float32))
    o_f, sd_f = gated_delta_chunked(qa, ka, va, log_a[:, :, 0], bt[:, :, 0], s_delta[:, 0])
    o_b, sd_b = gated_delta_chunked(flip_seq(qa), flip_seq(ka), flip_seq(va), flip_seq(log_a[:, :, 1]),
                                    flip_seq(bt[:, :, 1]), s_delta[:, 1])
    o_a = rmsnorm(o_f + flip_seq(o_b), lp['norm_a']) * jax.nn.silu(z_a.reshape(b, l, H_A, DV_A).astype(jnp.float32))
    br_a = o_a.reshape(b, l, W_A) @ lp['w_br_a']

    y_b, ss_re, ss_im = s5_branch(u_b, lp, s_re.astype(jnp.float32), s_im.astype(jnp.float32))
    br_b = y_b @ lp['w_br_b']

    s_ret = s_ret.astype(jnp.float32)
    qr = q_c.reshape(b, l, H_C, DK_C).astype(jnp.float32)
    kr = k_c.reshape(b, l, H_C, DK_C).astype(jnp.float32)
    if rope is not None:
        qr = apply_rope(qr, rope[0], rope[1])
        kr = apply_rope(kr, rope[0], rope[1])
    kr = kr * (DK_C ** -0.5)
    vr = v_c.reshape(b, l, H_C, DV_C).astype(jnp.float32)
    lg = retention_log_decay()
    r_f, sr_f = retention_chunked(qr, kr, vr, lg, s_ret[:, 0])
    r_b, sr_b = retention_chunked(flip_seq(qr), flip_seq(kr), flip_seq(vr), lg[::-1], s_ret[:, 1])
    o_c = rms_normalize(r_f + flip_seq(r_b)) * jax.nn.silu(g_c.reshape(b, l, H_C, DV_C).astype(jnp.float32))
    br_c = o_c.reshape(b, l, W_C) @ lp['w_br_c']

    gt = jax.nn.sigmoid(gates.astype(jnp.float32)).reshape(b, l, N_BRANCH, D_MODEL)
    merged = gt[:, :, 0] * br_a + gt[:, :, 1] * br_b + gt[:, :, 2] * br_c
    out = merged @ lp['w_o']
    new_states = (jnp.stack([sd_f, sd_b], axis=1), ss_re, ss_im, jnp.stack([sr_f, sr_b], axis=1))
    return out, new_states


def conv_glu_ffn(h, lp):
    hu = centred_dwconv(h @ lp['w_up'], lp['w_conv_ffn']) + lp['b_conv_ffn']
    gate_part, value_part = jnp.split(hu, 2, axis=-1)
    return (jax.nn.silu(gate_part) * value_part) @ lp['w_down']


def trunk_layer(x, cond, lp, rope, states):
    mod = (jax.nn.silu(cond.astype(jnp.float32)) @ lp['w_mod'] + lp['b_mod'])[:, None, :]
    sh1, sc1, g1, sh2, sc2, g2 = jnp.split(mod, N_MOD, axis=-1)
    h = rmsnorm(x, lp['norm1']) * (1.0 + sc1) + sh1
    mix, new_states = token_mixers(h, lp, rope, states[0], states[1], states[2], states[3])
    x = x + g1 * mix
    h = rmsnorm(x, lp['norm2']) * (1.0 + sc2) + sh2
    x = x + g2 * conv_glu_ffn(h, lp)
    return x, new_states


def setup_inputs(seed: int = 0) -> dict:
    key = jax.random.key(seed)
    ks = iter(jax.random.split(key, 64))
    f32 = jnp.float32

    def nrm(shape, scale):
        return scale * jax.random.normal(next(ks), shape, f32)

    def unif(shape, lo, hi):
        return jax.random.uniform(next(ks), shape, f32, lo, hi)

    x_prompt = nrm((BATCH, SEQ, D_MODEL), 1.0)
    x_sample = nrm((DEC_BATCH, DEC_SEQ, D_MODEL), 1.0)
    state_delta = nrm((DEC_BATCH, DEPTH, 2, H_A, DK_A, DV_A), 0.05)
    state_ssm_re = nrm((DEC_BATCH, DEPTH, 2, G_B, P_B), 0.1)
    state_ssm_im = nrm((DEC_BATCH, DEPTH, 2, G_B, P_B), 0.1)
    state_ret = nrm((DEC_BATCH, DEPTH, 2, H_C, DK_C, DV_C), 0.3)
    c = nrm((DEC_BATCH, D_MODEL), 1.0)
    c_ctx = nrm((D_MODEL,), 1.0)
    final_norm = 1.0 + nrm((D_MODEL,), 0.02)
    norm1 = 1.0 + nrm((DEPTH, D_MODEL), 0.02)
    norm2 = 1.0 + nrm((DEPTH, D_MODEL), 0.02)
    w_mod = nrm((DEPTH, D_MODEL, N_MOD * D_MODEL), D_MODEL ** -0.5)
    b_mod = nrm((DEPTH, N_MOD * D_MODEL), 0.02)
    w_in = nrm((DEPTH, D_MODEL, N_IN), D_MODEL ** -0.5)
    w_conv_qkv = nrm((DEPTH, QKV_CONV, IN_SIZES[0]), QKV_CONV ** -0.5)
    a_log = jnp.log(unif((DEPTH, 2, H_A), 1.0, 16.0))
    dt0 = jnp.exp(unif((DEPTH, 2, H_A), math.log(DT_MIN), math.log(DT_MAX)))
    dt_bias = dt0 + jnp.log(-jnp.expm1(-dt0))
    norm_a = 1.0 + nrm((DEPTH, DV_A), 0.02)
    w_br_a = nrm((DEPTH, W_A, D_MODEL), W_A ** -0.5)
    ssm_lam_re = -0.5 + nrm((DEPTH, 2, G_B, P_B), 0.01)
    ssm_lam_im = math.pi * jnp.arange(P_B, dtype=f32) + nrm((DEPTH, 2, G_B, P_B), 0.01)
    ssm_log_dt = unif((DEPTH, 2, G_B), math.log(DT_MIN), math.log(DT_MAX))
    ssm_b_re = nrm((DEPTH, G_B, P_B, GC_B), (2 * GC_B) ** -0.5)
    ssm_b_im = nrm((DEPTH, G_B, P_B, GC_B), (2 * GC_B) ** -0.5)
    ssm_c_re = nrm((DEPTH, G_B, GC_B, P_B), P_B ** -0.5)
    ssm_c_im = nrm((DEPTH, G_B, GC_B, P_B), P_B ** -0.5)
    ssm_d = nrm((DEPTH, W_B), 1.0)
    w_glu = nrm((DEPTH, W_B, W_B), W_B ** -0.5)
    b_glu = nrm((DEPTH, W_B), 0.02)
    w_br_b = nrm((DEPTH, W_B, D_MODEL), W_B ** -0.5)
    w_br_c = nrm((DEPTH, W_C, D_MODEL), W_C ** -0.5)
    w_o = nrm((DEPTH, D_MODEL, D_MODEL), D_MODEL ** -0.5)
    w_up = nrm((DEPTH, D_MODEL, 2 * D_FF), D_MODEL ** -0.5)
    w_conv_ffn = nrm((DEPTH, FFN_CONV, 2 * D_FF), FFN_CONV ** -0.5)
    b_conv_ffn = nrm((DEPTH, 2 * D_FF), 0.02)
    w_down = nrm((DEPTH, D_FF, D_MODEL), D_FF ** -0.5)
    return {'x_prompt': x_prompt, 'x_sample': x_sample, 'state_delta': state_delta,
            'state_ssm_re': state_ssm_re, 'state_ssm_im': state_ssm_im, 'state_ret': state_ret,
            'c': c, 'c_ctx': c_ctx, 'final_norm': final_norm, 'norm1': norm1, 'norm2': norm2,
            'w_mod': w_mod, 'b_mod': b_mod, 'w_in': w_in, 'w_conv_qkv': w_conv_qkv, 'a_log': a_log,
            'dt_bias': dt_bias, 'norm_a': norm_a, 'w_br_a': w_br_a, 'ssm_lam_re': ssm_lam_re,
            'ssm_lam_im': ssm_lam_im, 'ssm_log_dt': ssm_log_dt, 'ssm_b_re': ssm_b_re, 'ssm_b_im': ssm_b_im,
            'ssm_c_re': ssm_c_re, 'ssm_c_im': ssm_c_im, 'ssm_d': ssm_d, 'w_glu': w_glu, 'b_glu': b_glu,
            'w_br_b': w_br_b, 'w_br_c': w_br_c, 'w_o': w_o, 'w_up': w_up, 'w_conv_ffn': w_conv_ffn,
            'b_conv_ffn': b_conv_ffn, 'w_down': w_down}


def reference(x_prompt, x_sample, state_delta, state_ssm_re, state_ssm_im, state_ret, c, c_ctx,
              final_norm, norm1, norm2, w_mod, b_mod, w_in, w_conv_qkv, a_log, dt_bias, norm_a, w_br_a,
              ssm_lam_re, ssm_lam_im, ssm_log_dt, ssm_b_re, ssm_b_im, ssm_c_re, ssm_c_im, ssm_d,
              w_glu, b_glu, w_br_b, w_br_c, w_o, w_up, w_conv_ffn, b_conv_ffn, w_down):
    layers = [dict(norm1=norm1[i], norm2=norm2[i], w_mod=w_mod[i], b_mod=b_mod[i], w_in=w_in[i],
                   w_conv_qkv=w_conv_qkv[i], a_log=a_log[i], dt_bias=dt_bias[i], norm_a=norm_a[i],
                   w_br_a=w_br_a[i], ssm_lam_re=ssm_lam_re[i], ssm_lam_im=ssm_lam_im[i],
                   ssm_log_dt=ssm_log_dt[i], ssm_b_re=ssm_b_re[i], ssm_b_im=ssm_b_im[i],
                   ssm_c_re=ssm_c_re[i], ssm_c_im=ssm_c_im[i], ssm_d=ssm_d[i], w_glu=w_glu[i],
                   b_glu=b_glu[i], w_br_b=w_br_b[i], w_br_c=w_br_c[i], w_o=w_o[i], w_up=w_up[i],
                   w_conv_ffn=w_conv_ffn[i], b_conv_ffn=b_conv_ffn[i], w_down=w_down[i])
              for i in range(DEPTH)]

    bp = x_prompt.shape[0]
    zero_states = (jnp.zeros((bp, 2, H_A, DK_A, DV_A), jnp.float32),
                   jnp.zeros((bp, 2, G_B, P_B), jnp.float32),
                   jnp.zeros((bp, 2, G_B, P_B), jnp.float32),
                   jnp.zeros((bp, 2, H_C, DK_C, DV_C), jnp.float32))
    cond_ctx = c_ctx[None, :]
    xp = x_prompt
    ctx_states = []
    for i in range(DEPTH):
        xp, st = trunk_layer(xp, cond_ctx, layers[i], None, zero_states)
        ctx_states.append(st)
    y_prompt = rmsnorm(xp, final_norm)

    rope = grid_rope(x_sample.shape[1])
    xs = x_sample
    for i in range(DEPTH):
        xs, _ = trunk_layer(xs, c, layers[i], rope,
                            (state_delta[:, i], state_ssm_re[:, i], state_ssm_im[:, i], state_ret[:, i]))
    y_sample = rmsnorm(xs, final_norm)

    new_state_delta = jnp.stack([st[0] for st in ctx_states], axis=1)
    new_state_ssm_re = jnp.stack([st[1] for st in ctx_states], axis=1)
    new_state_ssm_im = jnp.stack([st[2] for st in ctx_states], axis=1)
    new_state_ret = jnp.stack([st[3] for st in ctx_states], axis=1)
    return (y_prompt, y_sample, new_state_delta, new_state_ssm_re, new_state_ssm_im, new_state_ret)
```

```python
import math
from contextlib import ExitStack

import numpy as np
import concourse.bass as bass
import concourse.mybir as mybir
from concourse.bass_utils import run_bass_kernel_spmd

F32 = mybir.dt.float32
BF16 = mybir.dt.bfloat16
AF = mybir.ActivationFunctionType
ALU = mybir.AluOpType

D = 1024
DEPTH = 2
NIN = 7696
DFF = 2816
EPS = 1e-6
NSEQ_P = 4
LP = 256
LS = 4096
TTOT = NSEQ_P * LP + LS
SEQS = [(i * LP, LP, 0) for i in range(NSEQ_P)] + [(NSEQ_P * LP, LS, 1)]
O_QKV, O_AL, O_BE, O_Z, O_U, O_QC, O_KC, O_VC, O_GC, O_GT = 0, 1536, 1544, 1552, 2064, 2576, 3088, 3600, 4112, 4624


class T:
    __slots__ = ("ap", "key")

    def __init__(self, ap, key):
        self.ap = ap
        self.key = key

    def __getitem__(self, idx):
        return self.ap[idx]


class Prog:
    NSLOT = 12

    def __init__(self, nc):
        self.nc = nc
        self.E = {"pe": nc.tensor, "dve": nc.vector, "act": nc.scalar, "pool": nc.gpsimd, "sp": nc.sync}
        self.sem = {e: nc.alloc_semaphore("sem_" + e) for e in ("pe", "dve", "act", "pool")}
        self.cnt = {e: 0 for e in self.sem}
        self.semid = {}
        self.waited = {}
        self.W = {}
        self.R = {}
        self.QS = ("sp", "pool", "actq")
        self.dsem = {q: [nc.alloc_semaphore("d%s%d" % (q, i)) for i in range(self.NSLOT)] for q in self.QS}
        self.dval = {q: [0] * self.NSLOT for q in self.QS}
        self.dnext = {q: 0 for q in self.QS}
        self.E["actq"] = nc.scalar
        self.ninst = 0
        self._uid = 0

    def uid(self, s):
        self._uid += 1
        return "%s_%d" % (s, self._uid)

    def _wait(self, eng, name, sem, val):
        if eng == "pe" and name == "pe":
            return
        k = (eng, name)
        if self.waited.get(k, 0) >= val:
            return
        self.E[eng].wait_ge(sem, val)
        self.waited[k] = val

    def _deps(self, eng, reads, writes):
        need = {}
        for b in reads:
            for n, (s, v) in self.W.get(b.key, {}).items():
                if need.get(n, (None, 0))[1] < v:
                    need[n] = (s, v)
        for b in writes:
            for dct in (self.W.get(b.key, {}), self.R.get(b.key, {})):
                for n, (s, v) in dct.items():
                    if need.get(n, (None, 0))[1] < v:
                        need[n] = (s, v)
        for n, (s, v) in need.items():
            self._wait(eng, n, s, v)

    def _mark(self, name, sem, val, reads, writes):
        for b in writes:
            self.W.setdefault(b.key, {})[name] = (sem, val)
        for b in reads:
            self.R.setdefault(b.key, {})[name] = (sem, val)

    def op(self, eng, fn, reads=(), writes=()):
        self._deps(eng, reads, writes)
        inst = fn(self.E[eng])
        self.cnt[eng] += 1
        inst.then_inc(self.sem[eng], 1)
        self._mark(eng, self.sem[eng], self.cnt[eng], reads, writes)
        self.ninst += 1

    def dma(self, q, out, in_, reads=(), writes=(), **kw):
        weng = "act" if q == "actq" else q
        self._deps(weng, reads, writes)
        i = self.dnext[q]
        self.dnext[q] = (i + 1) % self.NSLOT
        sem = self.dsem[q][i]
        name = "d%s%d" % (q, i)
        if self.dval[q][i] > 0:
            self._wait(weng, name, sem, self.dval[q][i])
        inst = self.E[q].dma_start(out=out, in_=in_, **kw)
        self.dval[q][i] += 16
        inst.then_inc(sem, 16)
        self._mark(name, sem, self.dval[q][i], reads, writes)
        self.ninst += 1

    def barrier(self):
        for eng in ("pe", "dve", "act", "pool", "sp"):
            for e2 in ("pe", "dve", "act", "pool"):
                if e2 != eng and self.cnt[e2] > 0:
                    self._wait(eng, e2, self.sem[e2], self.cnt[e2])
            for q in self.QS:
                for i in range(self.NSLOT):
                    if self.dval[q][i] > 0:
                        self._wait(eng, "d%s%d" % (q, i), self.dsem[q][i], self.dval[q][i])
        self.W.clear()
        self.R.clear()

    def finish(self):
        for q in self.QS:
            for i in range(self.NSLOT):
                if self.dval[q][i] > 0:
                    self.E["sp"].wait_ge(self.dsem[q][i], self.dval[q][i])


class Ctx:
    def __init__(self, p, es):
        self.p = p
        self.es = es
        self.nc = p.nc

    def sb(self, shape, dt, name):
        nm = self.p.uid(name)
        t = self.es.enter_context(self.nc.sbuf_tensor(nm, list(shape), dt))
        return T(t.ap() if hasattr(t, "ap") and callable(getattr(t, "ap")) else t, nm)


def segs(n, maxw=512):
    k = (n + maxw - 1) // maxw
    base = n // k
    rem = n % k
    out = []
    c = 0
    for i in range(k):
        w = base + (1 if i < rem else 0)
        out.append((c, w))
        c += w
    return out


def build(debug=(), dbg_opts=None):
    nc = bass.Bass("TRN2", target_bir_lowering=False)
    p = Prog(nc)
    dbg = set(debug)
    dbg_opts = dbg_opts or {}
    _ncd = nc.allow_non_contiguous_dma(reason="small strided parameter / layout DMAs")
    _ncd.__enter__()

    def din(name, shape, dt=F32):
        return nc.dram_tensor(name, list(shape), dt, kind="ExternalInput").ap()

    def dout(name, shape, dt=F32):
        return nc.dram_tensor(name, list(shape), dt, kind="ExternalOutput").ap()

    def dscr(name, shape, dt):
        kind = "ExternalOutput" if name in dbg else "Internal"
        return T(nc.dram_tensor(name, list(shape), dt, kind=kind).ap(), "dram_" + name)

    xin = T(din("xin", [TTOT, D]), "in_x")
    cond = din("cond", [2, D])
    st_delta = din("st_delta", [DEPTH, 2, 4, 128, 128])
    st_re = din("st_re", [DEPTH, 2, 32, 64])
    st_im = din("st_im", [DEPTH, 2, 32, 64])
    st_ret = din("st_ret", [DEPTH, 2, 4, 128, 128])
    final_norm = din("final_norm", [D])
    norm1 = din("norm1", [DEPTH, D])
    norm2 = din("norm2", [DEPTH, D])
    w_mod = din("w_mod", [DEPTH, D, 6 * D])
    b_mod = din("b_mod", [DEPTH, 6 * D])
    w_in = din("w_in", [DEPTH, D, NIN])
    w_conv_qkv = din("w_conv_qkv", [DEPTH, 3, 1536])
    a_log = din("a_log", [DEPTH, 8])
    dt_bias = din("dt_bias", [DEPTH, 8])
    norm_a = din("norm_a", [DEPTH, 128])
    w_br_a = din("w_br_a", [DEPTH, 512, D])
    lam_re = din("ssm_lam_re", [DEPTH, 2, 32, 64])
    lam_im = din("ssm_lam_im", [DEPTH, 2, 32, 64])
    log_dt = din("ssm_log_dt", [DEPTH, 2, 32])
    b_re = din("ssm_b_re", [DEPTH, 32, 64, 16])
    b_im = din("ssm_b_im", [DEPTH, 32, 64, 16])
    c_re = din("ssm_c_re", [DEPTH, 32, 16, 64])
    c_im = din("ssm_c_im", [DEPTH, 32, 16, 64])
    ssm_d = din("ssm_d", [DEPTH, 512])
    w_glu = din("w_glu", [DEPTH, 512, 512])
    b_glu = din("b_glu", [DEPTH, 512])
    w_br_b = din("w_br_b", [DEPTH, 512, D])
    w_br_c = din("w_br_c", [DEPTH, 512, D])
    w_o = din("w_o", [DEPTH, D, D])
    w_up = din("w_up", [DEPTH, D, 2 * DFF])
    w_conv_ffn = din("w_conv_ffn", [DEPTH, 3, 2 * DFF])
    b_conv_ffn = din("b_conv_ffn", [DEPTH, 2 * DFF])
    w_down = din("w_down", [DEPTH, DFF, D])
    c_ident = din("c_ident", [128, 128])
    c_rope = din("c_rope", [2, 128, LS])
    c_rdt = din("c_rdt", [128, 4, 128])
    c_rqd = din("c_rqd", [128, 2, 4, 128])
    c_rkd = din("c_rkd", [128, 2, 512])
    c_ut = din("c_ut", [128, 2, 128])
    c_nm = din("c_nm", [128, 4, 128])
    c_sel = din("c_sel", [4, 4, 128])
    c_nlm = din("c_nlm", [128, 2, 7, 128])
    c_kv513 = din("c_kv513", [128, 513])
    c_kv16 = din("c_kv16", [2, 16])
    c_mfb = din("c_mfb", [128, 2, 128])
    c_swp = din("c_swp", [128, 128])

    yout = T(dout("yout", [TTOT, D]), "out_y")
    ns_delta = T(dout("ns_delta", [NSEQ_P, DEPTH, 2, 4, 128, 128]), "out_nsd")
    ns_re = T(dout("ns_re", [NSEQ_P, DEPTH, 2, 32, 64]), "out_nsre")
    ns_im = T(dout("ns_im", [NSEQ_P, DEPTH, 2, 32, 64]), "out_nsim")
    ns_ret = T(dout("ns_ret", [NSEQ_P, DEPTH, 2, 4, 128, 128]), "out_nsr")

    XS = [dscr("xs0", [D, TTOT], F32), dscr("xs1", [D, TTOT], F32)]
    WB_in = dscr("wb_in", [DEPTH, D, NIN], BF16)
    WB_rot = dscr("wb_rot", [DEPTH, D, 1024], BF16)
    WB_up = dscr("wb_up", [DEPTH, D, 2 * DFF], BF16)
    WB_down = dscr("wb_down", [DEPTH, DFF, D], BF16)
    WB_o = dscr("wb_o", [DEPTH, D, D], BF16)
    WB_bra = dscr("wb_bra", [DEPTH, 512, D], BF16)
    WB_brb = dscr("wb_brb", [DEPTH, 512, D], BF16)
    WB_brc = dscr("wb_brc", [DEPTH, 512, D], BF16)
    WB_glu = dscr("wb_glu", [DEPTH, 512, 512], BF16)
    QA = dscr("qa", [512, TTOT], BF16)
    KA = dscr("ka", [512, TTOT], BF16)
    KAT = dscr("kat", [TTOT, 512], BF16)
    VAT = dscr("vat", [TTOT, 512], BF16)
    GA = dscr("ga", [TTOT, 8], F32)
    BA = dscr("ba", [TTOT, 8], F32)
    ZA = dscr("za", [512, TTOT], BF16)
    UB = dscr("ub", [8, 16, 32, TTOT // 8], BF16)
    QC = dscr("qc", [512, TTOT], BF16)
    KC = dscr("kc", [512, TTOT], BF16)
    KCT = dscr("kct", [TTOT, 512], BF16)
    VCT = dscr("vct", [TTOT, 512], BF16)
    GC = dscr("gc", [512, TTOT], BF16)
    GT = dscr("gt", [3072, TTOT], BF16)
    OA = dscr("oa", [512, TTOT], BF16)
    YB = dscr("yb", [512, TTOT], BF16)
    OC = dscr("oc", [512, TTOT], BF16)

    es0 = ExitStack()
    g = Ctx(p, es0)
    ident = g.sb([128, 128], F32, "ident")
    identb = g.sb([128, 128], BF16, "identb")
    ones = g.sb([128, 128], F32, "ones")
    p.dma("sp", ident.ap, c_ident, writes=[ident])
    p.op("dve", lambda e: e.tensor_copy(out=identb.ap, in_=ident.ap), reads=[ident], writes=[identb])
    p.op("dve", lambda e: e.memset(ones.ap, 1.0), writes=[ones])
    onesb = g.sb([128, 128], BF16, "onesb")
    p.op("dve", lambda e: e.memset(onesb.ap, 1.0), writes=[onesb])

    psum = [T(nc.alloc_psum_tensor("ps%d" % i, [128, 512], F32).ap(), "ps%d" % i) for i in range(8)]
    pstate = {"i": 0, "n": 8}

    def ps():
        pstate["i"] = pstate["i"] % pstate["n"]
        t = psum[pstate["i"]]
        pstate["i"] = (pstate["i"] + 1) % pstate["n"]
        return t

    def stage_input():
        with ExitStack() as es:
            c = Ctx(p, es)
            xt = [c.sb([128, D], F32, "xt") for _ in range(3)]
            stg = [c.sb([128, 8, 512], F32, "xstg") for _ in range(2)]
            xs_v = XS[0].ap.rearrange("(k q) t -> q k t", q=128)
            for blk in range(TTOT // 512):
                sg = stg[blk % 2]
                for tt in range(4):
                    tok0 = blk * 512 + tt * 128
                    x = xt[(blk * 4 + tt) % 3]
                    p.dma("sp", x.ap, xin.ap[tok0:tok0 + 128, :], reads=[xin], writes=[x])
                    for k2 in range(2):
                        pt = ps()
                        for kk in range(4):
                            k = k2 * 4 + kk
                            p.op("pe", lambda e, pt=pt, kk=kk, k=k, x=x: e.transpose(
                                out=pt.ap[:, kk * 128:(kk + 1) * 128], in_=x.ap[:, k * 128:(k + 1) * 128],
                                identity=ident.ap), reads=[x, ident], writes=[pt])
                        eng = "act" if k2 == 0 else "dve"
                        if eng == "act":
                            p.op("act", lambda e, pt=pt, k2=k2, tt=tt, sg=sg: e.copy(
                                out=sg.ap[:, k2 * 4:(k2 + 1) * 4, tt * 128:(tt + 1) * 128],
                                in_=pt.ap.rearrange("q (k t) -> q k t", k=4)), reads=[pt], writes=[sg])
                        else:
                            p.op("dve", lambda e, pt=pt, k2=k2, tt=tt, sg=sg: e.tensor_copy(
                                out=sg.ap[:, k2 * 4:(k2 + 1) * 4, tt * 128:(tt + 1) * 128],
                                in_=pt.ap.rearrange("q (k t) -> q k t", k=4)), reads=[pt], writes=[sg])
                p.dma("pool", xs_v[:, :, blk * 512:(blk + 1) * 512], sg.ap, reads=[sg], writes=[XS[0]])
        p.barrier()

    def cast_gen(c, layers, CW=2048, nbuf=3, lq="sp"):
        src = [c.sb([128, CW], F32, "csrc") for _ in range(nbuf)]
        dst = [c.sb([128, CW], BF16, "cdst") for _ in range(nbuf)]
        return _cast_gen(src, dst, layers, CW, nbuf, lq)

    def _cast_gen(src, dst, layers, CW, nbuf, lq):
        st = {"i": 0}
        engs = ["act", "dve", "pool"]

        def cast2d(src_ap, dst_t, nrows, ncols):
            for r0 in range(0, nrows, 128):
                for c0 in range(0, ncols, CW):
                    w = min(CW, ncols - c0)
                    i = st["i"]
                    st["i"] += 1
                    s_, d = src[i % nbuf], dst[i % nbuf]
                    p.dma(lq, s_.ap[:, :w], src_ap[r0:r0 + 128, c0:c0 + w], writes=[s_])
                    eng = engs[i % 3]
                    if eng == "act":
                        p.op("act", lambda e, s_=s_, d=d, w=w: e.copy(out=d.ap[:, :w], in_=s_.ap[:, :w]), reads=[s_], writes=[d])
                    else:
                        p.op(eng, lambda e, s_=s_, d=d, w=w: e.tensor_copy(out=d.ap[:, :w], in_=s_.ap[:, :w]), reads=[s_], writes=[d])
                    p.dma("pool", dst_t.ap[r0:r0 + 128, c0:c0 + w], d.ap[:, :w], reads=[d], writes=[dst_t])
                    yield

        for l in layers:
            yield from cast2d(w_in[l], T(WB_in.ap[l], WB_in.key), D, NIN)
            for part, off in ((0, O_QC), (1, O_KC)):
                for r0 in range(0, D, 128):
                    i = st["i"]
                    st["i"] += 1
                    s_, d = src[i % nbuf], dst[i % nbuf]
                    p.dma(lq, s_.ap[:, :512], w_in[l][r0:r0 + 128, off:off + 512], writes=[s_])
                    sv = s_.ap[:, :512].rearrange("q (h two j) -> q h two j", h=4, two=2)
                    dv = d.ap[:, :512].rearrange("q (h two j) -> q h two j", h=4, two=2)
                    p.op("act", lambda e, sv=sv, dv=dv: e.mul(out=dv[:, :, 0, :], in_=sv[:, :, 1, :], mul=-1.0), reads=[s_], writes=[d])
                    p.op("dve", lambda e, sv=sv, dv=dv: e.tensor_copy(out=dv[:, :, 1, :], in_=sv[:, :, 0, :]), reads=[s_], writes=[d])
                    p.dma("pool", WB_rot.ap[l][r0:r0 + 128, part * 512:(part + 1) * 512], d.ap[:, :512], reads=[d], writes=[WB_rot])
                    yield
            yield from cast2d(w_up[l], T(WB_up.ap[l], WB_up.key), D, 2 * DFF)
            yield from cast2d(w_down[l], T(WB_down.ap[l], WB_down.key), DFF, D)
            yield from cast2d(w_o[l], T(WB_o.ap[l], WB_o.key), D, D)
            yield from cast2d(w_br_a[l], T(WB_bra.ap[l], WB_bra.key), 512, D)
            yield from cast2d(w_br_b[l], T(WB_brb.ap[l], WB_brb.key), 512, D)
            yield from cast2d(w_br_c[l], T(WB_brc.ap[l], WB_brc.key), 512, D)
            yield from cast2d(w_glu[l], T(WB_glu.ap[l], WB_glu.key), 512, 512)

    def stage_cast(layers):
        with ExitStack() as es:
            c = Ctx(p, es)
            for _ in cast_gen(c, layers):
                pass
        p.barrier()

    LATE_CAST = dbg_opts.get("nlayers", DEPTH) > 1 and not dbg_opts.get("no_late_cast")
    stage_input()
    stage_cast([0] if LATE_CAST else list(range(DEPTH)))

    xs_views = [x.ap.rearrange("(k q) t -> q k t", q=128) for x in XS]

    BLOCKS = [dict(cond=0, rope=False, stok=0, pieces=[(s_ * LP, LP, False, False) for s_ in range(NSEQ_P)])]
    for b_ in range(4):
        BLOCKS.append(dict(cond=1, rope=True, stok=b_ * 1024,
                           pieces=[(NSEQ_P * LP + b_ * 1024, 1024, b_ > 0, b_ < 3)]))
    for B in BLOCKS:
        c0 = 0
        B["c0"] = []
        for (tok0, n, lv, rv) in B["pieces"]:
            B["c0"].append(c0)
            c0 += n + 2
        B["ncol"] = c0
        B["tok_lo"] = B["pieces"][0][0]
    NCOLMAX = max(B["ncol"] for B in BLOCKS)
    if "only_blocks" in dbg_opts:
        BLOCKS = [BLOCKS[i] for i in dbg_opts["only_blocks"]]

    def stage_mod(l, c):
        modT = c.sb([128, 48, 2], F32, "modT")
        A1 = c.sb([128, 8, 2], F32, "A1")
        A2 = c.sb([128, 8, 2], F32, "A2")
        with ExitStack() as es:
            t = Ctx(p, es)
            condT = t.sb([128, 8, 2], F32, "condT")
            scond = t.sb([128, 8, 2], F32, "scond")
            bm = t.sb([128, 48], F32, "bm")
            n1 = t.sb([128, 8], F32, "n1")
            n2 = t.sb([128, 8], F32, "n2")
            slab = [t.sb([128, 8, 768], F32, "wmslab") for _ in range(2)]
            for ci_ in range(2):
                p.dma("sp", condT.ap[:, :, ci_], cond[ci_].rearrange("(k q) -> q k", q=128), writes=[condT])
            p.dma("sp", bm.ap, b_mod[l].rearrange("(j q) -> q j", q=128), writes=[bm])
            p.dma("sp", n1.ap, norm1[l].rearrange("(k q) -> q k", q=128), writes=[n1])
            p.dma("sp", n2.ap, norm2[l].rearrange("(k q) -> q k", q=128), writes=[n2])
            p.op("act", lambda e: e.activation(out=scond.ap, in_=condT.ap, func=AF.Silu), reads=[condT], writes=[scond])
            wm_v = w_mod[l].rearrange("(k q) n -> q k n", q=128)
            pt = ps()
            for s_ in range(8):
                sl = slab[s_ % 2]
                p.dma("sp", sl.ap, wm_v[:, :, s_ * 768:(s_ + 1) * 768], writes=[sl])
                for j in range(6):
                    jj = s_ * 6 + j
                    for k in range(8):
                        p.op("pe", lambda e, sl=sl, j=j, jj=jj, k=k: e.matmul(
                            pt.ap[:, jj * 2:jj * 2 + 2], lhsT=sl.ap[:, k, j * 128:(j + 1) * 128], rhs=scond.ap[:, k, :],
                            start=(k == 0), stop=(k == 7)), reads=[sl, scond], writes=[pt])
            p.op("dve", lambda e: e.tensor_tensor(
                out=modT.ap, in0=pt.ap[:, 0:96].rearrange("q (j c) -> q j c", c=2),
                in1=bm.ap.unsqueeze(2).broadcast_to([128, 48, 2]), op=ALU.add), reads=[pt, bm], writes=[modT])
            p.op("dve", lambda e: e.scalar_tensor_tensor(
                out=A1.ap, in0=modT.ap[:, 8:16, :], scalar=1.0, in1=n1.ap.unsqueeze(2).broadcast_to([128, 8, 2]),
                op0=ALU.add, op1=ALU.mult), reads=[modT, n1], writes=[A1])
            p.op("dve", lambda e: e.scalar_tensor_tensor(
                out=A2.ap, in0=modT.ap[:, 32:40, :], scalar=1.0, in1=n2.ap.unsqueeze(2).broadcast_to([128, 8, 2]),
                op0=ALU.add, op1=ALU.mult), reads=[modT, n2], writes=[A2])
            p.barrier()
        return dict(modT=modT, A1=A1, A2=A2)

    def norm_mod(c, X, H, ncol, Amul, Bkey, Bk0, ci, sq, tmpf, rstd):
        for (a, w) in segs(ncol):
            pt = ps()
            for k in range(8):
                sqt = sq[k % len(sq)]
                p.op("act", lambda e, sqt=sqt, k=k, a=a, w=w: e.activation(
                    out=sqt.ap[:, :w], in_=X.ap[:, k, a:a + w], func=AF.Square), reads=[X], writes=[sqt])
                p.op("pe", lambda e, sqt=sqt, k=k, w=w, pt=pt: e.matmul(
                    pt.ap[:, :w], lhsT=onesb.ap, rhs=sqt.ap[:, :w], start=(k == 0), stop=(k == 7)),
                    reads=[sqt, onesb], writes=[pt])
            p.op("act", lambda e, pt=pt, a=a, w=w: e.activation(
                out=rstd.ap[:, a:a + w], in_=pt.ap[:, :w], func=AF.Ln, scale=1.0 / D, bias=epsc.ap[:, 0:1]),
                reads=[pt, epsc], writes=[rstd])
        p.op("act", lambda e: e.activation(out=rstd.ap[:, :ncol], in_=rstd.ap[:, :ncol], func=AF.Exp, scale=-0.5),
             reads=[rstd], writes=[rstd])
        for k in range(8):
            tf = tmpf[k % len(tmpf)]
            p.op("dve", lambda e, tf=tf, k=k: e.tensor_tensor(
                out=tf.ap[:, :ncol], in0=X.ap[:, k, :ncol], in1=rstd.ap[:, :ncol], op=ALU.mult),
                reads=[X, rstd], writes=[tf])
            p.op("act", lambda e, tf=tf, k=k: e.activation(
                out=H.ap[:, k, :ncol], in_=tf.ap[:, :ncol], func=AF.Identity,
                scale=Amul.ap[:, k, ci:ci + 1], bias=Bkey.ap[:, Bk0 + k, ci:ci + 1]),
                reads=[tf, Amul, Bkey], writes=[H])

    epsc = g.sb([128, 2], F32, "epsc")
    p.op("dve", lambda e: e.memset(epsc.ap[:, 0:1], EPS), writes=[epsc])
    p.op("dve", lambda e: e.memset(epsc.ap[:, 1:2], 1.0), writes=[epsc])

    def stage_A(l, xi, mod):
        xs_v = xs_views[xi]
        xsrc = XS[xi]
        with ExitStack() as es:
            c = Ctx(p, es)
            X = c.sb([128, 8, NCOLMAX], F32, "X")
            H = c.sb([128, 8, NCOLMAX], BF16, "H")
            H2 = c.sb([128, 8, NCOLMAX], BF16, "H2")
            sq = [c.sb([128, 512], BF16, "sq") for _ in range(3)]
            SQB = [c.sb([128, NCOLMAX], BF16, "SQB") for _ in range(2)]
            tmpf = [c.sb([128, NCOLMAX], F32, "tmpf") for _ in range(2)]
            rstd = c.sb([128, NCOLMAX], F32, "rstd")
            PRE = [c.sb([128, NCOLMAX], F32, "PRE") for _ in range(2)]
            ACC = [c.sb([128, NCOLMAX], F32, "ACC") for _ in range(2)]
            SIL = [c.sb([128, NCOLMAX], F32, "SIL") for _ in range(2)]
            OUTB = [c.sb([128, NCOLMAX], BF16, "OUTB") for _ in range(8)]
            WG = [c.sb([128, 8, 512], BF16, "WG") for _ in range(3)]
            KT = c.sb([128, 8, 512], BF16, "KT")
            VT = c.sb([128, 8, 512], BF16, "VT")
            USTG = c.sb([128, 4, 8, 128], BF16, "USTG")
            GST = c.sb([128, 8, 8], F32, "GST")
            BST = c.sb([128, 8, 8], F32, "BST")
            T1 = c.sb([128, 8, 8], F32, "T1")
            cw = c.sb([128, 12, 3], F32, "cw")
            dtb = c.sb([128, 8], F32, "dtb")
            nega = c.sb([128, 8], F32, "nega")
            COS = c.sb([128, 1024], F32, "COS")
            SIN = c.sb([128, 1024], F32, "SIN")
            COSK = c.sb([128, 1024], F32, "COSK")
            SINK = c.sb([128, 1024], F32, "SINK")
            for j_ in range(3):
                p.dma("sp", cw.ap[:, :, j_], w_conv_qkv[l][j_].rearrange("(c q) -> q c", q=128), writes=[cw])
            for t_ in PRE + ACC + SIL + OUTB + tmpf + [rstd]:
                p.op("pool", lambda e, t_=t_: e.memset(t_.ap, 0.0), writes=[t_])
            p.dma("sp", dtb.ap, dt_bias[l].partition_broadcast(128), writes=[dtb])
            p.dma("sp", nega.ap, a_log[l].partition_broadcast(128), writes=[nega])
            p.op("act", lambda e: e.activation(out=nega.ap, in_=nega.ap, func=AF.Exp), reads=[nega], writes=[nega])
            p.op("dve", lambda e: e.tensor_scalar(out=nega.ap, in0=nega.ap, scalar1=-1.0, scalar2=None, op0=ALU.mult),
                 reads=[nega], writes=[nega])
            wv = WB_in.ap[l].rearrange("(k q) n -> q k n", q=128)
            wrv = WB_rot.ap[l].rearrange("(k q) n -> q k n", q=128)
            st = {"wg": 0, "ob": 0, "ev": 0}

            wspecs = []
            for B_ in BLOCKS:
                for grp_ in range(3):
                    wspecs.append((wv, O_QKV + grp_ * 512, 512, WB_in))
                wspecs.append((wv, O_AL, 16, WB_in))
                wspecs.append((wv, O_Z, 512, WB_in))
                wspecs.append((wv, O_U, 512, WB_in))
                for part_, off_ in enumerate((O_QC, O_KC)):
                    wspecs.append((wv, off_, 512, WB_in))
                    if B_["rope"]:
                        wspecs.append((wrv, part_ * 512, 512, WB_rot))
                wspecs.append((wv, O_VC, 512, WB_in))
                wspecs.append((wv, O_GC, 512, WB_in))
                for gg_ in range(6):
                    wspecs.append((wv, O_GT + gg_ * 512, 512, WB_in))
            wloaded = []
            st["wi"] = 0

            def issue_next():
                if st["wi"] < len(wspecs):
                    view, off, w, key = wspecs[st["wi"]]
                    st["wi"] += 1
                    t_ = WG[st["wg"] % 3]
                    st["wg"] += 1
                    p.dma("sp", t_.ap[:, :, :w], view[:, :, off:off + w], reads=[key], writes=[t_])
                    wloaded.append((t_, off, w))

            def load_w(view, off, w, key):
                if not wloaded:
                    issue_next()
                t_, off_, w_ = wloaded.pop(0)
                assert (off_, w_) == (off, w), (off_, w_, off, w)
                issue_next()
                return t_

            def outb():
                t_ = OUTB[st["ob"] % 8]
                st["ob"] += 1
                return t_

            def evac(fn_act, fn_dve, reads, writes):
                st["ev"] += 1
                if st["ev"] % 4 != 0:
                    p.op("act", fn_act, reads=reads, writes=writes)
                else:
                    p.op("dve", fn_dve, reads=reads, writes=writes)

            def proj(pt, wt, col0, m, a, w):
                for k in range(8):
                    p.op("pe", lambda e, k=k: e.matmul(pt.ap[:m, :w], lhsT=wt.ap[:, k, col0:col0 + m],
                                                       rhs=H.ap[:, k, a:a + w], start=(k == 0), stop=(k == 7)),
                         reads=[wt, H], writes=[pt])

            def block_prep(B_, Hb):
                for (tok0, n, lv, rv), c0 in zip(B_["pieces"], B_["c0"]):
                    a = c0 + 1 - int(lv)
                    b = c0 + 1 + n + int(rv)
                    p.dma("sp", X.ap[:, :, a:b], xs_v[:, :, tok0 - int(lv):tok0 + n + int(rv)], reads=[xsrc], writes=[X])
                    if not lv:
                        p.op("pool", lambda e, c0=c0: e.memset(X.ap[:, :, c0:c0 + 1], 0.0), writes=[X])
                    if not rv:
                        p.op("pool", lambda e, c0=c0, n=n: e.memset(X.ap[:, :, c0 + n + 1:c0 + n + 2], 0.0), writes=[X])
                norm_mod(c, X, Hb, B_["ncol"], mod["A1"], mod["modT"], 0, B_["cond"], sq, tmpf, rstd)
                for (tok0, n, lv, rv), c0 in zip(B_["pieces"], B_["c0"]):
                    if not lv:
                        p.op("pool", lambda e, c0=c0: e.memset(Hb.ap[:, :, c0:c0 + 1], 0.0), writes=[Hb])
                    if not rv:
                        p.op("pool", lambda e, c0=c0, n=n: e.memset(Hb.ap[:, :, c0 + n + 1:c0 + n + 2], 0.0), writes=[Hb])

            Hs = [H, H2]
            block_prep(BLOCKS[0], Hs[0])
            for bi_, B in enumerate(BLOCKS):
                H = Hs[bi_ % 2]
                ci = B["cond"]
                ncol = B["ncol"]
                pieces = B["pieces"]
                C0 = B["c0"]
                tok_lo = B["tok_lo"]
                if B["rope"]:
                    p.dma("sp", COS.ap, c_rope[0][:, B["stok"]:B["stok"] + 1024], writes=[COS])
                    p.dma("sp", SIN.ap, c_rope[1][:, B["stok"]:B["stok"] + 1024], writes=[SIN])
                    p.op("act", lambda e: e.mul(out=COSK.ap, in_=COS.ap, mul=128.0 ** -0.5), reads=[COS], writes=[COSK])
                    p.op("act", lambda e: e.mul(out=SINK.ap, in_=SIN.ap, mul=128.0 ** -0.5), reads=[SIN], writes=[SINK])

                tiles = []
                for (tok0, n, lv, rv), c0 in zip(pieces, C0):
                    for t_ in range(n // 128):
                        tiles.append(c0 + 1 + 128 * t_)

                def center_segs():
                    out = []
                    for (tok0, n, lv, rv), c0 in zip(pieces, C0):
                        for (a, w) in segs(n):
                            out.append((c0 + 1 + a, w, tok0 + a))
                    return out

                def store_fm(dst, row0, ob):
                    for (tok0, n, lv, rv), c0 in zip(pieces, C0):
                        p.dma("pool", dst.ap[row0:row0 + 128, tok0:tok0 + n], ob.ap[:, c0 + 1:c0 + 1 + n],
                              reads=[ob], writes=[dst])

                def transposes_to(ob, stg, h):
                    for t4 in range(0, len(tiles), 4):
                        pt = ps()
                        ptb = pt.ap.bitcast(BF16)
                        for j in range(4):
                            cc = tiles[t4 + j]
                            p.op("pe", lambda e, j=j, cc=cc, ptb=ptb: e.transpose(
                                out=ptb[:, j * 128:(j + 1) * 128], in_=ob.ap[:, cc:cc + 128], identity=identb.ap),
                                reads=[ob, identb], writes=[pt])
                        evac(lambda e, ptb=ptb, t4=t4: e.copy(out=stg.ap[:, t4:t4 + 4, h * 128:(h + 1) * 128],
                                                              in_=ptb[:, 0:512].rearrange("q (t f) -> q t f", t=4)),
                             lambda e, ptb=ptb, t4=t4: e.tensor_copy(out=stg.ap[:, t4:t4 + 4, h * 128:(h + 1) * 128],
                                                                     in_=ptb[:, 0:512].rearrange("q (t f) -> q t f", t=4)),
                             [pt], [stg])

                def store_tm(dst, stg, width=512):
                    ti = 0
                    for (tok0, n, lv, rv), c0 in zip(pieces, C0):
                        nt = n // 128
                        p.dma("pool", dst.ap[tok0:tok0 + n, :].rearrange("(t q) f -> q t f", q=128),
                              stg.ap[:, ti:ti + nt, :], reads=[stg], writes=[dst])
                        ti += nt

                def head_gen(grp, h, wt):
                    cidx = grp * 4 + h
                    pre = PRE[h % 2]
                    acc = ACC[h % 2]
                    sil = SIL[h % 2]
                    for (tok0, n, lv, rv), c0 in zip(pieces, C0):
                        for (a, w) in segs(n + 2):
                            pt = ps()
                            proj(pt, wt, h * 128, 128, c0 + a, w)
                            evac(lambda e, pt=pt, a=a, w=w, c0=c0: e.copy(out=pre.ap[:, c0 + a:c0 + a + w], in_=pt.ap[:, :w]),
                                 lambda e, pt=pt, a=a, w=w, c0=c0: e.tensor_copy(out=pre.ap[:, c0 + a:c0 + a + w], in_=pt.ap[:, :w]),
                                 [pt], [pre])
                        p.op("dve", lambda e, c0=c0, n=n: e.tensor_scalar(
                            out=acc.ap[:, c0 + 1:c0 + 1 + n], in0=pre.ap[:, c0 + 1:c0 + 1 + n],
                            scalar1=cw.ap[:, cidx, 1:2], scalar2=None, op0=ALU.mult), reads=[pre, cw], writes=[acc])
                        p.op("dve", lambda e, c0=c0, n=n: e.scalar_tensor_tensor(
                            out=acc.ap[:, c0 + 1:c0 + 1 + n], in0=pre.ap[:, c0:c0 + n], scalar=cw.ap[:, cidx, 0:1],
                            in1=acc.ap[:, c0 + 1:c0 + 1 + n], op0=ALU.mult, op1=ALU.add), reads=[pre, cw, acc], writes=[acc])
                        p.op("dve", lambda e, c0=c0, n=n: e.scalar_tensor_tensor(
                            out=acc.ap[:, c0 + 1:c0 + 1 + n], in0=pre.ap[:, c0 + 2:c0 + 2 + n], scalar=cw.ap[:, cidx, 2:3],
                            in1=acc.ap[:, c0 + 1:c0 + 1 + n], op0=ALU.mult, op1=ALU.add), reads=[pre, cw, acc], writes=[acc])
                    yield
                    ob = outb()
                    if grp == 2:
                        p.op("act", lambda e: e.activation(out=ob.ap[:, :ncol], in_=acc.ap[:, :ncol], func=AF.Silu),
                             reads=[acc], writes=[ob])
                        transposes_to(ob, VT, h)
                        return
                    p.op("act", lambda e: e.activation(out=sil.ap[:, :ncol], in_=acc.ap[:, :ncol], func=AF.Silu),
                         reads=[acc], writes=[sil])
                    sqb = SQB[h % 2]
                    p.op("act", lambda e, sqb=sqb: e.activation(out=sqb.ap[:, :ncol], in_=sil.ap[:, :ncol], func=AF.Square),
                         reads=[sil], writes=[sqb])
                    for (cc, w, tk) in center_segs():
                        pt = ps()
                        p.op("pe", lambda e, pt=pt, cc=cc, w=w, sqb=sqb: e.matmul(pt.ap[:, :w], lhsT=onesb.ap, rhs=sqb.ap[:, cc:cc + w],
                                                                                    start=True, stop=True), reads=[onesb, sqb], writes=[pt])
                        p.op("act", lambda e, pt=pt, cc=cc, w=w: e.activation(
                            out=acc.ap[:, cc:cc + w], in_=pt.ap[:, :w], func=AF.Ln, bias=epsc.ap[:, 0:1]),
                            reads=[pt, epsc], writes=[acc])
                    p.op("act", lambda e: e.activation(out=acc.ap[:, :ncol], in_=acc.ap[:, :ncol], func=AF.Exp, scale=-0.5),
                         reads=[acc], writes=[acc])
                    qs = (128.0 ** -0.5) if grp == 0 else 1.0
                    p.op("dve", lambda e, qs=qs: e.scalar_tensor_tensor(
                        out=ob.ap[:, :ncol], in0=sil.ap[:, :ncol], scalar=qs, in1=acc.ap[:, :ncol],
                        op0=ALU.mult, op1=ALU.mult), reads=[sil, acc], writes=[ob])
                    store_fm(QA if grp == 0 else KA, h * 128, ob)
                    if grp == 1:
                        transposes_to(ob, KT, h)

                def group_end(grp):
                    if grp == 0 and bi_ + 1 < len(BLOCKS):
                        block_prep(BLOCKS[bi_ + 1], Hs[(bi_ + 1) % 2])
                    if grp == 1:
                        store_tm(KAT, KT)
                    if grp == 2:
                        store_tm(VAT, VT)

                prev_g = None
                for grp in range(3):
                    wt = load_w(wv, O_QKV + grp * 512, 512, WB_in)
                    for h in range(4):
                        g_ = head_gen(grp, h, wt)
                        next(g_)
                        if prev_g is not None:
                            for _ in prev_g[0]:
                                pass
                            if prev_g[2] == 3:
                                group_end(prev_g[1])
                        prev_g = (g_, grp, h)
                for _ in prev_g[0]:
                    pass
                group_end(prev_g[1])

                wt = load_w(wv, O_AL, 16, WB_in)
                pt = ps()
                for ti, cc in enumerate(tiles):
                    for k in range(8):
                        p.op("pe", lambda e, k=k, ti=ti, cc=cc: e.matmul(
                            pt.ap[:, ti * 16:(ti + 1) * 16], lhsT=H.ap[:, k, cc:cc + 128], rhs=wt.ap[:, k, 0:16],
                            start=(k == 0), stop=(k == 7)), reads=[wt, H], writes=[pt])
                ptv = pt.ap[:, 0:128].rearrange("q (t j) -> q t j", j=16)
                p.op("dve", lambda e: e.tensor_tensor(out=T1.ap, in0=ptv[:, :, 0:8], in1=dtb.ap.unsqueeze(1).broadcast_to([128, 8, 8]),
                                                      op=ALU.add), reads=[pt, dtb], writes=[T1])
                p.op("act", lambda e: e.activation(out=T1.ap, in_=T1.ap, func=AF.Exp), reads=[T1], writes=[T1])
                p.op("act", lambda e: e.activation(out=T1.ap, in_=T1.ap, func=AF.Ln, bias=epsc.ap[:, 1:2]), reads=[T1, epsc], writes=[T1])
                p.op("dve", lambda e: e.tensor_tensor(out=GST.ap, in0=T1.ap, in1=nega.ap.unsqueeze(1).broadcast_to([128, 8, 8]),
                                                      op=ALU.mult), reads=[T1, nega], writes=[GST])
                p.op("act", lambda e: e.activation(out=BST.ap, in_=ptv[:, :, 8:16], func=AF.Sigmoid), reads=[pt], writes=[BST])
                ti = 0
                for (tok0, n, lv, rv), c0 in zip(pieces, C0):
                    nt = n // 128
                    p.dma("pool", GA.ap[tok0:tok0 + n, :].rearrange("(t q) j -> q t j", q=128), GST.ap[:, ti:ti + nt, :],
                          reads=[GST], writes=[GA])
                    p.dma("pool", BA.ap[tok0:tok0 + n, :].rearrange("(t q) j -> q t j", q=128), BST.ap[:, ti:ti + nt, :],
                          reads=[BST], writes=[BA])
                    ti += nt

                def simple_group(off, dst, row0, func):
                    wt = load_w(wv, off, 512, WB_in)
                    for h in range(4):
                        ob = outb()
                        for (cc, w, tk) in center_segs():
                            pt = ps()
                            proj(pt, wt, h * 128, 128, cc, w)
                            p.op("act", lambda e, pt=pt, cc=cc, w=w: e.activation(out=ob.ap[:, cc:cc + w], in_=pt.ap[:, :w], func=func),
                                 reads=[pt], writes=[ob])
                        store_fm(dst, row0 + h * 128, ob)

                simple_group(O_Z, ZA, 0, AF.Silu)

                wt = load_w(wv, O_U, 512, WB_in)
                for j in range(4):
                    for (cc, w, tk) in center_segs():
                        pt = ps()
                        proj(pt, wt, j * 128, 128, cc, w)
                        nb = (tk - tok_lo) // 8
                        evac(lambda e, pt=pt, w=w, nb=nb, j=j: e.copy(
                            out=USTG.ap[:, j, :, nb:nb + w // 8], in_=pt.ap[:, :w].rearrange("q (n s) -> q s n", s=8)),
                            lambda e, pt=pt, w=w, nb=nb, j=j: e.tensor_copy(
                            out=USTG.ap[:, j, :, nb:nb + w // 8], in_=pt.ap[:, :w].rearrange("q (n s) -> q s n", s=8)),
                            [pt], [USTG])
                n0 = tok_lo // 8
                for gt_ in range(4):
                    for gl in range(8):
                        p.dma("pool", UB.ap[:, :, gt_ * 8 + gl, n0:n0 + 128].rearrange("s c n -> c s n"),
                              USTG.ap[gl * 16:(gl + 1) * 16, gt_, :, :], reads=[USTG], writes=[UB])

                for part, (off, dst) in enumerate(((O_QC, QC), (O_KC, KC))):
                    wt = load_w(wv, off, 512, WB_in)
                    wr = load_w(wrv, part * 512, 512, WB_rot) if B["rope"] else None
                    ct, sn = (COS, SIN) if part == 0 else (COSK, SINK)
                    for h in range(4):
                        ob = outb()
                        for (cc, w, tk) in center_segs():
                            pt = ps()
                            proj(pt, wt, h * 128, 128, cc, w)
                            if B["rope"]:
                                pt2 = ps()
                                proj(pt2, wr, h * 128, 128, cc, w)
                                tc0 = tk - tok_lo
                                t1 = tmpf[0]
                                t2 = tmpf[1]
                                p.op("dve", lambda e, pt=pt, w=w, tc0=tc0: e.tensor_tensor(
                                    out=t1.ap[:, :w], in0=pt.ap[:, :w], in1=ct.ap[:, tc0:tc0 + w], op=ALU.mult),
                                    reads=[pt, ct], writes=[t1])
                                p.op("dve", lambda e, pt2=pt2, w=w, tc0=tc0: e.tensor_tensor(
                                    out=t2.ap[:, :w], in0=pt2.ap[:, :w], in1=sn.ap[:, tc0:tc0 + w], op=ALU.mult),
                                    reads=[pt2, sn], writes=[t2])
                                p.op("pool", lambda e, cc=cc, w=w: e.tensor_tensor(
                                    out=ob.ap[:, cc:cc + w], in0=t1.ap[:, :w], in1=t2.ap[:, :w], op=ALU.add),
                                    reads=[t1, t2], writes=[ob])
                            else:
                                sc = 1.0 if part == 0 else 128.0 ** -0.5
                                evac(lambda e, pt=pt, cc=cc, w=w, sc=sc: e.mul(out=ob.ap[:, cc:cc + w], in_=pt.ap[:, :w], mul=sc),
                                     lambda e, pt=pt, cc=cc, w=w, sc=sc: e.tensor_scalar(
                                         out=ob.ap[:, cc:cc + w], in0=pt.ap[:, :w], scalar1=sc, scalar2=None, op0=ALU.mult),
                                     [pt], [ob])
                        store_fm(dst, h * 128, ob)
                        if part == 1:
                            transposes_to(ob, KT, h)
                    if part == 1:
                        store_tm(KCT, KT)
                wt = load_w(wv, O_VC, 512, WB_in)
                for ti, cc in enumerate(tiles):
                    pt = ps()
                    for k in range(8):
                        p.op("pe", lambda e, k=k, cc=cc, pt=pt: e.matmul(
                            pt.ap[:, :512], lhsT=H.ap[:, k, cc:cc + 128], rhs=wt.ap[:, k, 0:512],
                            start=(k == 0), stop=(k == 7)), reads=[wt, H], writes=[pt])
                    evac(lambda e, pt=pt, ti=ti: e.copy(out=VT.ap[:, ti, :], in_=pt.ap[:, :512]),
                         lambda e, pt=pt, ti=ti: e.tensor_copy(out=VT.ap[:, ti, :], in_=pt.ap[:, :512]), [pt], [VT])
                store_tm(VCT, VT)
                simple_group(O_GC, GC, 0, AF.Silu)
                for gg in range(6):
                    simple_group(O_GT + gg * 512, GT, gg * 512, AF.Sigmoid)
        p.barrier()

    def stage_C1(l, xi, xo, mod):
        with ExitStack() as es:
            c = Ctx(p, es)
            X = c.sb([128, 8, 1024], F32, "X1")
            MRG = c.sb([128, 8, 1024], BF16, "MRG")
            BRS = [[c.sb([128, 4, 1024], BF16, "BR%d_%d" % (i, s_)) for i in range(3)] for s_ in range(2)]
            GTt = [c.sb([128, 1024], BF16, "GTt") for _ in range(9)]
            gq = []
            gstate = {"n": 0, "c": 0}
            WBR = [c.sb([128, 4, 1024], BF16, "WBR%d" % i) for i in range(3)]
            WO = c.sb([128, 8, 1024], BF16, "WO")
            tm = [c.sb([128, 512], F32, "tm") for _ in range(4)]
            for i, wsrc in enumerate((WB_bra, WB_brb, WB_brc)):
                p.dma("sp", WBR[i].ap, wsrc.ap[l].rearrange("(k q) n -> q k n", q=128), reads=[wsrc], writes=[WBR[i]])
            p.dma("sp", WO.ap, WB_o.ap[l].rearrange("(k q) n -> q k n", q=128), reads=[WB_o], writes=[WO])
            gcount = 0
            def load_br(B_, set_):
                t0_ = B_["tok_lo"]
                for i, src in enumerate((OA, YB, OC)):
                    p.dma("sp", BRS[set_][i].ap, src.ap.rearrange("(k q) t -> q k t", q=128)[:, :, t0_:t0_ + 1024],
                          reads=[src], writes=[BRS[set_][i]])

            load_br(BLOCKS[0], 0)
            for bi_, B in enumerate(BLOCKS):
                ci = B["cond"]
                t0 = B["tok_lo"]
                BR = BRS[bi_ % 2]
                p.dma("sp", X.ap, xs_views[xi][:, :, t0:t0 + 1024], reads=[XS[xi]], writes=[X])
                if bi_ + 1 < len(BLOCKS):
                    load_br(BLOCKS[bi_ + 1], (bi_ + 1) % 2)
                for j in range(8):
                    while len(gq) < 2 and gstate["n"] < 8 * len(BLOCKS):
                        bi_, j_ = gstate["n"] // 8, gstate["n"] % 8
                        gstate["n"] += 1
                        t0_ = BLOCKS[bi_]["tok_lo"]
                        gts_ = []
                        for x_ in range(3):
                            gt_ = GTt[gstate["c"] % 9]
                            gstate["c"] += 1
                            p.dma("sp", gt_.ap, GT.ap[x_ * 1024 + j_ * 128:x_ * 1024 + (j_ + 1) * 128, t0_:t0_ + 1024],
                                  reads=[GT], writes=[gt_])
                            gts_.append(gt_)
                        gq.append(gts_)
                    gts = gq.pop(0)
                    for sg in range(2):
                        a = sg * 512
                        pts = []
                        for x_ in range(3):
                            pt = ps()
                            for k in range(4):
                                p.op("pe", lambda e, pt=pt, x_=x_, k=k, a=a: e.matmul(
                                    pt.ap[:, :512], lhsT=WBR[x_].ap[:, k, j * 128:(j + 1) * 128], rhs=BR[x_].ap[:, k, a:a + 512],
                                    start=(k == 0), stop=(k == 3)), reads=[WBR[x_], BR[x_]], writes=[pt])
                            pts.append(pt)
                        ta, tb, tcc = tm[(2 * sg) % 4], tm[(2 * sg + 1) % 4], tm[(2 * sg + 2) % 4]
                        p.op("dve", lambda e, a=a, ta=ta, pt=pts[0], g_=gts[0]: e.tensor_tensor(
                            out=ta.ap, in0=pt.ap[:, :512], in1=g_.ap[:, a:a + 512], op=ALU.mult), reads=[pts[0], gts[0]], writes=[ta])
                        p.op("dve", lambda e, a=a, tb=tb, pt=pts[1], g_=gts[1]: e.tensor_tensor(
                            out=tb.ap, in0=pt.ap[:, :512], in1=g_.ap[:, a:a + 512], op=ALU.mult), reads=[pts[1], gts[1]], writes=[tb])
                        p.op("dve", lambda e, ta=ta, tb=tb: e.tensor_tensor(out=ta.ap, in0=ta.ap, in1=tb.ap, op=ALU.add),
                             reads=[ta, tb], writes=[ta])
                        p.op("dve", lambda e, a=a, tb=tb, pt=pts[2], g_=gts[2]: e.tensor_tensor(
                            out=tb.ap, in0=pt.ap[:, :512], in1=g_.ap[:, a:a + 512], op=ALU.mult), reads=[pts[2], gts[2]], writes=[tb])
                        p.op("pool", lambda e, a=a, ta=ta, tb=tb: e.tensor_tensor(
                            out=MRG.ap[:, j, a:a + 512], in0=ta.ap, in1=tb.ap, op=ALU.add), reads=[ta, tb], writes=[MRG])
                for j in range(8):
                    for sg in range(2):
                        a = sg * 512
                        pt = ps()
                        for k in range(8):
                            p.op("pe", lambda e, pt=pt, k=k, a=a: e.matmul(
                                pt.ap[:, :512], lhsT=WO.ap[:, k, j * 128:(j + 1) * 128], rhs=MRG.ap[:, k, a:a + 512],
                                start=(k == 0), stop=(k == 7)), reads=[WO, MRG], writes=[pt])
                        p.op("dve", lambda e, pt=pt, a=a: e.scalar_tensor_tensor(
                            out=X.ap[:, j, a:a + 512], in0=pt.ap[:, :512], scalar=mod["modT"].ap[:, 16 + j, ci:ci + 1],
                            in1=X.ap[:, j, a:a + 512], op0=ALU.mult, op1=ALU.add), reads=[pt, X, mod["modT"]], writes=[X])
                p.dma("pool", xs_views[xo][:, :, t0:t0 + 1024], X.ap, reads=[X], writes=[XS[xo]])
        p.barrier()

    def stage_C2(l, xi, xo, mod):
        xs_v = xs_views[xi]
        with ExitStack() as es:
            c = Ctx(p, es)
            X = c.sb([128, 8, NCOLMAX], F32, "X2")
            H = c.sb([128, 8, NCOLMAX], BF16, "H2")
            sq = [c.sb([128, 512], BF16, "sq2") for _ in range(3)]
            tmpf = [c.sb([128, NCOLMAX], F32, "tmpf2") for _ in range(2)]
            rstd = c.sb([128, NCOLMAX], F32, "rstd2")
            PREgs = [c.sb([128, NCOLMAX], F32, "PREg") for _ in range(2)]
            PREvs = [c.sb([128, NCOLMAX], F32, "PREv") for _ in range(2)]
            ACgs = [c.sb([128, NCOLMAX], F32, "ACg") for _ in range(2)]
            ACvs = [c.sb([128, NCOLMAX], F32, "ACv") for _ in range(2)]
            ACTV = c.sb([128, 22, 1024], BF16, "ACTV")
            WU = [c.sb([128, 8, 256], BF16, "WU") for _ in range(4)]
            WD = [c.sb([128, 22, 128], BF16, "WD") for _ in range(3)]
            wu_q = []
            wd_q = []
            cwf = c.sb([128, 44, 3], F32, "cwf")
            bcf = c.sb([128, 44], F32, "bcf")
            for t_ in (PREgs[0], PREgs[1], PREvs[0], PREvs[1], ACgs[0], ACgs[1], ACvs[0], ACvs[1], rstd, tmpf[0], tmpf[1]):
                p.op("pool", lambda e, t_=t_: e.memset(t_.ap, 0.0), writes=[t_])
            for j_ in range(3):
                p.dma("sp", cwf.ap[:, :, j_], w_conv_ffn[l][j_].rearrange("(c q) -> q c", q=128), writes=[cwf])
            p.dma("sp", bcf.ap, b_conv_ffn[l].rearrange("(c q) -> q c", q=128), writes=[bcf])
            wuv = WB_up.ap[l].rearrange("(k q) n -> q k n", q=128)
            wdv = WB_down.ap[l].rearrange("(f q) n -> q f n", q=128)
            cnt = {"wu": 0, "wd": 0, "ev": 0, "wuj": 0, "wdj": 0}

            def evac(fn_act, fn_dve, reads, writes):
                p.op("act", fn_act, reads=reads, writes=writes)

            for B in BLOCKS:
                ci = B["cond"]
                ncol = B["ncol"]
                pieces = B["pieces"]
                C0 = B["c0"]
                t0 = B["tok_lo"]
                for (tok0, n, lv, rv), c0 in zip(pieces, C0):
                    a = c0 + 1 - int(lv)
                    b = c0 + 1 + n + int(rv)
                    p.dma("sp", X.ap[:, :, a:b], xs_v[:, :, tok0 - int(lv):tok0 + n + int(rv)], reads=[XS[xi]], writes=[X])
                    if not lv:
                        p.op("pool", lambda e, c0=c0: e.memset(X.ap[:, :, c0:c0 + 1], 0.0), writes=[X])
                    if not rv:
                        p.op("pool", lambda e, c0=c0, n=n: e.memset(X.ap[:, :, c0 + n + 1:c0 + n + 2], 0.0), writes=[X])
                norm_mod(c, X, H, ncol, mod["A2"], mod["modT"], 24, ci, sq, tmpf, rstd)
                for (tok0, n, lv, rv), c0 in zip(pieces, C0):
                    if not lv:
                        p.op("pool", lambda e, c0=c0: e.memset(H.ap[:, :, c0:c0 + 1], 0.0), writes=[H])
                    if not rv:
                        p.op("pool", lambda e, c0=c0, n=n: e.memset(H.ap[:, :, c0 + n + 1:c0 + n + 2], 0.0), writes=[H])
                pend_post = []
                for j in range(22):
                    while len(wu_q) < 3 and cnt["wuj"] < 22 * len(BLOCKS):
                        jj_ = cnt["wuj"] % 22
                        cnt["wuj"] += 1
                        wu_ = WU[cnt["wu"] % 4]
                        cnt["wu"] += 1
                        p.dma("sp", wu_.ap[:, :, 0:128], wuv[:, :, jj_ * 128:(jj_ + 1) * 128], reads=[WB_up], writes=[wu_])
                        p.dma("sp", wu_.ap[:, :, 128:256], wuv[:, :, DFF + jj_ * 128:DFF + (jj_ + 1) * 128], reads=[WB_up], writes=[wu_])
                        wu_q.append(wu_)
                    wu = wu_q.pop(0)
                    ACg, ACv = ACgs[j % 2], ACvs[j % 2]
                    PREg, PREv = PREgs[j % 2], PREvs[j % 2]
                    for part, (pre, acc) in enumerate(((PREg, ACg), (PREv, ACv))):
                        cidx = j + 22 * part
                        for (tok0, n, lv, rv), c0 in zip(pieces, C0):
                            for (a, w) in segs(n + 2):
                                pt = ps()
                                for k in range(8):
                                    p.op("pe", lambda e, pt=pt, k=k, a=a, w=w, c0=c0, part=part: e.matmul(
                                        pt.ap[:, :w], lhsT=wu.ap[:, k, part * 128:(part + 1) * 128], rhs=H.ap[:, k, c0 + a:c0 + a + w],
                                        start=(k == 0), stop=(k == 7)), reads=[wu, H], writes=[pt])
                                evac(lambda e, pt=pt, a=a, w=w, c0=c0, pre=pre: e.copy(out=pre.ap[:, c0 + a:c0 + a + w], in_=pt.ap[:, :w]),
                                     lambda e, pt=pt, a=a, w=w, c0=c0, pre=pre: e.tensor_copy(out=pre.ap[:, c0 + a:c0 + a + w], in_=pt.ap[:, :w]),
                                     [pt], [pre])
                            p.op("act", lambda e, c0=c0, n=n, pre=pre, acc=acc, cidx=cidx: e.activation(
                                out=acc.ap[:, c0 + 1:c0 + 1 + n], in_=pre.ap[:, c0 + 1:c0 + 1 + n], func=AF.Identity,
                                scale=cwf.ap[:, cidx, 1:2], bias=bcf.ap[:, cidx:cidx + 1]),
                                reads=[pre, cwf, bcf], writes=[acc])
                            p.op("dve", lambda e, c0=c0, n=n, pre=pre, acc=acc, cidx=cidx: e.scalar_tensor_tensor(
                                out=acc.ap[:, c0 + 1:c0 + 1 + n], in0=pre.ap[:, c0:c0 + n], scalar=cwf.ap[:, cidx, 0:1],
                                in1=acc.ap[:, c0 + 1:c0 + 1 + n], op0=ALU.mult, op1=ALU.add), reads=[pre, cwf, acc], writes=[acc])
                            p.op("dve", lambda e, c0=c0, n=n, pre=pre, acc=acc, cidx=cidx: e.scalar_tensor_tensor(
                                out=acc.ap[:, c0 + 1:c0 + 1 + n], in0=pre.ap[:, c0 + 2:c0 + 2 + n], scalar=cwf.ap[:, cidx, 2:3],
                                in1=acc.ap[:, c0 + 1:c0 + 1 + n], op0=ALU.mult, op1=ALU.add), reads=[pre, cwf, acc], writes=[acc])
                    def post(j=j, ACg=ACg, ACv=ACv):
                        p.op("act", lambda e: e.activation(out=ACg.ap[:, :ncol], in_=ACg.ap[:, :ncol], func=AF.Silu),
                             reads=[ACg], writes=[ACg])
                        for (tok0, n, lv, rv), c0 in zip(pieces, C0):
                            tl = tok0 - t0
                            p.op("pool", lambda e, c0=c0, n=n, tl=tl, j=j: e.tensor_tensor(
                                out=ACTV.ap[:, j, tl:tl + n], in0=ACg.ap[:, c0 + 1:c0 + 1 + n], in1=ACv.ap[:, c0 + 1:c0 + 1 + n],
                                op=ALU.mult), reads=[ACg, ACv], writes=[ACTV])
                    if pend_post:
                        pend_post.pop(0)()
                    pend_post.append(post)
                while pend_post:
                    pend_post.pop(0)()
                for oc in range(8):
                    while len(wd_q) < 2 and cnt["wdj"] < 8 * len(BLOCKS):
                        oc_ = cnt["wdj"] % 8
                        cnt["wdj"] += 1
                        wd_ = WD[cnt["wd"] % 3]
                        cnt["wd"] += 1
                        p.dma("sp", wd_.ap, wdv[:, :, oc_ * 128:(oc_ + 1) * 128], reads=[WB_down], writes=[wd_])
                        wd_q.append(wd_)
                    wd = wd_q.pop(0)
                    for (tok0, n, lv, rv), c0 in zip(pieces, C0):
                        tl = tok0 - t0
                        for (a, w) in segs(n):
                            pt = ps()
                            for f in range(22):
                                p.op("pe", lambda e, pt=pt, f=f, a=a, w=w, tl=tl: e.matmul(
                                    pt.ap[:, :w], lhsT=wd.ap[:, f, :], rhs=ACTV.ap[:, f, tl + a:tl + a + w],
                                    start=(f == 0), stop=(f == 21)), reads=[wd, ACTV], writes=[pt])
                            p.op("dve", lambda e, pt=pt, a=a, w=w, c0=c0: e.scalar_tensor_tensor(
                                out=X.ap[:, oc, c0 + 1 + a:c0 + 1 + a + w], in0=pt.ap[:, :w],
                                scalar=mod["modT"].ap[:, 40 + oc, ci:ci + 1], in1=X.ap[:, oc, c0 + 1 + a:c0 + 1 + a + w],
                                op0=ALU.mult, op1=ALU.add), reads=[pt, X, mod["modT"]], writes=[X])
                for (tok0, n, lv, rv), c0 in zip(pieces, C0):
                    p.dma("pool", xs_views[xo][:, :, tok0:tok0 + n], X.ap[:, :, c0 + 1:c0 + 1 + n], reads=[X], writes=[XS[xo]])
        p.barrier()

    def stage_output(xi):
        xsrc = XS[xi]
        with ExitStack() as es:
            c = Ctx(p, es)
            xt = [c.sb([128, 8, 512], F32, "ox") for _ in range(2)]
            yt = [c.sb([128, 8, 512], F32, "oy") for _ in range(2)]
            ot = [c.sb([128, D], F32, "ot") for _ in range(3)]
            sqo = [c.sb([128, 512], BF16, "sqo") for _ in range(3)]
            rs = c.sb([128, 512], F32, "rso")
            fn = c.sb([128, 8], F32, "fn")
            p.dma("sp", fn.ap, final_norm.rearrange("(k q) -> q k", q=128), writes=[fn])
            toks = [(B["tok_lo"] + a, 512) for B in BLOCKS for a in (0, 512)]
            for bi, (tk0, nn) in enumerate(toks):
                x = xt[bi % 2]
                y = yt[bi % 2]
                p.dma("sp", x.ap, xs_views[xi][:, :, tk0:tk0 + 512], reads=[xsrc], writes=[x])
                pt = ps()
                for k in range(8):
                    sqt = sqo[k % 3]
                    p.op("act", lambda e, sqt=sqt, k=k, x=x: e.activation(out=sqt.ap, in_=x.ap[:, k, :], func=AF.Square),
                         reads=[x], writes=[sqt])
                    p.op("pe", lambda e, sqt=sqt, k=k, pt=pt: e.matmul(pt.ap[:, :512], lhsT=onesb.ap, rhs=sqt.ap,
                                                                         start=(k == 0), stop=(k == 7)), reads=[sqt, onesb], writes=[pt])
                p.op("act", lambda e, pt=pt: e.activation(out=rs.ap, in_=pt.ap[:, :512], func=AF.Ln, scale=1.0 / D,
                                                          bias=epsc.ap[:, 0:1]), reads=[pt, epsc], writes=[rs])
                p.op("act", lambda e: e.activation(out=rs.ap, in_=rs.ap, func=AF.Exp, scale=-0.5), reads=[rs], writes=[rs])
                for k in range(8):
                    p.op("dve", lambda e, k=k, x=x, y=y: e.scalar_tensor_tensor(
                        out=y.ap[:, k, :], in0=x.ap[:, k, :], scalar=fn.ap[:, k:k + 1], in1=rs.ap, op0=ALU.mult, op1=ALU.mult),
                        reads=[x, fn, rs], writes=[y])
                for tt in range(4):
                    o = ot[(bi * 4 + tt) % 3]
                    for k2 in range(2):
                        pt = ps()
                        for kk in range(4):
                            k = k2 * 4 + kk
                            p.op("pe", lambda e, pt=pt, kk=kk, k=k, y=y, tt=tt: e.transpose(
                                out=pt.ap[:, kk * 128:(kk + 1) * 128], in_=y.ap[:, k, tt * 128:(tt + 1) * 128],
                                identity=ident.ap), reads=[y, ident], writes=[pt])
                        if k2 == 0:
                            p.op("act", lambda e, pt=pt, o=o, k2=k2: e.copy(out=o.ap[:, k2 * 512:(k2 + 1) * 512], in_=pt.ap),
                                 reads=[pt], writes=[o])
                        else:
                            p.op("dve", lambda e, pt=pt, o=o, k2=k2: e.tensor_copy(out=o.ap[:, k2 * 512:(k2 + 1) * 512], in_=pt.ap),
                                 reads=[pt], writes=[o])
                    tok0 = tk0 + tt * 128
                    p.dma("pool", yout.ap[tok0:tok0 + 128, :], o.ap, reads=[o], writes=[yout])
        p.barrier()

    GAM = [1.0 - 2.0 ** (-5 - h) for h in range(4)]
    GAMB = GAM[::-1]

    def mix_C(l):
        with ExitStack() as es:
            c = Ctx(p, es)
            RDTt = c.sb([128, 4, 128], F32, "RDT")
            RQDt = c.sb([128, 2, 4, 128], F32, "RQD")
            RKDt = c.sb([128, 2, 512], F32, "RKD")
            p.dma("sp", RDTt.ap, c_rdt, writes=[RDTt])
            p.dma("sp", RQDt.ap, c_rqd, writes=[RQDt])
            p.dma("sp", RKDt.ap, c_rkd, writes=[RKDt])
            S = c.sb([128, 2, 4, 128], F32, "Sr")
            Sbf = c.sb([128, 2, 32, 512], BF16, "Srbf")
            VTM = c.sb([128, 32, 512], BF16, "VTM")
            KTc = [c.sb([128, 512], BF16, "KTc") for _ in range(3)]
            KX = [c.sb([128, 512], BF16, "KX") for _ in range(3)]
            QF = [c.sb([128, 4, 512], BF16, "QF") for _ in range(2)]
            KF = [c.sb([128, 4, 512], BF16, "KF") for _ in range(2)]
            GF = [c.sb([128, 4, 512], BF16, "GF") for _ in range(2)]
            PT = [c.sb([128, 512], BF16, "PTr") for _ in range(3)]
            QXF = [c.sb([128, 4, 128], BF16, "QXF") for _ in range(3)]
            QXB = [c.sb([128, 4, 128], BF16, "QXB") for _ in range(3)]
            SQ = [c.sb([128, 512], BF16, "SQr") for _ in range(3)]
            RS = [c.sb([128, 512], F32, "RSr") for _ in range(3)]
            OT = [c.sb([128, 4, 128], F32, "OTr") for _ in range(3)]
            OST = [c.sb([128, 4, 512], BF16, "OSTr") for _ in range(2)]
            qv = QC.ap.rearrange("(h q) t -> q h t", q=128)
            kv = KC.ap.rearrange("(h q) t -> q h t", q=128)
            gv = GC.ap.rearrange("(h q) t -> q h t", q=128)
            ov = OC.ap.rearrange("(h q) t -> q h t", q=128)
            cnt = {"k": 0, "sp": 0, "ck": 0}
            for si, (tok0, L, ci) in enumerate(SEQS):
                NCk = L // 128
                if ci == 1:
                    for d_ in range(2):
                        p.dma("sp", S.ap[:, d_], st_ret[l][d_].rearrange("h q e -> q h e"), writes=[S])
                else:
                    p.op("pool", lambda e: e.memset(S.ap, 0.0), writes=[S])
                p.dma("sp", VTM.ap[:, 0:NCk, :], VCT.ap[tok0:tok0 + L, :].rearrange("(n q) f -> q n f", q=128),
                      reads=[VCT], writes=[VTM])
                for i in range(NCk):
                    for d_ in range(2):
                        n = i if d_ == 0 else NCk - 1 - i
                        kt = KTc[cnt["k"] % 3]
                        kx = KX[cnt["k"] % 3]
                        cnt["k"] += 1
                        p.dma("sp", kt.ap, KCT.ap[tok0 + n * 128:tok0 + (n + 1) * 128, :], reads=[KCT], writes=[kt])
                        p.op("pool", lambda e, kt=kt, kx=kx, d_=d_: e.tensor_tensor(out=kx.ap, in0=kt.ap, in1=RKDt.ap[:, d_, :], op=ALU.mult),
                             reads=[kt, RKDt], writes=[kx])
                        p.op("act", lambda e, d_=d_, n=n: e.copy(out=Sbf.ap[:, d_, n, :], in_=S.ap[:, d_].rearrange("q h e -> q (h e)")),
                             reads=[S], writes=[Sbf])
                        pt = ps()
                        for h in range(4):
                            hs = slice(h * 128, (h + 1) * 128)
                            p.op("pe", lambda e, pt=pt, hs=hs, kx=kx, n=n: e.matmul(pt.ap[:, hs], lhsT=kx.ap[:, hs], rhs=VTM.ap[:, n, hs],
                                                                                    start=True, stop=True), reads=[kx, VTM], writes=[pt])
                        for h in range(4):
                            hs = slice(h * 128, (h + 1) * 128)
                            gd = (GAM[h] if d_ == 0 else GAMB[h]) ** 128
                            p.op("dve", lambda e, pt=pt, hs=hs, h=h, d_=d_, gd=gd: e.scalar_tensor_tensor(
                                out=S.ap[:, d_, h, :], in0=S.ap[:, d_, h, :], scalar=float(gd), in1=pt.ap[:, hs],
                                op0=ALU.mult, op1=ALU.add), reads=[S, pt], writes=[S])
                if ci == 0:
                    for d_ in range(2):
                        p.dma("pool", ns_ret.ap[si, l, d_].rearrange("h q e -> q h e"), S.ap[:, d_], reads=[S], writes=[ns_ret])
                for span0 in range(0, L, 512):
                    w = min(512, L - span0)
                    qf, kf, gf, ost = QF[cnt["sp"] % 2], KF[cnt["sp"] % 2], GF[cnt["sp"] % 2], OST[cnt["sp"] % 2]
                    cnt["sp"] += 1
                    a = tok0 + span0
                    p.dma("sp", qf.ap[:, :, :w], qv[:, :, a:a + w], reads=[QC], writes=[qf])
                    p.dma("sp", kf.ap[:, :, :w], kv[:, :, a:a + w], reads=[KC], writes=[kf])
                    p.dma("sp", gf.ap[:, :, :w], gv[:, :, a:a + w], reads=[GC], writes=[gf])
                    pend_c = []
                    for cc in range(w // 128):
                        n = span0 // 128 + cc
                        cs = slice(cc * 128, (cc + 1) * 128)
                        k2 = cnt["ck"] % 3
                        cnt["ck"] += 1
                        ptt, qxf, qxb, sq, rs, ot = PT[k2], QXF[k2], QXB[k2], SQ[k2], RS[k2], OT[k2]
                        pts = ps()
                        for h in range(4):
                            hs = slice(h * 128, (h + 1) * 128)
                            p.op("pe", lambda e, pts=pts, hs=hs, h=h, cs=cs: e.matmul(pts.ap[:, hs], lhsT=kf.ap[:, h, cs], rhs=qf.ap[:, h, cs],
                                                                                      start=True, stop=True), reads=[kf, qf], writes=[pts])
                        p.op("dve", lambda e, pts=pts, ptt=ptt: e.tensor_tensor(out=ptt.ap, in0=pts.ap, in1=RDTt.ap.rearrange("q h c -> q (h c)"),
                                                                                op=ALU.mult), reads=[pts, RDTt], writes=[ptt])
                        p.op("pool", lambda e, qxf=qxf, cs=cs: e.tensor_tensor(out=qxf.ap, in0=qf.ap[:, :, cs], in1=RQDt.ap[:, 0], op=ALU.mult),
                             reads=[qf, RQDt], writes=[qxf])
                        p.op("pool", lambda e, qxb=qxb, cs=cs: e.tensor_tensor(out=qxb.ap, in0=qf.ap[:, :, cs], in1=RQDt.ap[:, 1], op=ALU.mult),
                             reads=[qf, RQDt], writes=[qxb])
                        pto = ps()
                        for h in range(4):
                            hs = slice(h * 128, (h + 1) * 128)
                            p.op("pe", lambda e, pto=pto, hs=hs, h=h, n=n, qxf=qxf: e.matmul(
                                pto.ap[:, hs], lhsT=Sbf.ap[:, 0, n, hs], rhs=qxf.ap[:, h, :], start=True, stop=False),
                                reads=[Sbf, qxf], writes=[pto])
                            p.op("pe", lambda e, pto=pto, hs=hs, h=h, n=n, qxb=qxb: e.matmul(
                                pto.ap[:, hs], lhsT=Sbf.ap[:, 1, n, hs], rhs=qxb.ap[:, h, :], start=False, stop=False),
                                reads=[Sbf, qxb], writes=[pto])
                            p.op("pe", lambda e, pto=pto, hs=hs, h=h, n=n, ptt=ptt: e.matmul(
                                pto.ap[:, hs], lhsT=VTM.ap[:, n, hs], rhs=ptt.ap[:, hs], start=False, stop=True),
                                reads=[VTM, ptt], writes=[pto])
                        p.op("act", lambda e, pto=pto, sq=sq: e.activation(out=sq.ap, in_=pto.ap, func=AF.Square), reads=[pto], writes=[sq])

                        def post_c(pto=pto, sq=sq, rs=rs, ot=ot, cs=cs, gf=gf, ost=ost):
                            _post_c(pto, sq, rs, ot, cs, gf, ost)
                        if pend_c:
                            pend_c.pop(0)()
                        pend_c.append(post_c)
                    while pend_c:
                        pend_c.pop(0)()
                    p.dma("pool", ov[:, :, a:a + w], ost.ap[:, :, :w], reads=[ost], writes=[OC])
        p.barrier()

    def _post_c(pto, sq, rs, ot, cs, gf, ost):
        if True:
            if True:
                if True:
                    if True:
                        ptn = ps()
                        p.op("pe", lambda e, ptn=ptn, sq=sq: e.matmul(ptn.ap, lhsT=onesb.ap, rhs=sq.ap, start=True, stop=True),
                             reads=[onesb, sq], writes=[ptn])
                        p.op("act", lambda e, ptn=ptn, rs=rs: e.activation(out=rs.ap, in_=ptn.ap, func=AF.Ln, scale=1.0 / 128, bias=epsc.ap[:, 0:1]),
                             reads=[ptn, epsc], writes=[rs])
                        p.op("act", lambda e, rs=rs: e.activation(out=rs.ap, in_=rs.ap, func=AF.Exp, scale=-0.5), reads=[rs], writes=[rs])
                        p.op("dve", lambda e, pto=pto, rs=rs, ot=ot: e.tensor_tensor(out=ot.ap.rearrange("q h c -> q (h c)"), in0=pto.ap, in1=rs.ap,
                                                                                     op=ALU.mult), reads=[pto, rs], writes=[ot])
                        p.op("pool", lambda e, ot=ot, cs=cs: e.tensor_tensor(out=ost.ap[:, :, cs], in0=ot.ap, in1=gf.ap[:, :, cs], op=ALU.mult),
                             reads=[ot, gf], writes=[ost])

    OAF = dscr("oaf", [512, TTOT], F32)
    OAB = dscr("oab", [512, TTOT], F32)

    def mix_A(l):
        with ExitStack() as es:
            c = Ctx(p, es)
            UT = c.sb([128, 2, 128], F32, "UT")
            NM = c.sb([128, 4, 128], F32, "NM")
            SEL = c.sb([4, 4, 128], F32, "SEL")
            NLM = c.sb([128, 2, 7, 128], BF16, "NLM")
            NLMf = c.sb([128, 2, 7, 128], F32, "NLMf")
            p.dma("sp", NLMf.ap, c_nlm, writes=[NLMf])
            p.op("dve", lambda e: e.tensor_copy(out=NLM.ap, in_=NLMf.ap), reads=[NLMf], writes=[NLM])
            na = c.sb([128, 1], F32, "na")
            p.dma("sp", UT.ap, c_ut, writes=[UT])
            p.dma("sp", NM.ap, c_nm, writes=[NM])
            p.dma("sp", SEL.ap, c_sel, writes=[SEL])
            SELb = c.sb([4, 4, 128], BF16, "SELb")
            p.op("dve", lambda e: e.tensor_copy(out=SELb.ap, in_=SEL.ap), reads=[SEL], writes=[SELb])
            p.dma("sp", na.ap, norm_a[l].rearrange("(q o) -> q o", o=1), writes=[na])
            S = c.sb([128, 2, 4, 128], F32, "Sd")
            Sbf = c.sb([128, 2, 512], BF16, "Sdbf")

            KI = 6

            cg = None
            if l == 0 and LATE_CAST:
                cg = cast_gen(c, [1], CW=1024, nbuf=2, lq="actq")
            esm = ExitStack()
            cur = {"c": Ctx(p, esm)}

            def rot(shape, dt, name, n=KI):
                return [cur["c"].sb(shape, dt, name) for _ in range(n)]
            GQ, GK = rot([128, 4, 6 * 128], BF16, "GQ", 2), rot([128, 4, 6 * 128], BF16, "GK", 2)
            GKT, GVT = rot([128, 6, 512], BF16, "GKT", 2), rot([128, 6, 512], BF16, "GVT", 2)
            GG, GB = rot([128, 6, 8], F32, "GG", 2), rot([128, 6, 8], F32, "GB", 2)
            gcol, ngcol, bgc, negb, kgs, egl, glc = (rot([128, 4], F32, nm_) for nm_ in ("gcol", "ngcol", "bgc", "negb", "kgs", "egl", "glc"))
            gcr = rot([4, 128], F32, "gcr")
            gcrh = rot([4, 128], BF16, "gcrh")
            gcrl = rot([4, 128], BF16, "gcrl")
            E_, ET_, eR_ = rot([128, 4, 128], F32, "E_"), rot([128, 4, 128], F32, "ET_"), rot([128, 4, 128], F32, "eR_", 3)
            XT_, Xm_, AT_, QG_, P_ = (rot([128, 4, 128], BF16, nm_) for nm_ in ("XT_", "Xm_", "AT_", "QG_", "P_"))
            Mm_, Zt_, TMb_ = (rot([128, 4, 128], BF16, nm_) for nm_ in ("Mm_", "Zt_", "TMb_"))
            KBG_, VB_, KG_, NWT_, VN_ = (rot([128, 4, 128], BF16, nm_) for nm_ in ("KBG_", "VB_", "KG_", "NWT_", "VN_"))
            OSTa = rot([128, 4, 128], F32, "OSTa", 3)
            cnt = {"i": 0, "ev": 0}

            def evac(fn_act, fn_dve, reads, writes):
                cnt["ev"] += 1
                if cnt["ev"] % 2 == 0:
                    p.op("act", fn_act, reads=reads, writes=writes)
                else:
                    p.op("dve", fn_dve, reads=reads, writes=writes)

            def f2(t_):
                return t_.ap.rearrange("q h c -> q (h c)")

            qv = QA.ap.rearrange("(h q) t -> q h t", q=128)
            kv = KA.ap.rearrange("(h q) t -> q h t", q=128)
            ofv = [OAF.ap.rearrange("(h q) t -> q h t", q=128), OAB.ap.rearrange("(h q) t -> q h t", q=128)]
            odst = [OAF, OAB]
            for si, (tok0, L, ci) in enumerate(SEQS):
                NCk = L // 128
                if ci == 1:
                    for d_ in range(2):
                        p.dma("sp", S.ap[:, d_], st_delta[l][d_].rearrange("h q e -> q h e"), writes=[S])
                else:
                    p.op("pool", lambda e: e.memset(S.ap, 0.0), writes=[S])
                p.op("act", lambda e: e.copy(out=Sbf.ap, in_=S.ap.rearrange("q d h e -> q d (h e)")), reads=[S], writes=[Sbf])
                items = []
                for i in range(NCk):
                    for d_ in range(2):
                        items.append((d_, i if d_ == 0 else NCk - 1 - i))

                def load_group(grp_, gs):
                    slots = {}
                    for dd, s0 in ((0, 0), (1, 3)):
                        ns = sorted(n_ for (d2, n_) in grp_ if d2 == dd)
                        if not ns:
                            continue
                        a0 = tok0 + ns[0] * 128
                        w = len(ns) * 128
                        for n_ in ns:
                            slots[(dd, n_)] = s0 + (n_ - ns[0])
                        p.dma("sp", GQ[gs].ap[:, :, s0 * 128:s0 * 128 + w], qv[:, :, a0:a0 + w], reads=[QA], writes=[GQ[gs]])
                        p.dma("sp", GK[gs].ap[:, :, s0 * 128:s0 * 128 + w], kv[:, :, a0:a0 + w], reads=[KA], writes=[GK[gs]])
                        p.dma("sp", GKT[gs].ap[:, s0:s0 + len(ns), :], KAT.ap[a0:a0 + w, :].rearrange("(n q) f -> q n f", q=128),
                              reads=[KAT], writes=[GKT[gs]])
                        p.dma("sp", GVT[gs].ap[:, s0:s0 + len(ns), :], VAT.ap[a0:a0 + w, :].rearrange("(n q) f -> q n f", q=128),
                              reads=[VAT], writes=[GVT[gs]])
                        p.dma("sp", GG[gs].ap[:, s0:s0 + len(ns), :], GA.ap[a0:a0 + w, :].rearrange("(n q) j -> q n j", q=128),
                              reads=[GA], writes=[GG[gs]])
                        p.dma("sp", GB[gs].ap[:, s0:s0 + len(ns), :], BA.ap[a0:a0 + w, :].rearrange("(n q) j -> q n j", q=128),
                              reads=[BA], writes=[GB[gs]])
                    return slots

                def prep_gen(d_, n, r2, gs, slot):
                    a = tok0 + n * 128
                    qf = T(GQ[gs].ap[:, :, slot * 128:(slot + 1) * 128], GQ[gs].key)
                    kf = T(GK[gs].ap[:, :, slot * 128:(slot + 1) * 128], GK[gs].key)
                    kt = T(GKT[gs].ap[:, slot, :].rearrange("q (h c) -> q h c", h=4), GKT[gs].key)
                    vt = T(GVT[gs].ap[:, slot, :].rearrange("q (h c) -> q h c", h=4), GVT[gs].key)
                    gg = T(GG[gs].ap[:, slot, :], GG[gs].key)
                    bb = T(GB[gs].ap[:, slot, :], GB[gs].key)
                    gco, ngc, bg, nb_, kg_s, eg, gl = gcol[r2], ngcol[r2], bgc[r2], negb[r2], kgs[r2], egl[r2], glc[r2]
                    gr = gcr[r2]
                    E, ET, eR = E_[r2], ET_[r2], eR_[r2 % 3]
                    XT, Xm, AT, QG, P = XT_[r2], Xm_[r2], AT_[r2], QG_[r2], P_[r2]
                    KBG, VB, KG, NWT, VN = KBG_[r2], VB_[r2], KG_[r2], NWT_[r2], VN_[r2]
                    ds_ = slice(d_ * 4, d_ * 4 + 4)
                    yield
                    pg = ps()
                    p.op("pe", lambda e, pg=pg, gg=gg, ds_=ds_, d_=d_: e.matmul(pg.ap[:, 0:4], lhsT=UT.ap[:, d_, :], rhs=gg.ap[:, ds_],
                                                                                start=True, stop=True), reads=[UT, gg], writes=[pg])
                    p.op("pe", lambda e, pg=pg, gg=gg, ds_=ds_: e.matmul(pg.ap[:, 4:8], lhsT=ones.ap, rhs=gg.ap[:, ds_],
                                                                         start=True, stop=True), reads=[ones, gg], writes=[pg])
                    p.op("dve", lambda e, pg=pg, gco=gco: e.tensor_copy(out=gco.ap, in_=pg.ap[:, 0:4]), reads=[pg], writes=[gco])
                    yield
                    p.op("dve", lambda e, pg=pg, ngc=ngc: e.tensor_scalar(out=ngc.ap, in0=pg.ap[:, 0:4], scalar1=-1.0, scalar2=None, op0=ALU.mult),
                         reads=[pg], writes=[ngc])
                    p.op("dve", lambda e, pg=pg, gl=gl: e.tensor_copy(out=gl.ap, in_=pg.ap[:, 4:8]), reads=[pg], writes=[gl])
                    p.op("act", lambda e, gco=gco, bg=bg: e.activation(out=bg.ap, in_=gco.ap, func=AF.Exp), reads=[gco], writes=[bg])
                    p.op("dve", lambda e, bg=bg, bb=bb, ds_=ds_: e.tensor_tensor(out=bg.ap, in0=bg.ap, in1=bb.ap[:, ds_], op=ALU.mult),
                         reads=[bg, bb], writes=[bg])
                    p.op("dve", lambda e, nb_=nb_, bb=bb, ds_=ds_: e.tensor_scalar(out=nb_.ap, in0=bb.ap[:, ds_], scalar1=-1.0, scalar2=None, op0=ALU.mult),
                         reads=[bb], writes=[nb_])
                    p.op("dve", lambda e, kg_s=kg_s, gl=gl, gco=gco: e.tensor_tensor(out=kg_s.ap, in0=gl.ap, in1=gco.ap, op=ALU.subtract),
                         reads=[gl, gco], writes=[kg_s])
                    p.op("act", lambda e, kg_s=kg_s: e.activation(out=kg_s.ap, in_=kg_s.ap, func=AF.Exp), reads=[kg_s], writes=[kg_s])
                    p.op("act", lambda e, eg=eg, gl=gl: e.activation(out=eg.ap, in_=gl.ap, func=AF.Exp), reads=[gl], writes=[eg])
                    yield
                    ptr = ps()
                    p.op("pe", lambda e, ptr=ptr, gco=gco: e.transpose(out=ptr.ap[0:4, 0:128], in_=gco.ap, identity=ident.ap),
                         reads=[gco, ident], writes=[ptr])
                    p.op("dve", lambda e, ptr=ptr, gr=gr: e.tensor_copy(out=gr.ap, in_=ptr.ap[0:4, 0:128]), reads=[ptr], writes=[gr])
                    grh, grl = gcrh[r2], gcrl[r2]
                    p.op("dve", lambda e, gr=gr, grh=grh: e.tensor_copy(out=grh.ap, in_=gr.ap), reads=[gr], writes=[grh])
                    p.op("dve", lambda e, gr=gr, grh=grh, grl=grl: e.tensor_tensor(out=grl.ap, in0=gr.ap, in1=grh.ap, op=ALU.subtract),
                         reads=[gr, grh], writes=[grl])
                    yield
                    pR = ps()
                    for h in range(4):
                        hs = slice(h * 128, (h + 1) * 128)
                        p.op("pe", lambda e, pR=pR, hs=hs, h=h, grh=grh: e.matmul(pR.ap[:, hs], lhsT=SELb.ap[:, h, :], rhs=grh.ap,
                                                                                  start=True, stop=False), reads=[SELb, grh], writes=[pR])
                        p.op("pe", lambda e, pR=pR, hs=hs, h=h, grl=grl: e.matmul(pR.ap[:, hs], lhsT=SELb.ap[:, h, :], rhs=grl.ap,
                                                                                  start=False, stop=True), reads=[SELb, grl], writes=[pR])
                    pRv = pR.ap.rearrange("q (h c) -> q h c", h=4)
                    mi = 0 if d_ == 0 else 1
                    ms = 2 if d_ == 0 else 3
                    yield
                    p.op("dve", lambda e, pRv=pRv, E=E, mi=mi: e.tensor_tensor(
                        out=E.ap, in0=pRv, in1=NM.ap[:, mi, :].unsqueeze(1).broadcast_to([128, 4, 128]), op=ALU.add),
                        reads=[pR, NM], writes=[E])
                    p.op("dve", lambda e, pRv=pRv, ET=ET, ms=ms: e.scalar_tensor_tensor(
                        out=ET.ap, in0=pRv, scalar=-1.0, in1=NM.ap[:, ms, :].unsqueeze(1).broadcast_to([128, 4, 128]),
                        op0=ALU.mult, op1=ALU.add), reads=[pR, NM], writes=[ET])
                    for h in range(4):
                        p.op("act", lambda e, E=E, h=h, ngc=ngc: e.activation(out=E.ap[:, h, :], in_=E.ap[:, h, :], func=AF.Exp,
                                                                             bias=ngc.ap[:, h:h + 1]), reads=[E, ngc], writes=[E])
                        p.op("act", lambda e, ET=ET, h=h, gco=gco: e.activation(out=ET.ap[:, h, :], in_=ET.ap[:, h, :], func=AF.Exp,
                                                                               bias=gco.ap[:, h:h + 1]), reads=[ET, gco], writes=[ET])
                    p.op("act", lambda e, eR=eR, pRv=pRv: e.activation(out=eR.ap, in_=pRv, func=AF.Exp), reads=[pR], writes=[eR])
                    p.op("pool", lambda e, QG=QG, qf=qf, eR=eR: e.tensor_tensor(out=QG.ap, in0=qf.ap, in1=eR.ap, op=ALU.mult),
                         reads=[qf, eR], writes=[QG])
                    yield
                    pkk = ps()
                    for h in range(4):
                        hs = slice(h * 128, (h + 1) * 128)
                        p.op("pe", lambda e, pkk=pkk, hs=hs, h=h, kf=kf: e.matmul(pkk.ap[:, hs], lhsT=kf.ap[:, h, :], rhs=kf.ap[:, h, :],
                                                                                  start=True, stop=True), reads=[kf], writes=[pkk])
                    yield
                    for h in range(4):
                        hs = slice(h * 128, (h + 1) * 128)
                        p.op("dve", lambda e, pkk=pkk, hs=hs, h=h, XT=XT, ET=ET, nb_=nb_: e.scalar_tensor_tensor(
                            out=XT.ap[:, h, :], in0=pkk.ap[:, hs], scalar=nb_.ap[:, h:h + 1], in1=ET.ap[:, h, :],
                            op0=ALU.mult, op1=ALU.mult), reads=[pkk, nb_, ET], writes=[XT])
                    yield
                    pqk = ps()
                    for h in range(4):
                        hs = slice(h * 128, (h + 1) * 128)
                        p.op("pe", lambda e, pqk=pqk, hs=hs, h=h, kf=kf, qf=qf: e.matmul(pqk.ap[:, hs], lhsT=kf.ap[:, h, :], rhs=qf.ap[:, h, :],
                                                                                         start=True, stop=True), reads=[kf, qf], writes=[pqk])
                    yield
                    p.op("dve", lambda e, pqk=pqk, AT=AT, E=E: e.tensor_tensor(out=f2(AT), in0=pqk.ap, in1=f2(E), op=ALU.mult),
                         reads=[pqk, E], writes=[AT])
                    yield
                    ptx = ps()
                    ptxb = ptx.ap.bitcast(BF16)
                    for h in range(4):
                        hs = slice(h * 128, (h + 1) * 128)
                        p.op("pe", lambda e, ptxb=ptxb, hs=hs, h=h, XT=XT: e.transpose(out=ptxb[:, hs], in_=XT.ap[:, h, :], identity=identb.ap),
                             reads=[XT, identb], writes=[ptx])
                    p.op("act", lambda e, ptxb=ptxb, Xm=Xm: e.copy(out=f2(Xm), in_=ptxb[:, 0:512]), reads=[ptx], writes=[Xm])
                    yield
                    Mm, Zt, TMb = Mm_[r2], Zt_[r2], TMb_[r2]
                    p.op("dve", lambda e, P=P, Xm=Xm, d_=d_: e.tensor_tensor(
                        out=P.ap, in0=Xm.ap, in1=NLM.ap[:, d_, 0, :].unsqueeze(1).broadcast_to([128, 4, 128]), op=ALU.mult),
                        reads=[Xm, NLM], writes=[P])
                    p.op("dve", lambda e, P=P: e.tensor_tensor(out=P.ap, in0=P.ap, in1=identb.ap.unsqueeze(1).broadcast_to([128, 4, 128]),
                                                                op=ALU.add), reads=[P, identb], writes=[P])
                    p.op("dve", lambda e, Mm=Mm, XT=XT, d_=d_: e.tensor_tensor(
                        out=Mm.ap, in0=XT.ap, in1=NLM.ap[:, 1 - d_, 0, :].unsqueeze(1).broadcast_to([128, 4, 128]), op=ALU.mult),
                        reads=[XT, NLM], writes=[Mm])
                    p.op("dve", lambda e, Mm=Mm: e.tensor_tensor(out=Mm.ap, in0=Mm.ap, in1=identb.ap.unsqueeze(1).broadcast_to([128, 4, 128]),
                                                                  op=ALU.add), reads=[Mm, identb], writes=[Mm])
                    yield
                    for lev in range(1, 7):
                        pz = ps()
                        for h in range(4):
                            hs = slice(h * 128, (h + 1) * 128)
                            p.op("pe", lambda e, pz=pz, hs=hs, h=h, XT=XT, P=P: e.matmul(pz.ap[:, hs], lhsT=XT.ap[:, h, :], rhs=P.ap[:, h, :],
                                                                                         start=True, stop=True), reads=[XT, P], writes=[pz])
                        yield
                        p.op("act", lambda e, pz=pz, Zt=Zt: e.copy(out=f2(Zt), in_=pz.ap), reads=[pz], writes=[Zt])
                        yield
                        pwq = ps()
                        for h in range(4):
                            hs = slice(h * 128, (h + 1) * 128)
                            p.op("pe", lambda e, pwq=pwq, hs=hs, h=h, Mm=Mm, Zt=Zt: e.matmul(pwq.ap[:, hs], lhsT=Mm.ap[:, h, :], rhs=Zt.ap[:, h, :],
                                                                                             start=True, stop=True), reads=[Mm, Zt], writes=[pwq])
                        yield
                        p.op("dve", lambda e, pwq=pwq, TMb=TMb, d_=d_, lev=lev: e.tensor_tensor(
                            out=TMb.ap, in0=pwq.ap.rearrange("q (h c) -> q h c", h=4),
                            in1=NLM.ap[:, d_, lev, :].unsqueeze(1).broadcast_to([128, 4, 128]), op=ALU.mult), reads=[pwq, NLM], writes=[TMb])
                        p.op("dve", lambda e, P=P, TMb=TMb: e.tensor_tensor(out=P.ap, in0=P.ap, in1=TMb.ap, op=ALU.add), reads=[P, TMb], writes=[P])
                        yield
                        if lev < 6:
                            ptm = ps()
                            ptmb = ptm.ap.bitcast(BF16)
                            for h in range(4):
                                hs = slice(h * 128, (h + 1) * 128)
                                p.op("pe", lambda e, ptmb=ptmb, hs=hs, h=h, P=P: e.transpose(out=ptmb[:, hs], in_=P.ap[:, h, :], identity=identb.ap),
                                     reads=[P, identb], writes=[ptm])
                            yield
                            p.op("act", lambda e, ptmb=ptmb, Mm=Mm: e.copy(out=f2(Mm), in_=ptmb[:, 0:512]), reads=[ptm], writes=[Mm])
                            yield
                    yield
                    p.op("pool", lambda e, KBG=KBG, kt=kt, bg=bg: e.tensor_tensor(
                        out=KBG.ap, in0=kt.ap, in1=bg.ap.unsqueeze(2).broadcast_to([128, 4, 128]), op=ALU.mult), reads=[kt, bg], writes=[KBG])
                    p.op("pool", lambda e, VB=VB, vt=vt, bb=bb, ds_=ds_: e.tensor_tensor(
                        out=VB.ap, in0=vt.ap, in1=bb.ap[:, ds_].unsqueeze(2).broadcast_to([128, 4, 128]), op=ALU.mult), reads=[vt, bb], writes=[VB])
                    p.op("pool", lambda e, KG=KG, kt=kt, kg_s=kg_s: e.tensor_tensor(
                        out=KG.ap, in0=kt.ap, in1=kg_s.ap.unsqueeze(2).broadcast_to([128, 4, 128]), op=ALU.mult), reads=[kt, kg_s], writes=[KG])
                    yield
                    pw = ps()
                    for h in range(4):
                        hs = slice(h * 128, (h + 1) * 128)
                        p.op("pe", lambda e, pw=pw, hs=hs, h=h, KBG=KBG, P=P: e.matmul(pw.ap[:, hs], lhsT=KBG.ap[:, h, :], rhs=P.ap[:, h, :],
                                                                                       start=True, stop=True), reads=[KBG, P], writes=[pw])
                    p.op("act", lambda e, pw=pw, NWT=NWT: e.mul(out=f2(NWT), in_=pw.ap, mul=-1.0), reads=[pw], writes=[NWT])

                def rec_gen(d_, n, r2):
                    a = tok0 + n * 128
                    gco, ngc, bg, nb_, kg_s, eg, gl = gcol[r2], ngcol[r2], bgc[r2], negb[r2], kgs[r2], egl[r2], glc[r2]
                    gr = gcr[r2]
                    E, ET, eR = E_[r2], ET_[r2], eR_[r2 % 3]
                    XT, Xm, AT, QG, P = XT_[r2], Xm_[r2], AT_[r2], QG_[r2], P_[r2]
                    KBG, VB, KG, NWT, VN = KBG_[r2], VB_[r2], KG_[r2], NWT_[r2], VN_[r2]
                    ds_ = slice(d_ * 4, d_ * 4 + 4)
                    pv = ps()
                    for h in range(4):
                        hs = slice(h * 128, (h + 1) * 128)
                        p.op("pe", lambda e, pv=pv, hs=hs, h=h, P=P, VB=VB: e.matmul(pv.ap[:, hs], lhsT=P.ap[:, h, :], rhs=VB.ap[:, h, :],
                                                                                     start=True, stop=False), reads=[P, VB], writes=[pv])
                        p.op("pe", lambda e, pv=pv, hs=hs, h=h, NWT=NWT, d_=d_: e.matmul(pv.ap[:, hs], lhsT=NWT.ap[:, h, :], rhs=Sbf.ap[:, d_, hs],
                                                                                         start=False, stop=True), reads=[NWT, Sbf], writes=[pv])
                    yield
                    p.op("act", lambda e, pv=pv, VN=VN: e.copy(out=f2(VN), in_=pv.ap), reads=[pv], writes=[VN])
                    yield
                    po = ps()
                    for h in range(4):
                        hs = slice(h * 128, (h + 1) * 128)
                        p.op("pe", lambda e, po=po, hs=hs, h=h, QG=QG, d_=d_: e.matmul(po.ap[:, hs], lhsT=Sbf.ap[:, d_, hs], rhs=QG.ap[:, h, :],
                                                                                       start=True, stop=False), reads=[Sbf, QG], writes=[po])
                        p.op("pe", lambda e, po=po, hs=hs, h=h, VN=VN, AT=AT: e.matmul(po.ap[:, hs], lhsT=VN.ap[:, h, :], rhs=AT.ap[:, h, :],
                                                                                       start=False, stop=True), reads=[VN, AT], writes=[po])
                    yield
                    ost = OSTa[r2 % 3]
                    p.op("dve", lambda e, po=po, ost=ost: e.tensor_copy(out=f2(ost), in_=po.ap), reads=[po], writes=[ost])
                    p.dma("pool", ofv[d_][:, :, a:a + 128], ost.ap, reads=[ost], writes=[odst[d_]])
                    pss = ps()
                    for h in range(4):
                        hs = slice(h * 128, (h + 1) * 128)
                        p.op("pe", lambda e, pss=pss, hs=hs, h=h, KG=KG, VN=VN: e.matmul(pss.ap[:, hs], lhsT=KG.ap[:, h, :], rhs=VN.ap[:, h, :],
                                                                                         start=True, stop=True), reads=[KG, VN], writes=[pss])
                    for h in range(4):
                        hs = slice(h * 128, (h + 1) * 128)
                        p.op("dve", lambda e, pss=pss, hs=hs, h=h, d_=d_, eg=eg: e.scalar_tensor_tensor(
                            out=S.ap[:, d_, h, :], in0=S.ap[:, d_, h, :], scalar=eg.ap[:, h:h + 1], in1=pss.ap[:, hs],
                            op0=ALU.mult, op1=ALU.add), reads=[S, eg, pss], writes=[S])
                    p.op("act", lambda e, d_=d_: e.copy(out=Sbf.ap[:, d_, :], in_=S.ap[:, d_].rearrange("q h e -> q (h e)")),
                         reads=[S], writes=[Sbf])

                def lockstep(gens, stagger=0):
                    gens = list(gens)
                    done = [False] * len(gens)
                    r_ = 0
                    while not all(done):
                        for j_, g_ in enumerate(gens):
                            if done[j_] or r_ < j_ * stagger:
                                continue
                            try:
                                next(g_)
                            except StopIteration:
                                done[j_] = True
                        r_ += 1

                groups = [items[g0:g0 + KI] for g0 in range(0, len(items), KI)]
                slotmaps = {0: load_group(groups[0], 0)}
                for gi_, grp in enumerate(groups):
                    if gi_ + 1 < len(groups):
                        slotmaps[gi_ + 1] = load_group(groups[gi_ + 1], (gi_ + 1) % 2)
                    sm_ = slotmaps.pop(gi_)
                    lockstep([prep_gen(d_, n, r_, gi_ % 2, sm_[(d_, n)]) for r_, (d_, n) in enumerate(grp)], stagger=dbg_opts.get("stagger", 6))
                    for q0 in range(0, len(grp), 2):
                        lockstep([rec_gen(d_, n, q0 + r_) for r_, (d_, n) in enumerate(grp[q0:q0 + 2])])
                    if cg is not None:
                        for _ in range(12):
                            if next(cg, "done") == "done":
                                cg = None
                                break
                if ci == 0:
                    for d_ in range(2):
                        p.dma("pool", ns_delta.ap[si, l, d_].rearrange("h q e -> q h e"), S.ap[:, d_], reads=[S], writes=[ns_delta])
            if cg is not None:
                for _ in cg:
                    pass
            p.barrier()
            esm.close()
            cur["c"] = c
            OFs = rot([128, 4, 512], F32, "OFs", 2)
            OBs = rot([128, 4, 512], F32, "OBs", 2)
            ZFs = rot([128, 4, 512], BF16, "ZFs", 2)
            OOs = rot([128, 4, 512], BF16, "OOs", 2)
            SQa = rot([128, 512], BF16, "SQa", 2)
            RSa = rot([128, 512], F32, "RSa", 2)
            zv = ZA.ap.rearrange("(h q) t -> q h t", q=128)
            oav = OA.ap.rearrange("(h q) t -> q h t", q=128)
            for bi, t0 in enumerate(range(0, TTOT, 512)):
                of_, ob_, zf, oo = OFs[bi % 2], OBs[bi % 2], ZFs[bi % 2], OOs[bi % 2]
                p.dma("sp", of_.ap, ofv[0][:, :, t0:t0 + 512], reads=[OAF], writes=[of_])
                p.dma("sp", ob_.ap, ofv[1][:, :, t0:t0 + 512], reads=[OAB], writes=[ob_])
                p.dma("sp", zf.ap, zv[:, :, t0:t0 + 512], reads=[ZA], writes=[zf])
                p.op("pool", lambda e, of_=of_, ob_=ob_: e.tensor_tensor(out=of_.ap, in0=of_.ap, in1=ob_.ap, op=ALU.add), reads=[of_, ob_], writes=[of_])
                for h in range(4):
                    sq, rs = SQa[h % 2], RSa[h % 2]
                    p.op("act", lambda e, sq=sq, of_=of_, h=h: e.activation(out=sq.ap, in_=of_.ap[:, h, :], func=AF.Square), reads=[of_], writes=[sq])
                    ptn = ps()
                    p.op("pe", lambda e, ptn=ptn, sq=sq: e.matmul(ptn.ap, lhsT=onesb.ap, rhs=sq.ap, start=True, stop=True), reads=[onesb, sq], writes=[ptn])
                    p.op("act", lambda e, ptn=ptn, rs=rs: e.activation(out=rs.ap, in_=ptn.ap, func=AF.Ln, scale=1.0 / 128, bias=epsc.ap[:, 0:1]),
                         reads=[ptn, epsc], writes=[rs])
                    p.op("act", lambda e, rs=rs: e.activation(out=rs.ap, in_=rs.ap, func=AF.Exp, scale=-0.5), reads=[rs], writes=[rs])
                    p.op("dve", lambda e, rs=rs, of_=of_, h=h: e.scalar_tensor_tensor(
                        out=rs.ap, in0=of_.ap[:, h, :], scalar=na.ap[:, 0:1], in1=rs.ap, op0=ALU.mult, op1=ALU.mult), reads=[of_, na, rs], writes=[rs])
                    p.op("pool", lambda e, rs=rs, zf=zf, oo=oo, h=h: e.tensor_tensor(out=oo.ap[:, h, :], in0=rs.ap, in1=zf.ap[:, h, :], op=ALU.mult),
                         reads=[rs, zf], writes=[oo])
                p.dma("pool", oav[:, :, t0:t0 + 512], oo.ap, reads=[oo], writes=[OA])
        p.barrier()

    YSCR = dscr("yscr", [8, 16, 32, TTOT // 8], F32)
    MAGIC = 12582912.0
    TWO_PI_LO = 6.2831845
    NCP = NSEQ_P * LP // 8
    NCS = LS // 8

    def mix_B(l):
        with ExitStack() as es:
            c = Ctx(p, es)
            Tg = c.sb([128, 32, 128], BF16, "Tg")
            Pm = [[c.sb([128, 32, 128], BF16, "Pm%d%d" % (d_, v_)) for v_ in range(2)] for d_ in range(2)]
            Qm = [[c.sb([128, 32, 128], BF16, "Qm%d%d" % (d_, v_)) for v_ in range(2)] for d_ in range(2)]
            RHO = c.sb([128, 2, 32], F32, "RHO")
            F8 = c.sb([128, 2, 32], F32, "F8")
            X0 = c.sb([128, 2, 32], F32, "X0")
            KV513 = c.sb([128, 513], F32, "KV513")
            hpi = c.sb([128, 1], F32, "hpi")
            p.dma("sp", KV513.ap, c_kv513, writes=[KV513])
            p.op("dve", lambda e: e.memset(hpi.ap, math.pi / 2), writes=[hpi])

            def rnd_frac(cx, t_, tmp_):
                p.op("dve", lambda e: e.tensor_scalar(out=tmp_.ap, in0=t_.ap, scalar1=MAGIC, scalar2=None, op0=ALU.add), reads=[t_], writes=[tmp_])
                p.op("dve", lambda e: e.tensor_scalar(out=tmp_.ap, in0=tmp_.ap, scalar1=-MAGIC, scalar2=None, op0=ALU.add), reads=[tmp_], writes=[tmp_])
                p.op("dve", lambda e: e.tensor_tensor(out=t_.ap, in0=t_.ap, in1=tmp_.ap, op=ALU.subtract), reads=[t_, tmp_], writes=[t_])

            def sincos_turns(t_, tmp_, sin_o, cos_o):
                rnd_frac(None, t_, tmp_)
                p.op("act", lambda e: e.activation(out=sin_o.ap, in_=t_.ap, func=AF.Sin, scale=TWO_PI_LO), reads=[t_], writes=[sin_o])
                p.op("dve", lambda e: e.scalar_tensor_tensor(out=tmp_.ap, in0=t_.ap, scalar=-1.0, in1=t_.ap, op0=ALU.mult, op1=ALU.max), reads=[t_], writes=[tmp_])
                p.op("act", lambda e: e.activation(out=cos_o.ap, in_=tmp_.ap, func=AF.Sin, scale=-TWO_PI_LO, bias=hpi.ap[:, 0:1]),
                     reads=[tmp_, hpi], writes=[cos_o])

            with ExitStack() as es2:
                t = Ctx(p, es2)
                LR = t.sb([128, 2, 32], F32, "LR")
                LI = t.sb([128, 2, 32], F32, "LI")
                DT = t.sb([128, 2, 32], F32, "DT")
                AR = t.sb([128, 2, 32], F32, "AR")
                TH = t.sb([128, 2, 32], F32, "TH")
                for half in range(2):
                    hs_ = slice(half * 64, half * 64 + 64)
                    for d_ in range(2):
                        p.dma("sp", LR.ap[hs_, d_, :], lam_re[l][d_].rearrange("g q -> q g"), writes=[LR])
                        p.dma("sp", LI.ap[hs_, d_, :], lam_im[l][d_].rearrange("g q -> q g"), writes=[LI])
                        p.dma("sp", X0.ap[hs_, d_, :], (st_re if half == 0 else st_im)[l][d_].rearrange("g q -> q g"), writes=[X0])
                p.dma("sp", DT.ap.rearrange("q d g -> q (d g)"), log_dt[l].rearrange("d g -> (d g)").partition_broadcast(128), writes=[DT])
                p.op("act", lambda e: e.activation(out=DT.ap, in_=DT.ap, func=AF.Exp), reads=[DT], writes=[DT])
                p.op("dve", lambda e: e.tensor_tensor(out=AR.ap, in0=LR.ap, in1=DT.ap, op=ALU.mult), reads=[LR, DT], writes=[AR])
                p.op("dve", lambda e: e.tensor_tensor(out=TH.ap, in0=LI.ap, in1=DT.ap, op=ALU.mult), reads=[LI, DT], writes=[TH])
                KVt = t.sb([128, 16], F32, "KVt")
                KVm = t.sb([128, 16], F32, "KVm")
                p.dma("sp", KVt.ap, c_kv16[0].partition_broadcast(128), writes=[KVt])
                p.dma("sp", KVm.ap, c_kv16[1].partition_broadcast(128), writes=[KVm])
                PWR = t.sb([128, 16, 64], F32, "PWR")
                PWI = t.sb([128, 16, 64], F32, "PWI")
                F8t = t.sb([128, 2, 32], F32, "F8t")
                PAIRS = {}
                for nm_, sh_ in (("BReIm", [128, 2, 32, 16]), ("BImRe", [128, 2, 32, 16]), ("CReIm", [128, 32, 16]), ("CImRe", [128, 32, 16])):
                    PAIRS[nm_] = (t.sb(sh_, F32, nm_ + "a"), t.sb(sh_, F32, nm_ + "b"))
                esA = ExitStack()
                tA = Ctx(p, esA)
                MAG = tA.sb([128, 16, 64], F32, "MAG")
                TT = tA.sb([128, 16, 64], F32, "TT")
                TT2 = tA.sb([128, 16, 64], F32, "TT2")
                thb = TH.ap.rearrange("q d g -> q (d g)").unsqueeze(1).broadcast_to([128, 16, 64])
                arb = AR.ap.rearrange("q d g -> q (d g)").unsqueeze(1).broadcast_to([128, 16, 64])
                p.op("dve", lambda e: e.tensor_tensor(out=TT.ap, in0=thb, in1=KVt.ap.unsqueeze(2).broadcast_to([128, 16, 64]), op=ALU.mult),
                     reads=[TH, KVt], writes=[TT])
                p.op("dve", lambda e: e.tensor_tensor(out=MAG.ap, in0=arb, in1=KVm.ap.unsqueeze(2).broadcast_to([128, 16, 64]), op=ALU.mult),
                     reads=[AR, KVm], writes=[MAG])
                p.op("act", lambda e: e.activation(out=MAG.ap, in_=MAG.ap, func=AF.Exp), reads=[MAG], writes=[MAG])
                sincos_turns(TT, TT2, PWI, PWR)
                p.op("dve", lambda e: e.tensor_tensor(out=PWR.ap, in0=PWR.ap, in1=MAG.ap, op=ALU.mult), reads=[PWR, MAG], writes=[PWR])
                p.op("dve", lambda e: e.tensor_tensor(out=PWI.ap, in0=PWI.ap, in1=MAG.ap, op=ALU.mult), reads=[PWI, MAG], writes=[PWI])

                def pw(k):
                    return k + 7
                p.op("dve", lambda e: e.tensor_copy(out=RHO.ap.rearrange("q d g -> q (d g)"), in_=MAG.ap[:, pw(8), :]), reads=[MAG], writes=[RHO])
                p.op("dve", lambda e: e.tensor_scalar(out=F8.ap, in0=TH.ap, scalar1=8.0 / (2 * math.pi), scalar2=None, op0=ALU.mult),
                     reads=[TH], writes=[F8])
                rnd_frac(None, F8, F8t)
                p.barrier()
                esA.close()
                esB = ExitStack()
                tB = Ctx(p, esB)
                A1R = PWR.ap[:, pw(1), :]
                A1I = PWI.ap[:, pw(1), :]
                ZR = tB.sb([128, 64], F32, "ZR")
                ZI = tB.sb([128, 64], F32, "ZI")
                DEN = tB.sb([128, 64], F32, "DEN")
                TMPa = tB.sb([128, 64], F32, "TMPa")
                TMPb = tB.sb([128, 64], F32, "TMPb")
                lr2 = LR.ap.rearrange("q d g -> q (d g)")
                li2 = LI.ap.rearrange("q d g -> q (d g)")
                p.op("dve", lambda e: e.tensor_tensor(out=DEN.ap, in0=lr2, in1=lr2, op=ALU.mult), reads=[LR], writes=[DEN])
                p.op("dve", lambda e: e.tensor_tensor(out=TMPa.ap, in0=li2, in1=li2, op=ALU.mult), reads=[LI], writes=[TMPa])
                p.op("dve", lambda e: e.tensor_tensor(out=DEN.ap, in0=DEN.ap, in1=TMPa.ap, op=ALU.add), reads=[DEN, TMPa], writes=[DEN])
                p.op("dve", lambda e: e.reciprocal(out=DEN.ap, in_=DEN.ap), reads=[DEN], writes=[DEN])
                p.op("dve", lambda e: e.tensor_scalar(out=TMPa.ap, in0=A1R, scalar1=-1.0, scalar2=None, op0=ALU.add), reads=[PWR], writes=[TMPa])
                p.op("dve", lambda e: e.tensor_tensor(out=ZR.ap, in0=TMPa.ap, in1=lr2, op=ALU.mult), reads=[TMPa, LR], writes=[ZR])
                p.op("dve", lambda e: e.tensor_tensor(out=TMPb.ap, in0=A1I, in1=li2, op=ALU.mult), reads=[PWI, LI], writes=[TMPb])
                p.op("dve", lambda e: e.tensor_tensor(out=ZR.ap, in0=ZR.ap, in1=TMPb.ap, op=ALU.add), reads=[ZR, TMPb], writes=[ZR])
                p.op("dve", lambda e: e.tensor_tensor(out=ZR.ap, in0=ZR.ap, in1=DEN.ap, op=ALU.mult), reads=[ZR, DEN], writes=[ZR])
                p.op("dve", lambda e: e.tensor_tensor(out=ZI.ap, in0=A1I, in1=lr2, op=ALU.mult), reads=[PWI, LR], writes=[ZI])
                p.op("dve", lambda e: e.tensor_tensor(out=TMPb.ap, in0=TMPa.ap, in1=li2, op=ALU.mult), reads=[TMPa, LI], writes=[TMPb])
                p.op("dve", lambda e: e.tensor_tensor(out=ZI.ap, in0=ZI.ap, in1=TMPb.ap, op=ALU.subtract), reads=[ZI, TMPb], writes=[ZI])
                p.op("dve", lambda e: e.tensor_tensor(out=ZI.ap, in0=ZI.ap, in1=DEN.ap, op=ALU.mult), reads=[ZI, DEN], writes=[ZI])
                BR = tB.sb([128, 32, 16], F32, "BR")
                BI = tB.sb([128, 32, 16], F32, "BI")
                CR = tB.sb([128, 32, 16], F32, "CR")
                CI = tB.sb([128, 32, 16], F32, "CI")
                for half in range(2):
                    hs_ = slice(half * 64, half * 64 + 64)
                    p.dma("sp", BR.ap[hs_], b_re[l].rearrange("g q c -> q g c"), writes=[BR])
                    p.dma("sp", BI.ap[hs_], b_im[l].rearrange("g q c -> q g c"), writes=[BI])
                csrc = tB.sb([128, 4, 128], F32, "csrc")
                for (cin, cout) in ((c_re, CR), (c_im, CI)):
                    cv_ = cin[l].rearrange("g c q -> (g c) q").rearrange("(t r) q -> r t q", r=128)
                    p.dma("sp", csrc.ap[:, :, 0:64], cv_, writes=[csrc])
                    p.dma("sp", csrc.ap[:, :, 64:128], cv_, writes=[csrc])
                    pt = ps()
                    for tt in range(4):
                        p.op("pe", lambda e, pt=pt, tt=tt: e.transpose(out=pt.ap[:, tt * 128:(tt + 1) * 128], in_=csrc.ap[:, tt, :], identity=ident.ap),
                             reads=[csrc, ident], writes=[pt])
                    p.op("dve", lambda e, pt=pt, cout=cout: e.tensor_copy(out=cout.ap.rearrange("q g c -> q (g c)"), in_=pt.ap), reads=[pt], writes=[cout])
                ZBR = tB.sb([128, 2, 32, 16], F32, "ZBR")
                ZBI = tB.sb([128, 2, 32, 16], F32, "ZBI")
                TZ = tB.sb([128, 2, 32, 16], F32, "TZ")
                zrb = ZR.ap.rearrange("q (d g) -> q d g", d=2).unsqueeze(3).broadcast_to([128, 2, 32, 16])
                zib = ZI.ap.rearrange("q (d g) -> q d g", d=2).unsqueeze(3).broadcast_to([128, 2, 32, 16])
                brb = BR.ap.unsqueeze(1).broadcast_to([128, 2, 32, 16])
                bib = BI.ap.unsqueeze(1).broadcast_to([128, 2, 32, 16])
                p.op("dve", lambda e: e.tensor_tensor(out=ZBR.ap, in0=zrb, in1=brb, op=ALU.mult), reads=[ZR, BR], writes=[ZBR])
                p.op("dve", lambda e: e.tensor_tensor(out=TZ.ap, in0=zib, in1=bib, op=ALU.mult), reads=[ZI, BI], writes=[TZ])
                p.op("dve", lambda e: e.tensor_tensor(out=ZBR.ap, in0=ZBR.ap, in1=TZ.ap, op=ALU.subtract), reads=[ZBR, TZ], writes=[ZBR])
                p.op("dve", lambda e: e.tensor_tensor(out=ZBI.ap, in0=zrb, in1=bib, op=ALU.mult), reads=[ZR, BI], writes=[ZBI])
                p.op("dve", lambda e: e.tensor_tensor(out=TZ.ap, in0=zib, in1=brb, op=ALU.mult), reads=[ZI, BR], writes=[TZ])
                p.op("dve", lambda e: e.tensor_tensor(out=ZBI.ap, in0=ZBI.ap, in1=TZ.ap, op=ALU.add), reads=[ZBI, TZ], writes=[ZBI])

                def mk_pair(name, top_a, sa_top, bot_a, sa_bot, top_b, sb_top, bot_b, sb_bot, shape):
                    Ma, Mb = PAIRS[name]
                    for (M_, top, s_top, bot, s_bot) in ((Ma, top_a, sa_top, bot_a, sa_bot), (Mb, top_b, sb_top, bot_b, sb_bot)):
                        p.op("act", lambda e, M_=M_, top=top, s_top=s_top: e.mul(out=M_.ap[0:64], in_=top.ap[0:64], mul=float(s_top)), reads=[top], writes=[M_])
                        p.op("act", lambda e, M_=M_, bot=bot, s_bot=s_bot: e.mul(out=M_.ap[64:128], in_=bot.ap[64:128], mul=float(s_bot)), reads=[bot], writes=[M_])
                    return Ma, Mb
                shB = [128, 2, 32, 16]
                shC = [128, 32, 16]
                B_ReIm = mk_pair("BReIm", ZBR, 1, ZBI, 1, ZBI, -1, ZBR, 1, shB)
                B_ImmRe = mk_pair("BImRe", ZBI, 1, ZBR, -1, ZBR, 1, ZBI, 1, shB)
                C_RemIm = mk_pair("CReIm", CR, 1, CI, -1, CI, -1, CR, -1, shC)
                C_mImmRe = mk_pair("CImRe", CI, -1, CR, -1, CR, -1, CI, 1, shC)

                p.barrier()
                esB.close()
                GT1 = t.sb([128, 16, 8, 16], F32, "GT1")
                GT2 = t.sb([128, 16, 8, 16], F32, "GT2")

                def factor(out_t, pair, d_, ks, is_b):
                    Ma, Mb = pair
                    k0, kstep = ks
                    for gh in (0, 16):
                        def pv(PW, gh=gh):
                            v = PW.ap[:, :, d_ * 32 + gh:d_ * 32 + gh + 16]
                            i0 = pw(k0)
                            if kstep == 1:
                                v = v[:, i0:i0 + 8, :]
                            else:
                                v = v[:, i0 - 7:i0 + 1, :][:, ::-1, :]
                            return v.rearrange("q k g -> q g k").unsqueeze(3).broadcast_to([128, 16, 8, 16])
                        ma = (Ma.ap[:, d_] if is_b else Ma.ap)[:, gh:gh + 16].unsqueeze(2).broadcast_to([128, 16, 8, 16])
                        mb = (Mb.ap[:, d_] if is_b else Mb.ap)[:, gh:gh + 16].unsqueeze(2).broadcast_to([128, 16, 8, 16])
                        p.op("dve", lambda e, pv=pv, ma=ma: e.tensor_tensor(out=GT1.ap, in0=pv(PWR), in1=ma, op=ALU.mult), reads=[PWR, Ma], writes=[GT1])
                        p.op("pool", lambda e, pv=pv, mb=mb: e.tensor_tensor(out=GT2.ap, in0=pv(PWI), in1=mb, op=ALU.mult), reads=[PWI, Mb], writes=[GT2])
                        p.op("dve", lambda e, gh=gh: e.tensor_tensor(out=out_t.ap[:, gh:gh + 16, :].rearrange("q g (t c) -> q g t c", c=16),
                                                                    in0=GT1.ap, in1=GT2.ap, op=ALU.add), reads=[GT1, GT2], writes=[out_t])

                factor(Qm[0][0], C_RemIm, 0, (1, 1), False)
                factor(Qm[0][1], C_mImmRe, 0, (1, 1), False)
                factor(Qm[1][0], C_RemIm, 1, (8, -1), False)
                factor(Qm[1][1], C_mImmRe, 1, (8, -1), False)
                esT = ExitStack()
                tT = Ctx(p, esT)
                Lf = tT.sb([128, 32, 128], BF16, "Lf")
                Rf = tT.sb([128, 32, 128], BF16, "Rf")
                Rb = tT.sb([128, 32, 128], BF16, "Rb")
                Lb = tT.sb([128, 32, 128], BF16, "Lb")
                factor(Lf, B_ReIm, 0, (0, -1), True)
                factor(Rf, C_RemIm, 0, (0, 1), False)
                factor(Rb, C_RemIm, 1, (0, -1), False)
                factor(Lb, B_ReIm, 1, (0, 1), True)
                MFB = tT.sb([128, 2, 128], F32, "MFB")
                dcol = tT.sb([128, 32], F32, "dcol")
                p.dma("sp", MFB.ap, c_mfb, writes=[MFB])
                for s_ in range(8):
                    p.dma("sp", dcol.ap[s_ * 16:(s_ + 1) * 16, :], ssm_d[l].rearrange("(g c) -> c g", c=16), writes=[dcol])
                T1 = tT.sb([128, 4, 128], F32, "T1t")
                T2 = tT.sb([128, 4, 128], F32, "T2t")
                for g4 in range(0, 32, 4):
                    pf = ps()
                    pb = ps()
                    for j in range(4):
                        gg = g4 + j
                        js = slice(j * 128, (j + 1) * 128)
                        p.op("pe", lambda e, pf=pf, js=js, gg=gg: e.matmul(pf.ap[:, js], lhsT=Lf.ap[:, gg, :], rhs=Rf.ap[:, gg, :], start=True, stop=True),
                             reads=[Lf, Rf], writes=[pf])
                        p.op("pe", lambda e, pb=pb, js=js, gg=gg: e.matmul(pb.ap[:, js], lhsT=Lb.ap[:, gg, :], rhs=Rb.ap[:, gg, :], start=True, stop=True),
                             reads=[Lb, Rb], writes=[pb])
                    p.op("dve", lambda e, pf=pf: e.tensor_tensor(out=T1.ap, in0=pf.ap.rearrange("q (g c) -> q g c", g=4),
                                                                 in1=MFB.ap[:, 0, :].unsqueeze(1).broadcast_to([128, 4, 128]), op=ALU.mult),
                         reads=[pf, MFB], writes=[T1])
                    p.op("dve", lambda e, pb=pb: e.tensor_tensor(out=T2.ap, in0=pb.ap.rearrange("q (g c) -> q g c", g=4),
                                                                 in1=MFB.ap[:, 1, :].unsqueeze(1).broadcast_to([128, 4, 128]), op=ALU.mult),
                         reads=[pb, MFB], writes=[T2])
                    p.op("pool", lambda e: e.tensor_tensor(out=T1.ap, in0=T1.ap, in1=T2.ap, op=ALU.add), reads=[T1, T2], writes=[T1])
                    for j in range(4):
                        gg = g4 + j
                        p.op("dve", lambda e, j=j, gg=gg: e.scalar_tensor_tensor(out=Tg.ap[:, gg, :], in0=ident.ap, scalar=dcol.ap[:, gg:gg + 1],
                                                                                 in1=T1.ap[:, j, :], op0=ALU.mult, op1=ALU.add),
                             reads=[ident, dcol, T1], writes=[Tg])
                p.barrier()
                esT.close()
                PTr = [t.sb([128, 32, 128], BF16, "PTr") for _ in range(2)]
                pi_ = 0
                for d_, v_, pair, ks in ((0, 0, B_ReIm, (7, -1)), (0, 1, B_ImmRe, (7, -1)), (1, 0, B_ReIm, (0, 1)), (1, 1, B_ImmRe, (0, 1))):
                    ptt_ = PTr[pi_ % 2]
                    pi_ += 1
                    factor(ptt_, pair, d_, ks, True)
                    for g4 in range(0, 32, 4):
                        pt = ps()
                        ptb = pt.ap.bitcast(BF16)
                        for j in range(4):
                            p.op("pe", lambda e, ptb=ptb, j=j, g4=g4, ptt_=ptt_: e.transpose(
                                out=ptb[:, j * 128:(j + 1) * 128], in_=ptt_.ap[:, g4 + j, :], identity=identb.ap),
                                reads=[ptt_, identb], writes=[pt])
                        p.op("act", lambda e, ptb=ptb, g4=g4, d_=d_, v_=v_: e.copy(
                            out=Pm[d_][v_].ap[:, g4:g4 + 4, :].rearrange("q g c -> q (g c)"), in_=ptb[:, 0:512]), reads=[pt], writes=[Pm[d_][v_]])
                p.barrier()

            Vp = c.sb([128, 32, NCP], BF16, "Vp")
            Vs = c.sb([128, 32, NCS], BF16, "Vs")
            ubv = UB.ap.rearrange("s c g n -> (s c) g n")
            p.dma("sp", Vp.ap, ubv[:, :, 0:NCP], reads=[UB], writes=[Vp])
            p.dma("sp", Vs.ap, ubv[:, :, NCP:NCP + NCS], reads=[UB], writes=[Vs])
            YSTp = c.sb([128, 32, NCP], F32, "YSTp")
            TAB = [c.sb([128, 2, 513], F32, "TAB") for _ in range(4)]
            TBt = [c.sb([128, 513], F32, "TBt") for _ in range(2)]
            TBu = [c.sb([128, 513], F32, "TBu") for _ in range(2)]
            M1 = [c.sb([128, 512], F32, "M1b") for _ in range(2)]
            M2 = [c.sb([128, 512], F32, "M2b") for _ in range(2)]
            Wx = [c.sb([128, 513], F32, "Wx") for _ in range(2)]
            Wxp = [c.sb([128, 4, 33], F32, "Wxp") for _ in range(2)]
            Ab = [c.sb([128, 512], BF16, "Ab") for _ in range(2)]
            Bb = [c.sb([128, 512], BF16, "Bb") for _ in range(2)]
            Abp = [c.sb([128, 4, 32], BF16, "Abp") for _ in range(2)]
            Bbp = [c.sb([128, 4, 32], BF16, "Bbp") for _ in range(2)]
            FINA = c.sb([128, 4, 2, 32], F32, "FINA")
            FINB = c.sb([128, 4, 2, 32], F32, "FINB")
            YSTs = [c.sb([128, NCS], F32, "YSTs") for _ in range(2)]
            yv = YSCR.ap.rearrange("s c g n -> (s c) g n")
            it = 0
            pstate["n"] = 4

            def lockstep_b(gens):
                alive = list(gens)
                while alive:
                    for gg_ in list(alive):
                        try:
                            next(gg_)
                        except StopIteration:
                            alive.remove(gg_)

            def tab_gen(g_, d_, r2):
                    tab, tbt, tbu, m1, m2, wx, wxp, ab, bb, abp, bbp = TAB[r2], TBt[d_], TBu[d_], M1[d_], M2[d_], Wx[d_], Wxp[d_], Ab[d_], Bb[d_], Abp[d_], Bbp[d_]
                    p.op("dve", lambda e, tbt=tbt, d_=d_, g_=g_: e.tensor_scalar(out=tbt.ap, in0=KV513.ap, scalar1=F8.ap[:, d_, g_:g_ + 1], scalar2=None,
                                                                                 op0=ALU.mult), reads=[KV513, F8], writes=[tbt])
                    p.op("dve", lambda e, tbt=tbt, tbu=tbu: e.tensor_scalar(out=tbu.ap, in0=tbt.ap, scalar1=MAGIC, scalar2=None, op0=ALU.add), reads=[tbt], writes=[tbu])
                    p.op("dve", lambda e, tbu=tbu: e.tensor_scalar(out=tbu.ap, in0=tbu.ap, scalar1=-MAGIC, scalar2=None, op0=ALU.add), reads=[tbu], writes=[tbu])
                    p.op("pool", lambda e, tbt=tbt, tbu=tbu: e.tensor_tensor(out=tbt.ap, in0=tbt.ap, in1=tbu.ap, op=ALU.subtract), reads=[tbt, tbu], writes=[tbt])
                    yield
                    p.op("act", lambda e, tab=tab, tbt=tbt: e.activation(out=tab.ap[:, 0, :], in_=tbt.ap, func=AF.Sin, scale=TWO_PI_LO), reads=[tbt], writes=[tab])
                    p.op("dve", lambda e, tbt=tbt, tbu=tbu: e.scalar_tensor_tensor(out=tbu.ap, in0=tbt.ap, scalar=-1.0, in1=tbt.ap, op0=ALU.mult, op1=ALU.max),
                         reads=[tbt], writes=[tbu])
                    p.op("act", lambda e, tab=tab, tbu=tbu: e.activation(out=tab.ap[:, 1, :], in_=tbu.ap, func=AF.Sin, scale=-TWO_PI_LO, bias=hpi.ap[:, 0:1]),
                         reads=[tbu, hpi], writes=[tab])

            def main_gen(g_, d_, r2, pyp, pys):
                    tab, tbt, tbu, m1, m2, wx, wxp, ab, bb, abp, bbp = TAB[r2], TBt[d_], TBu[d_], M1[d_], M2[d_], Wx[d_], Wxp[d_], Ab[d_], Bb[d_], Abp[d_], Bbp[d_]
                    last = (d_ == 1)
                    rho_b = RHO.ap[:, d_, g_:g_ + 1]
                    for grp in range(2):
                        V, ncols, py = (Vp, NCP, pyp) if grp == 0 else (Vs, NCS, pys)
                        px = ps()
                        pxs = ps()
                        p.op("pe", lambda e, px=px, V=V, ncols=ncols, d_=d_, g_=g_: e.matmul(px.ap[:, :ncols], lhsT=Pm[d_][0].ap[:, g_, :], rhs=V.ap[:, g_, :],
                                                                                             start=True, stop=True), reads=[Pm[d_][0], V], writes=[px])
                        p.op("pe", lambda e, pxs=pxs, V=V, ncols=ncols, d_=d_, g_=g_: e.matmul(pxs.ap[:, :ncols], lhsT=Pm[d_][1].ap[:, g_, :], rhs=V.ap[:, g_, :],
                                                                                               start=True, stop=True), reads=[Pm[d_][1], V], writes=[pxs])
                        yield
                        if grp == 0:
                            nseq, nc1 = 4, 32
                        else:
                            nseq, nc1 = 1, NCS
                        xv = px.ap[:, :ncols].rearrange("q (s n) -> q s n", s=nseq)
                        xsv = pxs.ap[:, :ncols].rearrange("q (s n) -> q s n", s=nseq)
                        if d_ == 1:
                            xv = xv[:, :, ::-1]
                            xsv = xsv[:, :, ::-1]
                        cosm = tab.ap[:, 1, 1:nc1 + 1].unsqueeze(1).broadcast_to([128, nseq, nc1])
                        sinm = tab.ap[:, 0, 1:nc1 + 1].unsqueeze(1).broadcast_to([128, nseq, nc1])
                        m1v = m1.ap[:, :ncols].rearrange("q (s n) -> q s n", s=nseq)
                        m2v = m2.ap[:, :ncols].rearrange("q (s n) -> q s n", s=nseq)
                        p.op("dve", lambda e, m1v=m1v, xv=xv, cosm=cosm: e.tensor_tensor(out=m1v, in0=xv, in1=cosm, op=ALU.mult), reads=[px, tab], writes=[m1])
                        p.op("dve", lambda e, m2v=m2v, xsv=xsv, sinm=sinm: e.tensor_tensor(out=m2v, in0=xsv, in1=sinm, op=ALU.mult), reads=[pxs, tab], writes=[m2])
                        p.op("pool", lambda e, m1v=m1v, m2v=m2v: e.tensor_tensor(out=m1v, in0=m1v, in1=m2v, op=ALU.add), reads=[m1, m2], writes=[m1])
                        yield
                        if grp == 0:
                            p.op("pool", lambda e, wxp=wxp: e.memset(wxp.ap[:, :, 0:1], 0.0), writes=[wxp])
                            for s_ in range(4):
                                p.op("dve", lambda e, wxp=wxp, m1v=m1v, s_=s_: e.tensor_tensor_scan(
                                    out=wxp.ap[:, s_, 1:33], data0=rho_b.broadcast_to([128, 32]), data1=m1v[:, s_, :], initial=0.0,
                                    op0=ALU.mult, op1=ALU.add), reads=[m1, RHO], writes=[wxp])
                            yield
                            wsrc = wxp.ap[:, :, 0:32]
                            if d_ == 1:
                                wsrc = wsrc[:, :, ::-1]
                                cd = tab.ap[:, 1, 0:32][:, ::-1].unsqueeze(1).broadcast_to([128, 4, 32])
                                sd = tab.ap[:, 0, 0:32][:, ::-1].unsqueeze(1).broadcast_to([128, 4, 32])
                            else:
                                cd = tab.ap[:, 1, 0:32].unsqueeze(1).broadcast_to([128, 4, 32])
                                sd = tab.ap[:, 0, 0:32].unsqueeze(1).broadcast_to([128, 4, 32])
                            p.op("dve", lambda e, abp=abp, wsrc=wsrc, cd=cd: e.tensor_tensor(out=abp.ap, in0=wsrc, in1=cd, op=ALU.mult), reads=[wxp, tab], writes=[abp])
                            p.op("pool", lambda e, bbp=bbp, wsrc=wsrc, sd=sd: e.tensor_tensor(out=bbp.ap, in0=wsrc, in1=sd, op=ALU.mult), reads=[wxp, tab], writes=[bbp])
                            p.op("dve", lambda e, wxp=wxp, tab=tab, d_=d_, g_=g_: e.tensor_scalar(
                                out=FINA.ap[:, :, d_, g_], in0=wxp.ap[:, :, 32], scalar1=tab.ap[:, 1, 32:33], scalar2=None, op0=ALU.mult),
                                reads=[wxp, tab], writes=[FINA])
                            p.op("dve", lambda e, wxp=wxp, tab=tab, d_=d_, g_=g_: e.tensor_scalar(
                                out=FINB.ap[:, :, d_, g_], in0=wxp.ap[:, :, 32], scalar1=tab.ap[:, 0, 32:33], scalar2=None, op0=ALU.mult),
                                reads=[wxp, tab], writes=[FINB])
                            arhs, brhs = abp.ap.rearrange("q s n -> q (s n)"), bbp.ap.rearrange("q s n -> q (s n)")
                        else:
                            p.op("pool", lambda e, wx=wx, d_=d_, g_=g_: e.tensor_copy(out=wx.ap[:, 0:1], in_=X0.ap[:, d_, g_:g_ + 1]), reads=[X0], writes=[wx])
                            p.op("dve", lambda e, wx=wx, m1=m1, d_=d_, g_=g_: e.tensor_tensor_scan(
                                out=wx.ap[:, 1:NCS + 1], data0=rho_b.broadcast_to([128, NCS]), data1=m1.ap[:, :NCS], initial=X0.ap[:, d_, g_:g_ + 1],
                                op0=ALU.mult, op1=ALU.add), reads=[m1, RHO, X0], writes=[wx])
                            yield
                            wsrc = wx.ap[:, 0:NCS]
                            cd = tab.ap[:, 1, 0:NCS]
                            sd = tab.ap[:, 0, 0:NCS]
                            if d_ == 1:
                                wsrc, cd, sd = wsrc[:, ::-1], cd[:, ::-1], sd[:, ::-1]
                            p.op("dve", lambda e, ab=ab, wsrc=wsrc, cd=cd: e.tensor_tensor(out=ab.ap, in0=wsrc, in1=cd, op=ALU.mult), reads=[wx, tab], writes=[ab])
                            p.op("pool", lambda e, bb=bb, wsrc=wsrc, sd=sd: e.tensor_tensor(out=bb.ap, in0=wsrc, in1=sd, op=ALU.mult), reads=[wx, tab], writes=[bb])
                            arhs, brhs = ab.ap, bb.ap
                        yield
                        p.op("pe", lambda e, py=py, ncols=ncols, arhs=arhs, d_=d_, g_=g_: e.matmul(py.ap[:, :ncols], lhsT=Qm[d_][0].ap[:, g_, :], rhs=arhs,
                                                                                                   start=False, stop=False),
                             reads=[Qm[d_][0], abp if grp == 0 else ab], writes=[py])
                        p.op("pe", lambda e, py=py, ncols=ncols, brhs=brhs, d_=d_, g_=g_, last=last: e.matmul(py.ap[:, :ncols], lhsT=Qm[d_][1].ap[:, g_, :], rhs=brhs,
                                                                                                              start=False, stop=last),
                             reads=[Qm[d_][1], bbp if grp == 0 else bb], writes=[py])

            lockstep_b([tab_gen(0, d_, d_) for d_ in range(2)])
            for g_ in range(32):
                pyp = psum[4 + 2 * (g_ % 2)]
                pys = psum[5 + 2 * (g_ % 2)]
                p.op("pe", lambda e, pyp=pyp, g_=g_: e.matmul(pyp.ap[:, :NCP], lhsT=Tg.ap[:, g_, :], rhs=Vp.ap[:, g_, :], start=True, stop=False),
                     reads=[Tg, Vp], writes=[pyp])
                p.op("pe", lambda e, pys=pys, g_=g_: e.matmul(pys.ap[:, :NCS], lhsT=Tg.ap[:, g_, :], rhs=Vs.ap[:, g_, :], start=True, stop=False),
                     reads=[Tg, Vs], writes=[pys])
                gens = [main_gen(g_, d_, (g_ % 2) * 2 + d_, pyp, pys) for d_ in range(2)]
                if g_ + 1 < 32:
                    gens += [tab_gen(g_ + 1, d_, ((g_ + 1) % 2) * 2 + d_) for d_ in range(2)]
                lockstep_b(gens)
                p.op("act", lambda e, pyp=pyp, g_=g_: e.copy(out=YSTp.ap[:, g_, :], in_=pyp.ap[:, :NCP]), reads=[pyp], writes=[YSTp])
                yst = YSTs[g_ % 2]
                p.op("act", lambda e, pys=pys, yst=yst: e.copy(out=yst.ap, in_=pys.ap[:, :NCS]), reads=[pys], writes=[yst])
                p.dma("pool", yv[:, g_, NCP:NCP + NCS], yst.ap, reads=[yst], writes=[YSCR])
            p.dma("pool", yv[:, :, 0:NCP], YSTp.ap, reads=[YSTp], writes=[YSCR])
            pstate["n"] = 8
            SWP = c.sb([128, 128], F32, "SWP")
            p.dma("sp", SWP.ap, c_swp, writes=[SWP])
            pfin = ps()
            fa = FINA.ap.rearrange("q s d g -> q (s d g)")
            fb = FINB.ap.rearrange("q s d g -> q (s d g)")
            p.op("pe", lambda e: e.matmul(pfin.ap[:, :256], lhsT=ident.ap, rhs=fa, start=True, stop=False), reads=[ident, FINA], writes=[pfin])
            p.op("pe", lambda e: e.matmul(pfin.ap[:, :256], lhsT=SWP.ap, rhs=fb, start=False, stop=True), reads=[SWP, FINB], writes=[pfin])
            XF = c.sb([128, 256], F32, "XF")
            p.op("dve", lambda e: e.tensor_copy(out=XF.ap, in_=pfin.ap[:, :256]), reads=[pfin], writes=[XF])
            pft = ps()
            for hh in range(2):
                p.op("pe", lambda e, hh=hh: e.transpose(out=pft.ap[:, hh * 128:(hh + 1) * 128], in_=XF.ap[:, hh * 128:(hh + 1) * 128], identity=ident.ap),
                     reads=[XF, ident], writes=[pft])
            XFT = c.sb([128, 2, 128], F32, "XFT")
            p.op("dve", lambda e: e.tensor_copy(out=XFT.ap.rearrange("q a b -> q (a b)"), in_=pft.ap[:, :256]), reads=[pft], writes=[XFT])
            for s_ in range(4):
                hh, s2 = s_ // 2, s_ % 2
                p.dma("pool", ns_re.ap[s_, l].rearrange("d g q -> (d g) q"), XFT.ap[s2 * 64:(s2 + 1) * 64, hh, 0:64], reads=[XFT], writes=[ns_re])
                p.dma("pool", ns_im.ap[s_, l].rearrange("d g q -> (d g) q"), XFT.ap[s2 * 64:(s2 + 1) * 64, hh, 64:128], reads=[XFT], writes=[ns_im])
            p.barrier()

        with ExitStack() as es:
            c = Ctx(p, es)
            WGL = c.sb([128, 4, 512], BF16, "WGL")
            bgl = c.sb([128, 4], F32, "bgl")
            p.dma("sp", WGL.ap, WB_glu.ap[l].rearrange("(k q) n -> q k n", q=128), reads=[WB_glu], writes=[WGL])
            p.dma("sp", bgl.ap, b_glu[l].rearrange("(j q) -> q j", q=128), writes=[bgl])
            YL = [c.sb([128, 4, 8, 64], F32, "YL") for _ in range(2)]
            YF = [c.sb([128, 4, 512], F32, "YF") for _ in range(2)]
            YQ = [c.sb([128, 4, 512], F32, "YQ") for _ in range(2)]
            YG = [c.sb([128, 4, 512], F32, "YG") for _ in range(2)]
            YGb = [c.sb([128, 4, 512], BF16, "YGb") for _ in range(2)]
            SG = [c.sb([128, 512], F32, "SGb") for _ in range(2)]
            YO = [c.sb([128, 4, 512], BF16, "YO") for _ in range(2)]
            ybv = YB.ap.rearrange("(k q) t -> q k t", q=128)
            K0 = 2.0 * math.sqrt(2.0 / math.pi)
            for bi, t0 in enumerate(range(0, TTOT, 512)):
                yl, yf, yq, yg, ygb, yo = YL[bi % 2], YF[bi % 2], YQ[bi % 2], YG[bi % 2], YGb[bi % 2], YO[bi % 2]
                n0 = t0 // 8
                for gt_ in range(4):
                    for gl in range(8):
                        p.dma("sp", yl.ap[gl * 16:(gl + 1) * 16, gt_, :, :],
                              YSCR.ap[:, :, gt_ * 8 + gl, n0:n0 + 64].rearrange("s c n -> c s n"), reads=[YSCR], writes=[yl])
                for gt_ in range(4):
                    evn = (gt_ % 2 == 0)
                    if evn:
                        p.op("act", lambda e, gt_=gt_: e.copy(out=yf.ap[:, gt_, :].rearrange("q (n s) -> q s n", s=8), in_=yl.ap[:, gt_, :, :]),
                             reads=[yl], writes=[yf])
                    else:
                        p.op("pool", lambda e, gt_=gt_: e.tensor_copy(out=yf.ap[:, gt_, :].rearrange("q (n s) -> q s n", s=8), in_=yl.ap[:, gt_, :, :]),
                             reads=[yl], writes=[yf])
                p.op("act", lambda e: e.activation(out=yq.ap, in_=yf.ap, func=AF.Square), reads=[yf], writes=[yq])
                p.op("dve", lambda e: e.tensor_scalar(out=yq.ap, in0=yq.ap, scalar1=0.044715, scalar2=1.0, op0=ALU.mult, op1=ALU.add), reads=[yq], writes=[yq])
                p.op("pool", lambda e: e.tensor_tensor(out=yq.ap, in0=yq.ap, in1=yf.ap, op=ALU.mult), reads=[yq, yf], writes=[yq])
                p.op("act", lambda e: e.activation(out=yq.ap, in_=yq.ap, func=AF.Sigmoid, scale=K0), reads=[yq], writes=[yq])
                p.op("dve", lambda e: e.tensor_tensor(out=yg.ap, in0=yq.ap, in1=yf.ap, op=ALU.mult), reads=[yq, yf], writes=[yg])
                p.op("pool", lambda e: e.tensor_copy(out=ygb.ap, in_=yg.ap), reads=[yg], writes=[ygb])
                for j in range(4):
                    pt = ps()
                    for k in range(4):
                        p.op("pe", lambda e, pt=pt, j=j, k=k: e.matmul(pt.ap, lhsT=WGL.ap[:, k, j * 128:(j + 1) * 128], rhs=ygb.ap[:, k, :],
                                                                       start=(k == 0), stop=(k == 3)), reads=[WGL, ygb], writes=[pt])
                    sg = SG[j % 2]
                    p.op("act", lambda e, pt=pt, sg=sg, j=j: e.activation(out=sg.ap, in_=pt.ap, func=AF.Sigmoid, bias=bgl.ap[:, j:j + 1]),
                         reads=[pt, bgl], writes=[sg])
                    p.op("dve", lambda e, sg=sg, j=j: e.tensor_tensor(out=yo.ap[:, j, :], in0=sg.ap, in1=yg.ap[:, j, :], op=ALU.mult), reads=[sg, yg], writes=[yo])
                p.dma("pool", ybv[:, :, t0:t0 + 512], yo.ap, reads=[yo], writes=[YB])
        p.barrier()

    def stage_mix_stub(l):
        p.dma("sp", OA.ap, ZA.ap, reads=[ZA], writes=[OA])
        p.dma("sp", YB.ap, GC.ap, reads=[GC], writes=[YB])
        p.dma("sp", OC.ap, QC.ap, reads=[QC], writes=[OC])
        p.barrier()

    nlayers = dbg_opts.get("nlayers", DEPTH)
    xi = 0
    for l in range(nlayers):
        es_l = ExitStack()
        lc = Ctx(p, es_l)
        mod = stage_mod(l, lc)
        stage_A(l, xi, mod)
        if dbg_opts.get("stub_mix"):
            stage_mix_stub(l)
        else:
            mixs = dbg_opts.get("mix", "ABC")
            if "C" in mixs:
                mix_C(l)
            if "A" in mixs:
                mix_A(l)
            if "B" in mixs:
                mix_B(l)
        if dbg_opts.get("stop_after_mix"):
            es_l.close()
            break
        stage_C1(l, xi, 1 - xi, mod)
        stage_C2(l, 1 - xi, xi, mod)
        es_l.close()
        p.barrier()
    stage_output(xi)
    p.finish()
    es0.close()
    print("n instructions", p.ninst)
    return nc


def host_consts():
    ident = np.eye(128, dtype=np.float32)
    t = np.arange(LS)
    rows = (t // 64).astype(np.float32)
    cols = (t % 64).astype(np.float32)
    freqs = (10000.0 ** (-np.arange(32, dtype=np.float32) / 32)).astype(np.float32)
    ang = np.concatenate([rows[None, :] * freqs[:, None], cols[None, :] * freqs[:, None]], axis=0)
    ang = ang.astype(np.float32)
    cos = np.cos(ang).astype(np.float32)
    sin = np.sin(ang).astype(np.float32)
    rope = np.stack([np.concatenate([cos, cos], 0), np.concatenate([sin, sin], 0)], 0).astype(np.float32)
    gam = 1.0 - 2.0 ** (-5.0 - np.arange(4))
    gamb = gam[::-1]
    C = 128
    ii = np.arange(C)
    dist = ii[None, :] - ii[:, None]
    rdt = np.zeros((C, 4, C), np.float64)
    for h in range(4):
        rdt[:, h, :] = np.where(dist > 0, gam[h] ** np.maximum(dist, 0), 0.0) + np.where(dist < 0, gamb[h] ** np.maximum(-dist, 0), 0.0) \
            + np.where(dist == 0, 2.0, 0.0)
    rqd = np.zeros((128, 2, 4, C), np.float64)
    rkd = np.zeros((C, 2, 4, 128), np.float64)
    for h in range(4):
        rqd[:, 0, h, :] = (gam[h] ** (ii + 1.0))[None, :]
        rqd[:, 1, h, :] = (gamb[h] ** (C - ii * 1.0))[None, :]
        rkd[:, 0, h, :] = (gam[h] ** (C - 1.0 - ii))[:, None]
        rkd[:, 1, h, :] = (gamb[h] ** (ii * 1.0))[:, None]
    ut = np.zeros((C, 2, C), np.float32)
    ut[:, 0, :] = (ii[:, None] <= ii[None, :])
    ut[:, 1, :] = (ii[:, None] >= ii[None, :])
    BIG = -1.0e6
    nm = np.zeros((C, 4, C), np.float32)
    nm[:, 0, :] = np.where(ii[:, None] <= ii[None, :], 0.0, BIG)
    nm[:, 1, :] = np.where(ii[:, None] >= ii[None, :], 0.0, BIG)
    nm[:, 2, :] = np.where(ii[:, None] > ii[None, :], 0.0, BIG)
    nm[:, 3, :] = np.where(ii[:, None] < ii[None, :], 0.0, BIG)
    sel = np.zeros((4, 4, 128), np.float32)
    for h in range(4):
        sel[h, h, :] = 1.0
    nlm = np.zeros((C, 2, 7, C), np.float32)
    for lev in range(7):
        b = 1 << lev
        mrow = ii[:, None]
        ccol = ii[None, :]
        mk = ((mrow // (2 * b)) == (ccol // (2 * b))) & ((mrow % (2 * b)) < b) & ((ccol % (2 * b)) >= b)
        nlm[:, 0, lev, :] = mk
        nlm[:, 1, lev, :] = mk.T
    kv513 = np.broadcast_to(np.arange(513, dtype=np.float32)[None, :], (128, 513)).copy()
    ks = np.arange(-7, 9, dtype=np.float64)
    kv16 = np.stack([ks / (2 * np.pi), ks], 0).astype(np.float32)
    sI = (ii // 16)[:, None]
    tI = (ii // 16)[None, :]
    mfb = np.stack([(tI >= sI), (sI >= tI)], 1).astype(np.float32)
    swp = np.zeros((128, 128), np.float32)
    for q in range(64):
        swp[64 + q, q] = -1.0
        swp[q, 64 + q] = 1.0
    extra = {"c_ut": ut, "c_nm": nm, "c_sel": sel, "c_nlm": nlm, "c_kv513": kv513, "c_kv16": kv16, "c_mfb": mfb, "c_swp": swp}
    return {**extra, "c_ident": ident, "c_rope": rope, "c_rdt": rdt.astype(np.float32), "c_rqd": rqd.astype(np.float32),
            "c_rkd": rkd.reshape(C, 2, 512).astype(np.float32)}


_CACHE = {}


def run_raw(inp, debug=(), dbg_opts=None):
    inp = {k: np.asarray(v) for k, v in inp.items()}
    key = tuple(sorted(debug))
    if key not in _CACHE:
        _CACHE[key] = build(debug, dbg_opts)
    nc = _CACHE[key]
    consts = host_consts()
    in_maps = []
    for core in range(8):
        b = core // 4
        m = {}
        m["xin"] = np.ascontiguousarray(np.concatenate(
            [inp["x_prompt"][core * 4:(core + 1) * 4].reshape(NSEQ_P * LP, D), inp["x_sample"][b]], axis=0))
        m["cond"] = np.ascontiguousarray(np.stack([inp["c_ctx"], inp["c"][b]], 0))
        m["st_delta"] = np.ascontiguousarray(inp["state_delta"][b])
        m["st_re"] = np.ascontiguousarray(inp["state_ssm_re"][b])
        m["st_im"] = np.ascontiguousarray(inp["state_ssm_im"][b])
        m["st_ret"] = np.ascontiguousarray(inp["state_ret"][b])
        for k in ("final_norm", "norm1", "norm2", "w_mod", "b_mod", "w_in", "w_conv_qkv", "norm_a", "w_br_a",
                  "ssm_lam_re", "ssm_lam_im", "ssm_log_dt", "ssm_b_re", "ssm_b_im", "ssm_c_re", "ssm_c_im",
                  "ssm_d", "w_glu", "b_glu", "w_br_b", "w_br_c", "w_o", "w_up", "w_conv_ffn", "b_conv_ffn",
                  "w_down"):
            m[k] = inp[k]
        m["a_log"] = np.ascontiguousarray(inp["a_log"].reshape(DEPTH, 8))
        m["dt_bias"] = np.ascontiguousarray(inp["dt_bias"].reshape(DEPTH, 8))
        m.update(consts)
        in_maps.append(m)
    res = run_bass_kernel_spmd(nc, in_maps, core_ids=list(range(8)))
    return res.results


def kernel(**inp):
    R = run_raw(inp)
    y_prompt = np.concatenate([R[c]["yout"][:NSEQ_P * LP].reshape(NSEQ_P, LP, D) for c in range(8)], 0)
    y_sample = np.stack([R[0]["yout"][NSEQ_P * LP:], R[4]["yout"][NSEQ_P * LP:]], 0)
    nsd = np.concatenate([R[c]["ns_delta"] for c in range(8)], 0)
    nsre = np.concatenate([R[c]["ns_re"] for c in range(8)], 0)
    nsim = np.concatenate([R[c]["ns_im"] for c in range(8)], 0)
    nsr = np.concatenate([R[c]["ns_ret"] for c in range(8)], 0)
    return (y_prompt.astype(np.float32), y_sample.astype(np.float32), nsd.astype(np.float32),
            nsre.astype(np.float32), nsim.astype(np.float32), nsr.astype(np.float32))
```

```python
import math
from contextlib import ExitStack

import numpy as np
import concourse.bass as bass
import concourse.mybir as mybir
from concourse.bass_utils import run_bass_kernel_spmd

F32 = mybir.dt.float32
BF16 = mybir.dt.bfloat16
AF = mybir.ActivationFunctionType
ALU = mybir.AluOpType

D = 1024
DEPTH = 2
NIN = 7696
DFF = 2816
EPS = 1e-6
NSEQ_P = 4
LP = 256
LS = 4096
TTOT = NSEQ_P * LP + LS
SEQS = [(i * LP, LP, 0) for i in range(NSEQ_P)] + [(NSEQ_P * LP, LS, 1)]
O_QKV, O_AL, O_BE, O_Z, O_U, O_QC, O_KC, O_VC, O_GC, O_GT = 0, 1536, 1544, 1552, 2064, 2576, 3088, 3600, 4112, 4624


class T:
    __slots__ = ("ap", "key")

    def __init__(self, ap, key):
        self.ap = ap
        self.key = key

    def __getitem__(self, idx):
        return self.ap[idx]


class Prog:
    NSLOT = 12

    def __init__(self, nc):
        self.nc = nc
        self.E = {"pe": nc.tensor, "dve": nc.vector, "act": nc.scalar, "pool": nc.gpsimd, "sp": nc.sync}
        self.sem = {e: nc.alloc_semaphore("sem_" + e) for e in ("pe", "dve", "act", "pool")}
        self.cnt = {e: 0 for e in self.sem}
        self.semid = {}
        self.waited = {}
        self.W = {}
        self.R = {}
        self.QS = ("sp", "pool", "actq")
        self.dsem = {q: [nc.alloc_semaphore("d%s%d" % (q, i)) for i in range(self.NSLOT)] for q in self.QS}
        self.dval = {q: [0] * self.NSLOT for q in self.QS}
        self.dnext = {q: 0 for q in self.QS}
        self.E["actq"] = nc.scalar
        self.ninst = 0
        self._uid = 0

    def uid(self, s):
        self._uid += 1
        return "%s_%d" % (s, self._uid)

    def _wait(self, eng, name, sem, val):
        if eng == "pe" and name == "pe":
            return
        k = (eng, name)
        if self.waited.get(k, 0) >= val:
            return
        self.E[eng].wait_ge(sem, val)
        self.waited[k] = val

    def _deps(self, eng, reads, writes):
        need = {}
        for b in reads:
            for n, (s, v) in self.W.get(b.key, {}).items():
                if need.get(n, (None, 0))[1] < v:
                    need[n] = (s, v)
        for b in writes:
            for dct in (self.W.get(b.key, {}), self.R.get(b.key, {})):
                for n, (s, v) in dct.items():
                    if need.get(n, (None, 0))[1] < v:
                        need[n] = (s, v)
        for n, (s, v) in need.items():
            self._wait(eng, n, s, v)

    def _mark(self, name, sem, val, reads, writes):
        for b in writes:
            self.W.setdefault(b.key, {})[name] = (sem, val)
        for b in reads:
            self.R.setdefault(b.key, {})[name] = (sem, val)

    def op(self, eng, fn, reads=(), writes=()):
        self._deps(eng, reads, writes)
        inst = fn(self.E[eng])
        self.cnt[eng] += 1
        inst.then_inc(self.sem[eng], 1)
        self._mark(eng, self.sem[eng], self.cnt[eng], reads, writes)
        self.ninst += 1

    def dma(self, q, out, in_, reads=(), writes=(), **kw):
        weng = "act" if q == "actq" else q
        self._deps(weng, reads, writes)
        i = self.dnext[q]
        self.dnext[q] = (i + 1) % self.NSLOT
        sem = self.dsem[q][i]
        name = "d%s%d" % (q, i)
        if self.dval[q][i] > 0:
            self._wait(weng, name, sem, self.dval[q][i])
        inst = self.E[q].dma_start(out=out, in_=in_, **kw)
        self.dval[q][i] += 16
        inst.then_inc(sem, 16)
        self._mark(name, sem, self.dval[q][i], reads, writes)
        self.ninst += 1

    def barrier(self):
        for eng in ("pe", "dve", "act", "pool", "sp"):
            for e2 in ("pe", "dve", "act", "pool"):
                if e2 != eng and self.cnt[e2] > 0:
                    self._wait(eng, e2, self.sem[e2], self.cnt[e2])
            for q in self.QS:
                for i in range(self.NSLOT):
                    if self.dval[q][i] > 0:
                        self._wait(eng, "d%s%d" % (q, i), self.dsem[q][i], self.dval[q][i])
        self.W.clear()
        self.R.clear()

    def finish(self):
        for q in self.QS:
            for i in range(self.NSLOT):
                if self.dval[q][i] > 0:
                    self.E["sp"].wait_ge(self.dsem[q][i], self.dval[q][i])


class Ctx:
    def __init__(self, p, es):
        self.p = p
        self.es = es
        self.nc = p.nc

    def sb(self, shape, dt, name):
        nm = self.p.uid(name)
        t = self.es.enter_context(self.nc.sbuf_tensor(nm, list(shape), dt))
        return T(t.ap() if hasattr(t, "ap") and callable(getattr(t, "ap")) else t, nm)


def segs(n, maxw=512):
    k = (n + maxw - 1) // maxw
    base = n // k
    rem = n % k
    out = []
    c = 0
    for i in range(k):
        w = base + (1 if i < rem else 0)
        out.append((c, w))
        c += w
    return out


def build(debug=(), dbg_opts=None):
    nc = bass.Bass("TRN2", target_bir_lowering=False)
    p = Prog(nc)
    dbg = set(debug)
    dbg_opts = dbg_opts or {}
    _ncd = nc.allow_non_contiguous_dma(reason="small strided parameter / layout DMAs")
    _ncd.__enter__()

    def din(name, shape, dt=F32):
        return nc.dram_tensor(name, list(shape), dt, kind="ExternalInput").ap()

    def dout(name, shape, dt=F32):
        return nc.dram_tensor(name, list(shape), dt, kind="ExternalOutput").ap()

    def dscr(name, shape, dt):
        kind = "ExternalOutput" if name in dbg else "Internal"
        return T(nc.dram_tensor(name, list(shape), dt, kind=kind).ap(), "dram_" + name)

    xin = T(din("xin", [TTOT, D]), "in_x")
    cond = din("cond", [2, D])
    st_delta = din("st_delta", [DEPTH, 2, 4, 128, 128])
    st_re = din("st_re", [DEPTH, 2, 32, 64])
    st_im = din("st_im", [DEPTH, 2, 32, 64])
    st_ret = din("st_ret", [DEPTH, 2, 4, 128, 128])
    final_norm = din("final_norm", [D])
    norm1 = din("norm1", [DEPTH, D])
    norm2 = din("norm2", [DEPTH, D])
    w_mod = din("w_mod", [DEPTH, D, 6 * D])
    b_mod = din("b_mod", [DEPTH, 6 * D])
    w_in = din("w_in", [DEPTH, D, NIN])
    w_conv_qkv = din("w_conv_qkv", [DEPTH, 3, 1536])
    a_log = din("a_log", [DEPTH, 8])
    dt_bias = din("dt_bias", [DEPTH, 8])
    norm_a = din("norm_a", [DEPTH, 128])
    w_br_a = din("w_br_a", [DEPTH, 512, D])
    lam_re = din("ssm_lam_re", [DEPTH, 2, 32, 64])
    lam_im = din("ssm_lam_im", [DEPTH, 2, 32, 64])
    log_dt = din("ssm_log_dt", [DEPTH, 2, 32])
    b_re = din("ssm_b_re", [DEPTH, 32, 64, 16])
    b_im = din("ssm_b_im", [DEPTH, 32, 64, 16])
    c_re = din("ssm_c_re", [DEPTH, 32, 16, 64])
    c_im = din("ssm_c_im", [DEPTH, 32, 16, 64])
    ssm_d = din("ssm_d", [DEPTH, 512])
    w_glu = din("w_glu", [DEPTH, 512, 512])
    b_glu = din("b_glu", [DEPTH, 512])
    w_br_b = din("w_br_b", [DEPTH, 512, D])
    w_br_c = din("w_br_c", [DEPTH, 512, D])
    w_o = din("w_o", [DEPTH, D, D])
    w_up = din("w_up", [DEPTH, D, 2 * DFF])
    w_conv_ffn = din("w_conv_ffn", [DEPTH, 3, 2 * DFF])
    b_conv_ffn = din("b_conv_ffn", [DEPTH, 2 * DFF])
    w_down = din("w_down", [DEPTH, DFF, D])
    c_ident = din("c_ident", [128, 128])
    c_rope = din("c_rope", [2, 128, LS])
    c_rdt = din("c_rdt", [128, 4, 128])
    c_rqd = din("c_rqd", [128, 2, 4, 128])
    c_rkd = din("c_rkd", [128, 2, 512])
    c_ut = din("c_ut", [128, 2, 128])
    c_nm = din("c_nm", [128, 4, 128])
    c_sel = din("c_sel", [4, 4, 128])
    c_nlm = din("c_nlm", [128, 2, 7, 128])
    c_kv513 = din("c_kv513", [128, 513])
    c_kv16 = din("c_kv16", [2, 16])
    c_mfb = din("c_mfb", [128, 2, 128])
    c_swp = din("c_swp", [128, 128])

    yout = T(dout("yout", [TTOT, D]), "out_y")
    ns_delta = T(dout("ns_delta", [NSEQ_P, DEPTH, 2, 4, 128, 128]), "out_nsd")
    ns_re = T(dout("ns_re", [NSEQ_P, DEPTH, 2, 32, 64]), "out_nsre")
    ns_im = T(dout("ns_im", [NSEQ_P, DEPTH, 2, 32, 64]), "out_nsim")
    ns_ret = T(dout("ns_ret", [NSEQ_P, DEPTH, 2, 4, 128, 128]), "out_nsr")

    XS = [dscr("xs0", [D, TTOT], F32), dscr("xs1", [D, TTOT], F32)]
    WB_in = dscr("wb_in", [DEPTH, D, NIN], BF16)
    WB_rot = dscr("wb_rot", [DEPTH, D, 1024], BF16)
    WB_up = dscr("wb_up", [DEPTH, D, 2 * DFF], BF16)
    WB_down = dscr("wb_down", [DEPTH, DFF, D], BF16)
    WB_o = dscr("wb_o", [DEPTH, D, D], BF16)
    WB_bra = dscr("wb_bra", [DEPTH, 512, D], BF16)
    WB_brb = dscr("wb_brb", [DEPTH, 512, D], BF16)
    WB_brc = dscr("wb_brc", [DEPTH, 512, D], BF16)
    WB_glu = dscr("wb_glu", [DEPTH, 512, 512], BF16)
    QA = dscr("qa", [512, TTOT], BF16)
    KA = dscr("ka", [512, TTOT], BF16)
    KAT = dscr("kat", [TTOT, 512], BF16)
    VAT = dscr("vat", [TTOT, 512], BF16)
    GA = dscr("ga", [TTOT, 8], F32)
    BA = dscr("ba", [TTOT, 8], F32)
    ZA = dscr("za", [512, TTOT], BF16)
    UB = dscr("ub", [8, 16, 32, TTOT // 8], BF16)
    QC = dscr("qc", [512, TTOT], BF16)
    KC = dscr("kc", [512, TTOT], BF16)
    KCT = dscr("kct", [TTOT, 512], BF16)
    VCT = dscr("vct", [TTOT, 512], BF16)
    GC = dscr("gc", [512, TTOT], BF16)
    GT = dscr("gt", [3072, TTOT], BF16)
    OA = dscr("oa", [512, TTOT], BF16)
    YB = dscr("yb", [512, TTOT], BF16)
    OC = dscr("oc", [512, TTOT], BF16)

    es0 = ExitStack()
    g = Ctx(p, es0)
    ident = g.sb([128, 128], F32, "ident")
    identb = g.sb([128, 128], BF16, "identb")
    ones = g.sb([128, 128], F32, "ones")
    p.dma("sp", ident.ap, c_ident, writes=[ident])
    p.op("dve", lambda e: e.tensor_copy(out=identb.ap, in_=ident.ap), reads=[ident], writes=[identb])
    p.op("dve", lambda e: e.memset(ones.ap, 1.0), writes=[ones])
    onesb = g.sb([128, 128], BF16, "onesb")
    p.op("dve", lambda e: e.memset(onesb.ap, 1.0), writes=[onesb])

    psum = [T(nc.alloc_psum_tensor("ps%d" % i, [128, 512], F32).ap(), "ps%d" % i) for i in range(8)]
    pstate = {"i": 0, "n": 8}

    def ps():
        pstate["i"] = pstate["i"] % pstate["n"]
        t = psum[pstate["i"]]
        pstate["i"] = (pstate["i"] + 1) % pstate["n"]
        return t

    def stage_input():
        with ExitStack() as es:
            c = Ctx(p, es)
            xt = [c.sb([128, D], F32, "xt") for _ in range(3)]
            stg = [c.sb([128, 8, 512], F32, "xstg") for _ in range(2)]
            xs_v = XS[0].ap.rearrange("(k q) t -> q k t", q=128)
            for blk in range(TTOT // 512):
                sg = stg[blk % 2]
                for tt in range(4):
                    tok0 = blk * 512 + tt * 128
                    x = xt[(blk * 4 + tt) % 3]
                    p.dma("sp", x.ap, xin.ap[tok0:tok0 + 128, :], reads=[xin], writes=[x])
                    for k2 in range(2):
                        pt = ps()
                        for kk in range(4):
                            k = k2 * 4 + kk
                            p.op("pe", lambda e, pt=pt, kk=kk, k=k, x=x: e.transpose(
                                out=pt.ap[:, kk * 128:(kk + 1) * 128], in_=x.ap[:, k * 128:(k + 1) * 128],
                                identity=ident.ap), reads=[x, ident], writes=[pt])
                        eng = "act" if k2 == 0 else "dve"
                        if eng == "act":
                            p.op("act", lambda e, pt=pt, k2=k2, tt=tt, sg=sg: e.copy(
                                out=sg.ap[:, k2 * 4:(k2 + 1) * 4, tt * 128:(tt + 1) * 128],
                                in_=pt.ap.rearrange("q (k t) -> q k t", k=4)), reads=[pt], writes=[sg])
                        else:
                            p.op("dve", lambda e, pt=pt, k2=k2, tt=tt, sg=sg: e.tensor_copy(
                                out=sg.ap[:, k2 * 4:(k2 + 1) * 4, tt * 128:(tt + 1) * 128],
                                in_=pt.ap.rearrange("q (k t) -> q k t", k=4)), reads=[pt], writes=[sg])
                p.dma("pool", xs_v[:, :, blk * 512:(blk + 1) * 512], sg.ap, reads=[sg], writes=[XS[0]])
        p.barrier()

    def cast_gen(c, layers, CW=2048, nbuf=3, lq="sp"):
        src = [c.sb([128, CW], F32, "csrc") for _ in range(nbuf)]
        dst = [c.sb([128, CW], BF16, "cdst") for _ in range(nbuf)]
        return _cast_gen(src, dst, layers, CW, nbuf, lq)

    def _cast_gen(src, dst, layers, CW, nbuf, lq):
        st = {"i": 0}
        engs = ["act", "dve", "pool"]

        def cast2d(src_ap, dst_t, nrows, ncols):
            for r0 in range(0, nrows, 128):
                for c0 in range(0, ncols, CW):
                    w = min(CW, ncols - c0)
                    i = st["i"]
                    st["i"] += 1
                    s_, d = src[i % nbuf], dst[i % nbuf]
                    p.dma(lq, s_.ap[:, :w], src_ap[r0:r0 + 128, c0:c0 + w], writes=[s_])
                    eng = engs[i % 3]
                    if eng == "act":
                        p.op("act", lambda e, s_=s_, d=d, w=w: e.copy(out=d.ap[:, :w], in_=s_.ap[:, :w]), reads=[s_], writes=[d])
                    else:
                        p.op(eng, lambda e, s_=s_, d=d, w=w: e.tensor_copy(out=d.ap[:, :w], in_=s_.ap[:, :w]), reads=[s_], writes=[d])
                    p.dma("pool", dst_t.ap[r0:r0 + 128, c0:c0 + w], d.ap[:, :w], reads=[d], writes=[dst_t])
                    yield

        for l in layers:
            yield from cast2d(w_in[l], T(WB_in.ap[l], WB_in.key), D, NIN)
            for part, off in ((0, O_QC), (1, O_KC)):
                for r0 in range(0, D, 128):
                    i = st["i"]
                    st["i"] += 1
                    s_, d = src[i % nbuf], dst[i % nbuf]
                    p.dma(lq, s_.ap[:, :512], w_in[l][r0:r0 + 128, off:off + 512], writes=[s_])
                    sv = s_.ap[:, :512].rearrange("q (h two j) -> q h two j", h=4, two=2)
                    dv = d.ap[:, :512].rearrange("q (h two j) -> q h two j", h=4, two=2)
                    p.op("act", lambda e, sv=sv, dv=dv: e.mul(out=dv[:, :, 0, :], in_=sv[:, :, 1, :], mul=-1.0), reads=[s_], writes=[d])
                    p.op("dve", lambda e, sv=sv, dv=dv: e.tensor_copy(out=dv[:, :, 1, :], in_=sv[:, :, 0, :]), reads=[s_], writes=[d])
                    p.dma("pool", WB_rot.ap[l][r0:r0 + 128, part * 512:(part + 1) * 512], d.ap[:, :512], reads=[d], writes=[WB_rot])
                    yield
            yield from cast2d(w_up[l], T(WB_up.ap[l], WB_up.key), D, 2 * DFF)
            yield from cast2d(w_down[l], T(WB_down.ap[l], WB_down.key), DFF, D)
            yield from cast2d(w_o[l], T(WB_o.ap[l], WB_o.key), D, D)
            yield from cast2d(w_br_a[l], T(WB_bra.ap[l], WB_bra.key), 512, D)
            yield from cast2d(w_br_b[l], T(WB_brb.ap[l], WB_brb.key), 512, D)
            yield from cast2d(w_br_c[l], T(WB_brc.ap[l], WB_brc.key), 512, D)
            yield from cast2d(w_glu[l], T(WB_glu.ap[l], WB_glu.key), 512, 512)

    def stage_cast(layers):
        with ExitStack() as es:
            c = Ctx(p, es)
            for _ in cast_gen(c, layers):
                pass
        p.barrier()

    LATE_CAST = dbg_opts.get("nlayers", DEPTH) > 1 and not dbg_opts.get("no_late_cast")
    stage_input()
    stage_cast([0] if LATE_CAST else list(range(DEPTH)))

    xs_views = [x.ap.rearrange("(k q) t -> q k t", q=128) for x in XS]

    BLOCKS = [dict(cond=0, rope=False, stok=0, pieces=[(s_ * LP, LP, False, False) for s_ in range(NSEQ_P)])]
    for b_ in range(4):
        BLOCKS.append(dict(cond=1, rope=True, stok=b_ * 1024,
                           pieces=[(NSEQ_P * LP + b_ * 1024, 1024, b_ > 0, b_ < 3)]))
    for B in BLOCKS:
        c0 = 0
        B["c0"] = []
        for (tok0, n, lv, rv) in B["pieces"]:
            B["c0"].append(c0)
            c0 += n + 2
        B["ncol"] = c0
        B["tok_lo"] = B["pieces"][0][0]
    NCOLMAX = max(B["ncol"] for B in BLOCKS)
    if "only_blocks" in dbg_opts:
        BLOCKS = [BLOCKS[i] for i in dbg_opts["only_blocks"]]

    def stage_mod(l, c):
        modT = c.sb([128, 48, 2], F32, "modT")
        A1 = c.sb([128, 8, 2], F32, "A1")
        A2 = c.sb([128, 8, 2], F32, "A2")
        with ExitStack() as es:
            t = Ctx(p, es)
            condT = t.sb([128, 8, 2], F32, "condT")
            scond = t.sb([128, 8, 2], F32, "scond")
            bm = t.sb([128, 48], F32, "bm")
            n1 = t.sb([128, 8], F32, "n1")
            n2 = t.sb([128, 8], F32, "n2")
            slab = [t.sb([128, 8, 768], F32, "wmslab") for _ in range(2)]
            for ci_ in range(2):
                p.dma("sp", condT.ap[:, :, ci_], cond[ci_].rearrange("(k q) -> q k", q=128), writes=[condT])
            p.dma("sp", bm.ap, b_mod[l].rearrange("(j q) -> q j", q=128), writes=[bm])
            p.dma("sp", n1.ap, norm1[l].rearrange("(k q) -> q k", q=128), writes=[n1])
            p.dma("sp", n2.ap, norm2[l].rearrange("(k q) -> q k", q=128), writes=[n2])
            p.op("act", lambda e: e.activation(out=scond.ap, in_=condT.ap, func=AF.Silu), reads=[condT], writes=[scond])
            wm_v = w_mod[l].rearrange("(k q) n -> q k n", q=128)
            pt = ps()
            for s_ in range(8):
                sl = slab[s_ % 2]
                p.dma("sp", sl.ap, wm_v[:, :, s_ * 768:(s_ + 1) * 768], writes=[sl])
                for j in range(6):
                    jj = s_ * 6 + j
                    for k in range(8):
                        p.op("pe", lambda e, sl=sl, j=j, jj=jj, k=k: e.matmul(
                            pt.ap[:, jj * 2:jj * 2 + 2], lhsT=sl.ap[:, k, j * 128:(j + 1) * 128], rhs=scond.ap[:, k, :],
                            start=(k == 0), stop=(k == 7)), reads=[sl, scond], writes=[pt])
            p.op("dve", lambda e: e.tensor_tensor(
                out=modT.ap, in0=pt.ap[:, 0:96].rearrange("q (j c) -> q j c", c=2),
                in1=bm.ap.unsqueeze(2).broadcast_to([128, 48, 2]), op=ALU.add), reads=[pt, bm], writes=[modT])
            p.op("dve", lambda e: e.scalar_tensor_tensor(
                out=A1.ap, in0=modT.ap[:, 8:16, :], scalar=1.0, in1=n1.ap.unsqueeze(2).broadcast_to([128, 8, 2]),
                op0=ALU.add, op1=ALU.mult), reads=[modT, n1], writes=[A1])
            p.op("dve", lambda e: e.scalar_tensor_tensor(
                out=A2.ap, in0=modT.ap[:, 32:40, :], scalar=1.0, in1=n2.ap.unsqueeze(2).broadcast_to([128, 8, 2]),
                op0=ALU.add, op1=ALU.mult), reads=[modT, n2], writes=[A2])
            p.barrier()
        return dict(modT=modT, A1=A1, A2=A2)

    def norm_mod(c, X, H, ncol, Amul, Bkey, Bk0, ci, sq, tmpf, rstd):
        for (a, w) in segs(ncol):
            pt = ps()
            for k in range(8):
                sqt = sq[k % len(sq)]
                p.op("act", lambda e, sqt=sqt, k=k, a=a, w=w: e.activation(
                    out=sqt.ap[:, :w], in_=X.ap[:, k, a:a + w], func=AF.Square), reads=[X], writes=[sqt])
                p.op("pe", lambda e, sqt=sqt, k=k, w=w, pt=pt: e.matmul(
                    pt.ap[:, :w], lhsT=onesb.ap, rhs=sqt.ap[:, :w], start=(k == 0), stop=(k == 7)),
                    reads=[sqt, onesb], writes=[pt])
            p.op("act", lambda e, pt=pt, a=a, w=w: e.activation(
                out=rstd.ap[:, a:a + w], in_=pt.ap[:, :w], func=AF.Ln, scale=1.0 / D, bias=epsc.ap[:, 0:1]),
                reads=[pt, epsc], writes=[rstd])
        p.op("act", lambda e: e.activation(out=rstd.ap[:, :ncol], in_=rstd.ap[:, :ncol], func=AF.Exp, scale=-0.5),
             reads=[rstd], writes=[rstd])
        for k in range(8):
            tf = tmpf[k % len(tmpf)]
            p.op("dve", lambda e, tf=tf, k=k: e.tensor_tensor(
                out=tf.ap[:, :ncol], in0=X.ap[:, k, :ncol], in1=rstd.ap[:, :ncol], op=ALU.mult),
                reads=[X, rstd], writes=[tf])
            p.op("act", lambda e, tf=tf, k=k: e.activation(
                out=H.ap[:, k, :ncol], in_=tf.ap[:, :ncol], func=AF.Identity,
                scale=Amul.ap[:, k, ci:ci + 1], bias=Bkey.ap[:, Bk0 + k, ci:ci + 1]),
                reads=[tf, Amul, Bkey], writes=[H])

    epsc = g.sb([128, 2], F32, "epsc")
    p.op("dve", lambda e: e.memset(epsc.ap[:, 0:1], EPS), writes=[epsc])
    p.op("dve", lambda e: e.memset(epsc.ap[:, 1:2], 1.0), writes=[epsc])

    def stage_A(l, xi, mod):
        xs_v = xs_views[xi]
        xsrc = XS[xi]
        with ExitStack() as es:
            c = Ctx(p, es)
            X = c.sb([128, 8, NCOLMAX], F32, "X")
            H = c.sb([128, 8, NCOLMAX], BF16, "H")
            H2 = c.sb([128, 8, NCOLMAX], BF16, "H2")
            sq = [c.sb([128, 512], BF16, "sq") for _ in range(3)]
            SQB = [c.sb([128, NCOLMAX], BF16, "SQB") for _ in range(2)]
            tmpf = [c.sb([128, NCOLMAX], F32, "tmpf") for _ in range(2)]
            rstd = c.sb([128, NCOLMAX], F32, "rstd")
            PRE = [c.sb([128, NCOLMAX], F32, "PRE") for _ in range(2)]
            ACC = [c.sb([128, NCOLMAX], F32, "ACC") for _ in range(2)]
            SIL = [c.sb([128, NCOLMAX], F32, "SIL") for _ in range(2)]
            OUTB = [c.sb([128, NCOLMAX], BF16, "OUTB") for _ in range(8)]
            WG = [c.sb([128, 8, 512], BF16, "WG") for _ in range(3)]
            KT = c.sb([128, 8, 512], BF16, "KT")
            VT = c.sb([128, 8, 512], BF16, "VT")
            USTG = c.sb([128, 4, 8, 128], BF16, "USTG")
            GST = c.sb([128, 8, 8], F32, "GST")
            BST = c.sb([128, 8, 8], F32, "BST")
            T1 = c.sb([128, 8, 8], F32, "T1")
            cw = c.sb([128, 12, 3], F32, "cw")
            dtb = c.sb([128, 8], F32, "dtb")
            nega = c.sb([128, 8], F32, "nega")
            COS = c.sb([128, 1024], F32, "COS")
            SIN = c.sb([128, 1024], F32, "SIN")
            COSK = c.sb([128, 1024], F32, "COSK")
            SINK = c.sb([128, 1024], F32, "SINK")
            for j_ in range(3):
                p.dma("sp", cw.ap[:, :, j_], w_conv_qkv[l][j_].rearrange("(c q) -> q c", q=128), writes=[cw])
            for t_ in PRE + ACC + SIL + OUTB + tmpf + [rstd]:
                p.op("pool", lambda e, t_=t_: e.memset(t_.ap, 0.0), writes=[t_])
            p.dma("sp", dtb.ap, dt_bias[l].partition_broadcast(128), writes=[dtb])
            p.dma("sp", nega.ap, a_log[l].partition_broadcast(128), writes=[nega])
            p.op("act", lambda e: e.activation(out=nega.ap, in_=nega.ap, func=AF.Exp), reads=[nega], writes=[nega])
            p.op("dve", lambda e: e.tensor_scalar(out=nega.ap, in0=nega.ap, scalar1=-1.0, scalar2=None, op0=ALU.mult),
                 reads=[nega], writes=[nega])
            wv = WB_in.ap[l].rearrange("(k q) n -> q k n", q=128)
            wrv = WB_rot.ap[l].rearrange("(k q) n -> q k n", q=128)
            st = {"wg": 0, "ob": 0, "ev": 0}

            wspecs = []
            for B_ in BLOCKS:
                for grp_ in range(3):
                    wspecs.append((wv, O_QKV + grp_ * 512, 512, WB_in))
                wspecs.append((wv, O_AL, 16, WB_in))
                wspecs.append((wv, O_Z, 512, WB_in))
                wspecs.append((wv, O_U, 512, WB_in))
                for part_, off_ in enumerate((O_QC, O_KC)):
                    wspecs.append((wv, off_, 512, WB_in))
                    if B_["rope"]:
                        wspecs.append((wrv, part_ * 512, 512, WB_rot))
                wspecs.append((wv, O_VC, 512, WB_in))
                wspecs.append((wv, O_GC, 512, WB_in))
                for gg_ in range(6):
                    wspecs.append((wv, O_GT + gg_ * 512, 512, WB_in))
            wloaded = []
            st["wi"] = 0

            def issue_next():
                if st["wi"] < len(wspecs):
                    view, off, w, key = wspecs[st["wi"]]
                    st["wi"] += 1
                    t_ = WG[st["wg"] % 3]
                    st["wg"] += 1
                    p.dma("sp", t_.ap[:, :, :w], view[:, :, off:off + w], reads=[key], writes=[t_])
                    wloaded.append((t_, off, w))

            def load_w(view, off, w, key):
                if not wloaded:
                    issue_next()
                t_, off_, w_ = wloaded.pop(0)
                assert (off_, w_) == (off, w), (off_, w_, off, w)
                issue_next()
                return t_

            def outb():
                t_ = OUTB[st["ob"] % 8]
                st["ob"] += 1
                return t_

            def evac(fn_act, fn_dve, reads, writes):
                st["ev"] += 1
                if st["ev"] % 4 != 0:
                    p.op("act", fn_act, reads=reads, writes=writes)
                else:
                    p.op("dve", fn_dve, reads=reads, writes=writes)

            def proj(pt, wt, col0, m, a, w):
                for k in range(8):
                    p.op("pe", lambda e, k=k: e.matmul(pt.ap[:m, :w], lhsT=wt.ap[:, k, col0:col0 + m],
                                                       rhs=H.ap[:, k, a:a + w], start=(k == 0), stop=(k == 7)),
                         reads=[wt, H], writes=[pt])

            def block_prep(B_, Hb):
                for (tok0, n, lv, rv), c0 in zip(B_["pieces"], B_["c0"]):
                    a = c0 + 1 - int(lv)
                    b = c0 + 1 + n + int(rv)
                    p.dma("sp", X.ap[:, :, a:b], xs_v[:, :, tok0 - int(lv):tok0 + n + int(rv)], reads=[xsrc], writes=[X])
                    if not lv:
                        p.op("pool", lambda e, c0=c0: e.memset(X.ap[:, :, c0:c0 + 1], 0.0), writes=[X])
                    if not rv:
                        p.op("pool", lambda e, c0=c0, n=n: e.memset(X.ap[:, :, c0 + n + 1:c0 + n + 2], 0.0), writes=[X])
                norm_mod(c, X, Hb, B_["ncol"], mod["A1"], mod["modT"], 0, B_["cond"], sq, tmpf, rstd)
                for (tok0, n, lv, rv), c0 in zip(B_["pieces"], B_["c0"]):
                    if not lv:
                        p.op("pool", lambda e, c0=c0: e.memset(Hb.ap[:, :, c0:c0 + 1], 0.0), writes=[Hb])
                    if not rv:
                        p.op("pool", lambda e, c0=c0, n=n: e.memset(Hb.ap[:, :, c0 + n + 1:c0 + n + 2], 0.0), writes=[Hb])

            Hs = [H, H2]
            block_prep(BLOCKS[0], Hs[0])
            for bi_, B in enumerate(BLOCKS):
                H = Hs[bi_ % 2]
                ci = B["cond"]
                ncol = B["ncol"]
                pieces = B["pieces"]
                C0 = B["c0"]
                tok_lo = B["tok_lo"]
                if B["rope"]:
                    p.dma("sp", COS.ap, c_rope[0][:, B["stok"]:B["stok"] + 1024], writes=[COS])
                    p.dma("sp", SIN.ap, c_rope[1][:, B["stok"]:B["stok"] + 1024], writes=[SIN])
                    p.op("act", lambda e: e.mul(out=COSK.ap, in_=COS.ap, mul=128.0 ** -0.5), reads=[COS], writes=[COSK])
                    p.op("act", lambda e: e.mul(out=SINK.ap, in_=SIN.ap, mul=128.0 ** -0.5), reads=[SIN], writes=[SINK])

                tiles = []
                for (tok0, n, lv, rv), c0 in zip(pieces, C0):
                    for t_ in range(n // 128):
                        tiles.append(c0 + 1 + 128 * t_)

                def center_segs():
                    out = []
                    for (tok0, n, lv, rv), c0 in zip(pieces, C0):
                        for (a, w) in segs(n):
                            out.append((c0 + 1 + a, w, tok0 + a))
                    return out

                def store_fm(dst, row0, ob):
                    for (tok0, n, lv, rv), c0 in zip(pieces, C0):
                        p.dma("pool", dst.ap[row0:row0 + 128, tok0:tok0 + n], ob.ap[:, c0 + 1:c0 + 1 + n],
                              reads=[ob], writes=[dst])

                def transposes_to(ob, stg, h):
                    for t4 in range(0, len(tiles), 4):
                        pt = ps()
                        ptb = pt.ap.bitcast(BF16)
                        for j in range(4):
                            cc = tiles[t4 + j]
                            p.op("pe", lambda e, j=j, cc=cc, ptb=ptb: e.transpose(
                                out=ptb[:, j * 128:(j + 1) * 128], in_=ob.ap[:, cc:cc + 128], identity=identb.ap),
                                reads=[ob, identb], writes=[pt])
                        evac(lambda e, ptb=ptb, t4=t4: e.copy(out=stg.ap[:, t4:t4 + 4, h * 128:(h + 1) * 128],
                                                              in_=ptb[:, 0:512].rearrange("q (t f) -> q t f", t=4)),
                             lambda e, ptb=ptb, t4=t4: e.tensor_copy(out=stg.ap[:, t4:t4 + 4, h * 128:(h + 1) * 128],
                                                                     in_=ptb[:, 0:512].rearrange("q (t f) -> q t f", t=4)),
                             [pt], [stg])

                def store_tm(dst, stg, width=512):
                    ti = 0
                    for (tok0, n, lv, rv), c0 in zip(pieces, C0):
                        nt = n // 128
                        p.dma("pool", dst.ap[tok0:tok0 + n, :].rearrange("(t q) f -> q t f", q=128),
                              stg.ap[:, ti:ti + nt, :], reads=[stg], writes=[dst])
                        ti += nt

                def head_gen(grp, h, wt):
                    cidx = grp * 4 + h
                    pre = PRE[h % 2]
                    acc = ACC[h % 2]
                    sil = SIL[h % 2]
                    for (tok0, n, lv, rv), c0 in zip(pieces, C0):
                        for (a, w) in segs(n + 2):
                            pt = ps()
                            proj(pt, wt, h * 128, 128, c0 + a, w)
                            evac(lambda e, pt=pt, a=a, w=w, c0=c0: e.copy(out=pre.ap[:, c0 + a:c0 + a + w], in_=pt.ap[:, :w]),
                                 lambda e, pt=pt, a=a, w=w, c0=c0: e.tensor_copy(out=pre.ap[:, c0 + a:c0 + a + w], in_=pt.ap[:, :w]),
                                 [pt], [pre])
                        p.op("dve", lambda e, c0=c0, n=n: e.tensor_scalar(
                            out=acc.ap[:, c0 + 1:c0 + 1 + n], in0=pre.ap[:, c0 + 1:c0 + 1 + n],
                            scalar1=cw.ap[:, cidx, 1:2], scalar2=None, op0=ALU.mult), reads=[pre, cw], writes=[acc])
                        p.op("dve", lambda e, c0=c0, n=n: e.scalar_tensor_tensor(
                            out=acc.ap[:, c0 + 1:c0 + 1 + n], in0=pre.ap[:, c0:c0 + n], scalar=cw.ap[:, cidx, 0:1],
                            in1=acc.ap[:, c0 + 1:c0 + 1 + n], op0=ALU.mult, op1=ALU.add), reads=[pre, cw, acc], writes=[acc])
                        p.op("dve", lambda e, c0=c0, n=n: e.scalar_tensor_tensor(
                            out=acc.ap[:, c0 + 1:c0 + 1 + n], in0=pre.ap[:, c0 + 2:c0 + 2 + n], scalar=cw.ap[:, cidx, 2:3],
                            in1=acc.ap[:, c0 + 1:c0 + 1 + n], op0=ALU.mult, op1=ALU.add), reads=[pre, cw, acc], writes=[acc])
                    yield
                    ob = outb()
                    if grp == 2:
                        p.op("act", lambda e: e.activation(out=ob.ap[:, :ncol], in_=acc.ap[:, :ncol], func=AF.Silu),
                             reads=[acc], writes=[ob])
                        transposes_to(ob, VT, h)
                        return
                    p.op("act", lambda e: e.activation(out=sil.ap[:, :ncol], in_=acc.ap[:, :ncol], func=AF.Silu),
                         reads=[acc], writes=[sil])
                    sqb = SQB[h % 2]
                    p.op("act", lambda e, sqb=sqb: e.activation(out=sqb.ap[:, :ncol], in_=sil.ap[:, :ncol], func=AF.Square),
                         reads=[sil], writes=[sqb])
                    for (cc, w, tk) in center_segs():
                        pt = ps()
                        p.op("pe", lambda e, pt=pt, cc=cc, w=w, sqb=sqb: e.matmul(pt.ap[:, :w], lhsT=onesb.ap, rhs=sqb.ap[:, cc:cc + w],
                                                                                    start=True, stop=True), reads=[onesb, sqb], writes=[pt])
                        p.op("act", lambda e, pt=pt, cc=cc, w=w: e.activation(
                            out=acc.ap[:, cc:cc + w], in_=pt.ap[:, :w], func=AF.Ln, bias=epsc.ap[:, 0:1]),
                            reads=[pt, epsc], writes=[acc])
                    p.op("act", lambda e: e.activation(out=acc.ap[:, :ncol], in_=acc.ap[:, :ncol], func=AF.Exp, scale=-0.5),
                         reads=[acc], writes=[acc])
                    qs = (128.0 ** -0.5) if grp == 0 else 1.0
                    p.op("dve", lambda e, qs=qs: e.scalar_tensor_tensor(
                        out=ob.ap[:, :ncol], in0=sil.ap[:, :ncol], scalar=qs, in1=acc.ap[:, :ncol],
                        op0=ALU.mult, op1=ALU.mult), reads=[sil, acc], writes=[ob])
                    store_fm(QA if grp == 0 else KA, h * 128, ob)
                    if grp == 1:
                        transposes_to(ob, KT, h)

                def group_end(grp):
                    if grp == 0 and bi_ + 1 < len(BLOCKS):
                        block_prep(BLOCKS[bi_ + 1], Hs[(bi_ + 1) % 2])
                    if grp == 1:
                        store_tm(KAT, KT)
                    if grp == 2:
                        store_tm(VAT, VT)

                prev_g = None
                for grp in range(3):
                    wt = load_w(wv, O_QKV + grp * 512, 512, WB_in)
                    for h in range(4):
                        g_ = head_gen(grp, h, wt)
                        next(g_)
                        if prev_g is not None:
                            for _ in prev_g[0]:
                                pass
                            if prev_g[2] == 3:
                                group_end(prev_g[1])
                        prev_g = (g_, grp, h)
                for _ in prev_g[0]:
                    pass
                group_end(prev_g[1])

                wt = load_w(wv, O_AL, 16, WB_in)
                pt = ps()
                for ti, cc in enumerate(tiles):
                    for k in range(8):
                        p.op("pe", lambda e, k=k, ti=ti, cc=cc: e.matmul(
                            pt.ap[:, ti * 16:(ti + 1) * 16], lhsT=H.ap[:, k, cc:cc + 128], rhs=wt.ap[:, k, 0:16],
                            start=(k == 0), stop=(k == 7)), reads=[wt, H], writes=[pt])
                ptv = pt.ap[:, 0:128].rearrange("q (t j) -> q t j", j=16)
                p.op("dve", lambda e: e.tensor_tensor(out=T1.ap, in0=ptv[:, :, 0:8], in1=dtb.ap.unsqueeze(1).broadcast_to([128, 8, 8]),
                                                      op=ALU.add), reads=[pt, dtb], writes=[T1])
                p.op("act", lambda e: e.activation(out=T1.ap, in_=T1.ap, func=AF.Exp), reads=[T1], writes=[T1])
                p.op("act", lambda e: e.activation(out=T1.ap, in_=T1.ap, func=AF.Ln, bias=epsc.ap[:, 1:2]), reads=[T1, epsc], writes=[T1])
                p.op("dve", lambda e: e.tensor_tensor(out=GST.ap, in0=T1.ap, in1=nega.ap.unsqueeze(1).broadcast_to([128, 8, 8]),
                                                      op=ALU.mult), reads=[T1, nega], writes=[GST])
                p.op("act", lambda e: e.activation(out=BST.ap, in_=ptv[:, :, 8:16], func=AF.Sigmoid), reads=[pt], writes=[BST])
                ti = 0
                for (tok0, n, lv, rv), c0 in zip(pieces, C0):
                    nt = n // 128
                    p.dma("pool", GA.ap[tok0:tok0 + n, :].rearrange("(t q) j -> q t j", q=128), GST.ap[:, ti:ti + nt, :],
                          reads=[GST], writes=[GA])
                    p.dma("pool", BA.ap[tok0:tok0 + n, :].rearrange("(t q) j -> q t j", q=128), BST.ap[:, ti:ti + nt, :],
                          reads=[BST], writes=[BA])
                    ti += nt

                def simple_group(off, dst, row0, func):
                    wt = load_w(wv, off, 512, WB_in)
                    for h in range(4):
                        ob = outb()
                        for (cc, w, tk) in center_segs():
                            pt = ps()
                            proj(pt, wt, h * 128, 128, cc, w)
                            p.op("act", lambda e, pt=pt, cc=cc, w=w: e.activation(out=ob.ap[:, cc:cc + w], in_=pt.ap[:, :w], func=func),
                                 reads=[pt], writes=[ob])
                        store_fm(dst, row0 + h * 128, ob)

                simple_group(O_Z, ZA, 0, AF.Silu)

                wt = load_w(wv, O_U, 512, WB_in)
                for j in range(4):
                    for (cc, w, tk) in center_segs():
                        pt = ps()
                        proj(pt, wt, j * 128, 128, cc, w)
                        nb = (tk - tok_lo) // 8
                        evac(lambda e, pt=pt, w=w, nb=nb, j=j: e.copy(
                            out=USTG.ap[:, j, :, nb:nb + w // 8], in_=pt.ap[:, :w].rearrange("q (n s) -> q s n", s=8)),
                            lambda e, pt=pt, w=w, nb=nb, j=j: e.tensor_copy(
                            out=USTG.ap[:, j, :, nb:nb + w // 8], in_=pt.ap[:, :w].rearrange("q (n s) -> q s n", s=8)),
                            [pt], [USTG])
                n0 = tok_lo // 8
                for gt_ in range(4):
                    for gl in range(8):
                        p.dma("pool", UB.ap[:, :, gt_ * 8 + gl, n0:n0 + 128].rearrange("s c n -> c s n"),
                              USTG.ap[gl * 16:(gl + 1) * 16, gt_, :, :], reads=[USTG], writes=[UB])

                for part, (off, dst) in enumerate(((O_QC, QC), (O_KC, KC))):
                    wt = load_w(wv, off, 512, WB_in)
                    wr = load_w(wrv, part * 512, 512, WB_rot) if B["rope"] else None
                    ct, sn = (COS, SIN) if part == 0 else (COSK, SINK)
                    for h in range(4):
                        ob = outb()
                        for (cc, w, tk) in center_segs():
                            pt = ps()
                            proj(pt, wt, h * 128, 128, cc, w)
                            if B["rope"]:
                                pt2 = ps()
                                proj(pt2, wr, h * 128, 128, cc, w)
                                tc0 = tk - tok_lo
                                t1 = tmpf[0]
                                t2 = tmpf[1]
                                p.op("dve", lambda e, pt=pt, w=w, tc0=tc0: e.tensor_tensor(
                                    out=t1.ap[:, :w], in0=pt.ap[:, :w], in1=ct.ap[:, tc0:tc0 + w], op=ALU.mult),
                                    reads=[pt, ct], writes=[t1])
                                p.op("dve", lambda e, pt2=pt2, w=w, tc0=tc0: e.tensor_tensor(
                                    out=t2.ap[:, :w], in0=pt2.ap[:, :w], in1=sn.ap[:, tc0:tc0 + w], op=ALU.mult),
                                    reads=[pt2, sn], writes=[t2])
                                p.op("pool", lambda e, cc=cc, w=w: e.tensor_tensor(
                                    out=ob.ap[:, cc:cc + w], in0=t1.ap[:, :w], in1=t2.ap[:, :w], op=ALU.add),
                                    reads=[t1, t2], writes=[ob])
                            else:
                                sc = 1.0 if part == 0 else 128.0 ** -0.5
                                evac(lambda e, pt=pt, cc=cc, w=w, sc=sc: e.mul(out=ob.ap[:, cc:cc + w], in_=pt.ap[:, :w], mul=sc),
                                     lambda e, pt=pt, cc=cc, w=w, sc=sc: e.tensor_scalar(
                                         out=ob.ap[:, cc:cc + w], in0=pt.ap[:, :w], scalar1=sc, scalar2=None, op0=ALU.mult),
                                     [pt], [ob])
                        store_fm(dst, h * 128, ob)
                        if part == 1:
                            transposes_to(ob, KT, h)
                    if part == 1:
                        store_tm(KCT, KT)
                wt = load_w(wv, O_VC, 512, WB_in)
                for ti, cc in enumerate(tiles):
                    pt = ps()
                    for k in range(8):
                        p.op("pe", lambda e, k=k, cc=cc, pt=pt: e.matmul(
                            pt.ap[:, :512], lhsT=H.ap[:, k, cc:cc + 128], rhs=wt.ap[:, k, 0:512],
                            start=(k == 0), stop=(k == 7)), reads=[wt, H], writes=[pt])
                    evac(lambda e, pt=pt, ti=ti: e.copy(out=VT.ap[:, ti, :], in_=pt.ap[:, :512]),
                         lambda e, pt=pt, ti=ti: e.tensor_copy(out=VT.ap[:, ti, :], in_=pt.ap[:, :512]), [pt], [VT])
                store_tm(VCT, VT)
                simple_group(O_GC, GC, 0, AF.Silu)
                for gg in range(6):
                    simple_group(O_GT + gg * 512, GT, gg * 512, AF.Sigmoid)
        p.barrier()

    def stage_C1(l, xi, xo, mod):
        with ExitStack() as es:
            c = Ctx(p, es)
            X = c.sb([128, 8, 1024], F32, "X1")
            MRG = c.sb([128, 8, 1024], BF16, "MRG")
            BRS = [[c.sb([128, 4, 1024], BF16, "BR%d_%d" % (i, s_)) for i in range(3)] for s_ in range(2)]
            GTt = [c.sb([128, 1024], BF16, "GTt") for _ in range(9)]
            gq = []
            gstate = {"n": 0, "c": 0}
            WBR = [c.sb([128, 4, 1024], BF16, "WBR%d" % i) for i in range(3)]
            WO = c.sb([128, 8, 1024], BF16, "WO")
            tm = [c.sb([128, 512], F32, "tm") for _ in range(4)]
            for i, wsrc in enumerate((WB_bra, WB_brb, WB_brc)):
                p.dma("sp", WBR[i].ap, wsrc.ap[l].rearrange("(k q) n -> q k n", q=128), reads=[wsrc], writes=[WBR[i]])
            p.dma("sp", WO.ap, WB_o.ap[l].rearrange("(k q) n -> q k n", q=128), reads=[WB_o], writes=[WO])
            gcount = 0
            def load_br(B_, set_):
                t0_ = B_["tok_lo"]
                for i, src in enumerate((OA, YB, OC)):
                    p.dma("sp", BRS[set_][i].ap, src.ap.rearrange("(k q) t -> q k t", q=128)[:, :, t0_:t0_ + 1024],
                          reads=[src], writes=[BRS[set_][i]])

            load_br(BLOCKS[0], 0)
            for bi_, B in enumerate(BLOCKS):
                ci = B["cond"]
                t0 = B["tok_lo"]
                BR = BRS[bi_ % 2]
                p.dma("sp", X.ap, xs_views[xi][:, :, t0:t0 + 1024], reads=[XS[xi]], writes=[X])
                if bi_ + 1 < len(BLOCKS):
                    load_br(BLOCKS[bi_ + 1], (bi_ + 1) % 2)
                for j in range(8):
                    while len(gq) < 2 and gstate["n"] < 8 * len(BLOCKS):
                        bi_, j_ = gstate["n"] // 8, gstate["n"] % 8
                        gstate["n"] += 1
                        t0_ = BLOCKS[bi_]["tok_lo"]
                        gts_ = []
                        for x_ in range(3):
                            gt_ = GTt[gstate["c"] % 9]
                            gstate["c"] += 1
                            p.dma("sp", gt_.ap, GT.ap[x_ * 1024 + j_ * 128:x_ * 1024 + (j_ + 1) * 128, t0_:t0_ + 1024],
                                  reads=[GT], writes=[gt_])
                            gts_.append(gt_)
                        gq.append(gts_)
                    gts = gq.pop(0)
                    for sg in range(2):
                        a = sg * 512
                        pts = []
                        for x_ in range(3):
                            pt = ps()
                            for k in range(4):
                                p.op("pe", lambda e, pt=pt, x_=x_, k=k, a=a: e.matmul(
                                    pt.ap[:, :512], lhsT=WBR[x_].ap[:, k, j * 128:(j + 1) * 128], rhs=BR[x_].ap[:, k, a:a + 512],
                                    start=(k == 0), stop=(k == 3)), reads=[WBR[x_], BR[x_]], writes=[pt])
                            pts.append(pt)
                        ta, tb, tcc = tm[(2 * sg) % 4], tm[(2 * sg + 1) % 4], tm[(2 * sg + 2) % 4]
                        p.op("dve", lambda e, a=a, ta=ta, pt=pts[0], g_=gts[0]: e.tensor_tensor(
                            out=ta.ap, in0=pt.ap[:, :512], in1=g_.ap[:, a:a + 512], op=ALU.mult), reads=[pts[0], gts[0]], writes=[ta])
                        p.op("dve", lambda e, a=a, tb=tb, pt=pts[1], g_=gts[1]: e.tensor_tensor(
                            out=tb.ap, in0=pt.ap[:, :512], in1=g_.ap[:, a:a + 512], op=ALU.mult), reads=[pts[1], gts[1]], writes=[tb])
                        p.op("dve", lambda e, ta=ta, tb=tb: e.tensor_tensor(out=ta.ap, in0=ta.ap, in1=tb.ap, op=ALU.add),
                             reads=[ta, tb], writes=[ta])
                        p.op("dve", lambda e, a=a, tb=tb, pt=pts[2], g_=gts[2]: e.tensor_tensor(
                            out=tb.ap, in0=pt.ap[:, :512], in1=g_.ap[:, a:a + 512], op=ALU.mult), reads=[pts[2], gts[2]], writes=[tb])
                        p.op("pool", lambda e, a=a, ta=ta, tb=tb: e.tensor_tensor(
                            out=MRG.ap[:, j, a:a + 512], in0=ta.ap, in1=tb.ap, op=ALU.add), reads=[ta, tb], writes=[MRG])
                for j in range(8):
                    for sg in range(2):
                        a = sg * 512
                        pt = ps()
                        for k in range(8):
                            p.op("pe", lambda e, pt=pt, k=k, a=a: e.matmul(
                                pt.ap[:, :512], lhsT=WO.ap[:, k, j * 128:(j + 1) * 128], rhs=MRG.ap[:, k, a:a + 512],
                                start=(k == 0), stop=(k == 7)), reads=[WO, MRG], writes=[pt])
                        p.op("dve", lambda e, pt=pt, a=a: e.scalar_tensor_tensor(
                            out=X.ap[:, j, a:a + 512], in0=pt.ap[:, :512], scalar=mod["modT"].ap[:, 16 + j, ci:ci + 1],
                            in1=X.ap[:, j, a:a + 512], op0=ALU.mult, op1=ALU.add), reads=[pt, X, mod["modT"]], writes=[X])
                p.dma("pool", xs_views[xo][:, :, t0:t0 + 1024], X.ap, reads=[X], writes=[XS[xo]])
        p.barrier()

    def stage_C2(l, xi, xo, mod):
        xs_v = xs_views[xi]
        with ExitStack() as es:
            c = Ctx(p, es)
            X = c.sb([128, 8, NCOLMAX], F32, "X2")
            H = c.sb([128, 8, NCOLMAX], BF16, "H2")
            sq = [c.sb([128, 512], BF16, "sq2") for _ in range(3)]
            tmpf = [c.sb([128, NCOLMAX], F32, "tmpf2") for _ in range(2)]
            rstd = c.sb([128, NCOLMAX], F32, "rstd2")
            PREgs = [c.sb([128, NCOLMAX], F32, "PREg") for _ in range(2)]
            PREvs = [c.sb([128, NCOLMAX], F32, "PREv") for _ in range(2)]
            ACgs = [c.sb([128, NCOLMAX], F32, "ACg") for _ in range(2)]
            ACvs = [c.sb([128, NCOLMAX], F32, "ACv") for _ in range(2)]
            ACTV = c.sb([128, 22, 1024], BF16, "ACTV")
            WU = [c.sb([128, 8, 256], BF16, "WU") for _ in range(4)]
            WD = [c.sb([128, 22, 128], BF16, "WD") for _ in range(3)]
            wu_q = []
            wd_q = []
            cwf = c.sb([128, 44, 3], F32, "cwf")
            bcf = c.sb([128, 44], F32, "bcf")
            for t_ in (PREgs[0], PREgs[1], PREvs[0], PREvs[1], ACgs[0], ACgs[1], ACvs[0], ACvs[1], rstd, tmpf[0], tmpf[1]):
                p.op("pool", lambda e, t_=t_: e.memset(t_.ap, 0.0), writes=[t_])
            for j_ in range(3):
                p.dma("sp", cwf.ap[:, :, j_], w_conv_ffn[l][j_].rearrange("(c q) -> q c", q=128), writes=[cwf])
            p.dma("sp", bcf.ap, b_conv_ffn[l].rearrange("(c q) -> q c", q=128), writes=[bcf])
            wuv = WB_up.ap[l].rearrange("(k q) n -> q k n", q=128)
            wdv = WB_down.ap[l].rearrange("(f q) n -> q f n", q=128)
            cnt = {"wu": 0, "wd": 0, "ev": 0, "wuj": 0, "wdj": 0}

            def evac(fn_act, fn_dve, reads, writes):
                p.op("act", fn_act, reads=reads, writes=writes)

            for B in BLOCKS:
                ci = B["cond"]
                ncol = B["ncol"]
                pieces = B["pieces"]
                C0 = B["c0"]
                t0 = B["tok_lo"]
                for (tok0, n, lv, rv), c0 in zip(pieces, C0):
                    a = c0 + 1 - int(lv)
                    b = c0 + 1 + n + int(rv)
                    p.dma("sp", X.ap[:, :, a:b], xs_v[:, :, tok0 - int(lv):tok0 + n + int(rv)], reads=[XS[xi]], writes=[X])
                    if not lv:
                        p.op("pool", lambda e, c0=c0: e.memset(X.ap[:, :, c0:c0 + 1], 0.0), writes=[X])
                    if not rv:
                        p.op("pool", lambda e, c0=c0, n=n: e.memset(X.ap[:, :, c0 + n + 1:c0 + n + 2], 0.0), writes=[X])
                norm_mod(c, X, H, ncol, mod["A2"], mod["modT"], 24, ci, sq, tmpf, rstd)
                for (tok0, n, lv, rv), c0 in zip(pieces, C0):
                    if not lv:
                        p.op("pool", lambda e, c0=c0: e.memset(H.ap[:, :, c0:c0 + 1], 0.0), writes=[H])
                    if not rv:
                        p.op("pool", lambda e, c0=c0, n=n: e.memset(H.ap[:, :, c0 + n + 1:c0 + n + 2], 0.0), writes=[H])
                pend_post = []
                for j in range(22):
                    while len(wu_q) < 3 and cnt["wuj"] < 22 * len(BLOCKS):
                        jj_ = cnt["wuj"] % 22
                        cnt["wuj"] += 1
                        wu_ = WU[cnt["wu"] % 4]
                        cnt["wu"] += 1
                        p.dma("sp", wu_.ap[:, :, 0:128], wuv[:, :, jj_ * 128:(jj_ + 1) * 128], reads=[WB_up], writes=[wu_])
                        p.dma("sp", wu_.ap[:, :, 128:256], wuv[:, :, DFF + jj_ * 128:DFF + (jj_ + 1) * 128], reads=[WB_up], writes=[wu_])
                        wu_q.append(wu_)
                    wu = wu_q.pop(0)
                    ACg, ACv = ACgs[j % 2], ACvs[j % 2]
                    PREg, PREv = PREgs[j % 2], PREvs[j % 2]
                    for part, (pre, acc) in enumerate(((PREg, ACg), (PREv, ACv))):
                        cidx = j + 22 * part
                        for (tok0, n, lv, rv), c0 in zip(pieces, C0):
                            for (a, w) in segs(n + 2):
                                pt = ps()
                                for k in range(8):
                                    p.op("pe", lambda e, pt=pt, k=k, a=a, w=w, c0=c0, part=part: e.matmul(
                                        pt.ap[:, :w], lhsT=wu.ap[:, k, part * 128:(part + 1) * 128], rhs=H.ap[:, k, c0 + a:c0 + a + w],
                                        start=(k == 0), stop=(k == 7)), reads=[wu, H], writes=[pt])
                                evac(lambda e, pt=pt, a=a, w=w, c0=c0, pre=pre: e.copy(out=pre.ap[:, c0 + a:c0 + a + w], in_=pt.ap[:, :w]),
                                     lambda e, pt=pt, a=a, w=w, c0=c0, pre=pre: e.tensor_copy(out=pre.ap[:, c0 + a:c0 + a + w], in_=pt.ap[:, :w]),
                                     [pt], [pre])
                            p.op("act", lambda e, c0=c0, n=n, pre=pre, acc=acc, cidx=cidx: e.activation(
                                out=acc.ap[:, c0 + 1:c0 + 1 + n], in_=pre.ap[:, c0 + 1:c0 + 1 + n], func=AF.Identity,
                                scale=cwf.ap[:, cidx, 1:2], bias=bcf.ap[:, cidx:cidx + 1]),
                                reads=[pre, cwf, bcf], writes=[acc])
                            p.op("dve", lambda e, c0=c0, n=n, pre=pre, acc=acc, cidx=cidx: e.scalar_tensor_tensor(
                                out=acc.ap[:, c0 + 1:c0 + 1 + n], in0=pre.ap[:, c0:c0 + n], scalar=cwf.ap[:, cidx, 0:1],
                                in1=acc.ap[:, c0 + 1:c0 + 1 + n], op0=ALU.mult, op1=ALU.add), reads=[pre, cwf, acc], writes=[acc])
                            p.op("dve", lambda e, c0=c0, n=n, pre=pre, acc=acc, cidx=cidx: e.scalar_tensor_tensor(
                                out=acc.ap[:, c0 + 1:c0 + 1 + n], in0=pre.ap[:, c0 + 2:c0 + 2 + n], scalar=cwf.ap[:, cidx, 2:3],
                                in1=acc.ap[:, c0 + 1:c0 + 1 + n], op0=ALU.mult, op1=ALU.add), reads=[pre, cwf, acc], writes=[acc])
                    def post(j=j, ACg=ACg, ACv=ACv):
                        p.op("act", lambda e: e.activation(out=ACg.ap[:, :ncol], in_=ACg.ap[:, :ncol], func=AF.Silu),
                             reads=[ACg], writes=[ACg])
                        for (tok0, n, lv, rv), c0 in zip(pieces, C0):
                            tl = tok0 - t0
                            p.op("pool", lambda e, c0=c0, n=n, tl=tl, j=j: e.tensor_tensor(
                                out=ACTV.ap[:, j, tl:tl + n], in0=ACg.ap[:, c0 + 1:c0 + 1 + n], in1=ACv.ap[:, c0 + 1:c0 + 1 + n],
                                op=ALU.mult), reads=[ACg, ACv], writes=[ACTV])
                    if pend_post:
                        pend_post.pop(0)()
                    pend_post.append(post)
                while pend_post:
                    pend_post.pop(0)()
                for oc in range(8):
                    while len(wd_q) < 2 and cnt["wdj"] < 8 * len(BLOCKS):
                        oc_ = cnt["wdj"] % 8
                        cnt["wdj"] += 1
                        wd_ = WD[cnt["wd"] % 3]
                        cnt["wd"] += 1
                        p.dma("sp", wd_.ap, wdv[:, :, oc_ * 128:(oc_ + 1) * 128], reads=[WB_down], writes=[wd_])
                        wd_q.append(wd_)
                    wd = wd_q.pop(0)
                    for (tok0, n, lv, rv), c0 in zip(pieces, C0):
                        tl = tok0 - t0
                        for (a, w) in segs(n):
                            pt = ps()
                            for f in range(22):
                                p.op("pe", lambda e, pt=pt, f=f, a=a, w=w, tl=tl: e.matmul(
                                    pt.ap[:, :w], lhsT=wd.ap[:, f, :], rhs=ACTV.ap[:, f, tl + a:tl + a + w],
                                    start=(f == 0), stop=(f == 21)), reads=[wd, ACTV], writes=[pt])
                            p.op("dve", lambda e, pt=pt, a=a, w=w, c0=c0: e.scalar_tensor_tensor(
                                out=X.ap[:, oc, c0 + 1 + a:c0 + 1 + a + w], in0=pt.ap[:, :w],
                                scalar=mod["modT"].ap[:, 40 + oc, ci:ci + 1], in1=X.ap[:, oc, c0 + 1 + a:c0 + 1 + a + w],
                                op0=ALU.mult, op1=ALU.add), reads=[pt, X, mod["modT"]], writes=[X])
                for (tok0, n, lv, rv), c0 in zip(pieces, C0):
                    p.dma("pool", xs_views[xo][:, :, tok0:tok0 + n], X.ap[:, :, c0 + 1:c0 + 1 + n], reads=[X], writes=[XS[xo]])
        p.barrier()

    def stage_output(xi):
        xsrc = XS[xi]
        with ExitStack() as es:
            c = Ctx(p, es)
            xt = [c.sb([128, 8, 512], F32, "ox") for _ in range(2)]
            yt = [c.sb([128, 8, 512], F32, "oy") for _ in range(2)]
            ot = [c.sb([128, D], F32, "ot") for _ in range(3)]
            sqo = [c.sb([128, 512], BF16, "sqo") for _ in range(3)]
            rs = c.sb([128, 512], F32, "rso")
            fn = c.sb([128, 8], F32, "fn")
            p.dma("sp", fn.ap, final_norm.rearrange("(k q) -> q k", q=128), writes=[fn])
            toks = [(B["tok_lo"] + a, 512) for B in BLOCKS for a in (0, 512)]
            for bi, (tk0, nn) in enumerate(toks):
                x = xt[bi % 2]
                y = yt[bi % 2]
                p.dma("sp", x.ap, xs_views[xi][:, :, tk0:tk0 + 512], reads=[xsrc], writes=[x])
                pt = ps()
                for k in range(8):
                    sqt = sqo[k % 3]
                    p.op("act", lambda e, sqt=sqt, k=k, x=x: e.activation(out=sqt.ap, in_=x.ap[:, k, :], func=AF.Square),
                         reads=[x], writes=[sqt])
                    p.op("pe", lambda e, sqt=sqt, k=k, pt=pt: e.matmul(pt.ap[:, :512], lhsT=onesb.ap, rhs=sqt.ap,
                                                                         start=(k == 0), stop=(k == 7)), reads=[sqt, onesb], writes=[pt])
                p.op("act", lambda e, pt=pt: e.activation(out=rs.ap, in_=pt.ap[:, :512], func=AF.Ln, scale=1.0 / D,
                                                          bias=epsc.ap[:, 0:1]), reads=[pt, epsc], writes=[rs])
                p.op("act", lambda e: e.activation(out=rs.ap, in_=rs.ap, func=AF.Exp, scale=-0.5), reads=[rs], writes=[rs])
                for k in range(8):
                    p.op("dve", lambda e, k=k, x=x, y=y: e.scalar_tensor_tensor(
                        out=y.ap[:, k, :], in0=x.ap[:, k, :], scalar=fn.ap[:, k:k + 1], in1=rs.ap, op0=ALU.mult, op1=ALU.mult),
                        reads=[x, fn, rs], writes=[y])
                for tt in range(4):
                    o = ot[(bi * 4 + tt) % 3]
                    for k2 in range(2):
                        pt = ps()
                        for kk in range(4):
                            k = k2 * 4 + kk
                            p.op("pe", lambda e, pt=pt, kk=kk, k=k, y=y, tt=tt: e.transpose(
                                out=pt.ap[:, kk * 128:(kk + 1) * 128], in_=y.ap[:, k, tt * 128:(tt + 1) * 128],
                                identity=ident.ap), reads=[y, ident], writes=[pt])
                        if k2 == 0:
                            p.op("act", lambda e, pt=pt, o=o, k2=k2: e.copy(out=o.ap[:, k2 * 512:(k2 + 1) * 512], in_=pt.ap),
                                 reads=[pt], writes=[o])
                        else:
                            p.op("dve", lambda e, pt=pt, o=o, k2=k2: e.tensor_copy(out=o.ap[:, k2 * 512:(k2 + 1) * 512], in_=pt.ap),
                                 reads=[pt], writes=[o])
                    tok0 = tk0 + tt * 128
                    p.dma("pool", yout.ap[tok0:tok0 + 128, :], o.ap, reads=[o], writes=[yout])
        p.barrier()

    GAM = [1.0 - 2.0 ** (-5 - h) for h in range(4)]
    GAMB = GAM[::-1]

    def mix_C(l):
        with ExitStack() as es:
            c = Ctx(p, es)
            RDTt = c.sb([128, 4, 128], F32, "RDT")
            RQDt = c.sb([128, 2, 4, 128], F32, "RQD")
            RKDt = c.sb([128, 2, 512], F32, "RKD")
            p.dma("sp", RDTt.ap, c_rdt, writes=[RDTt])
            p.dma("sp", RQDt.ap, c_rqd, writes=[RQDt])
            p.dma("sp", RKDt.ap, c_rkd, writes=[RKDt])
            S = c.sb([128, 2, 4, 128], F32, "Sr")
            Sbf = c.sb([128, 2, 32, 512], BF16, "Srbf")
            VTM = c.sb([128, 32, 512], BF16, "VTM")
            KTc = [c.sb([128, 512], BF16, "KTc") for _ in range(3)]
            KX = [c.sb([128, 512], BF16, "KX") for _ in range(3)]
            QF = [c.sb([128, 4, 512], BF16, "QF") for _ in range(2)]
            KF = [c.sb([128, 4, 512], BF16, "KF") for _ in range(2)]
            GF = [c.sb([128, 4, 512], BF16, "GF") for _ in range(2)]
            PT = [c.sb([128, 512], BF16, "PTr") for _ in range(3)]
            QXF = [c.sb([128, 4, 128], BF16, "QXF") for _ in range(3)]
            QXB = [c.sb([128, 4, 128], BF16, "QXB") for _ in range(3)]
            SQ = [c.sb([128, 512], BF16, "SQr") for _ in range(3)]
            RS = [c.sb([128, 512], F32, "RSr") for _ in range(3)]
            OT = [c.sb([128, 4, 128], F32, "OTr") for _ in range(3)]
            OST = [c.sb([128, 4, 512], BF16, "OSTr") for _ in range(2)]
            qv = QC.ap.rearrange("(h q) t -> q h t", q=128)
            kv = KC.ap.rearrange("(h q) t -> q h t", q=128)
            gv = GC.ap.rearrange("(h q) t -> q h t", q=128)
            ov = OC.ap.rearrange("(h q) t -> q h t", q=128)
            cnt = {"k": 0, "sp": 0, "ck": 0}
            for si, (tok0, L, ci) in enumerate(SEQS):
                NCk = L // 128
                if ci == 1:
                    for d_ in range(2):
                        p.dma("sp", S.ap[:, d_], st_ret[l][d_].rearrange("h q e -> q h e"), writes=[S])
                else:
                    p.op("pool", lambda e: e.memset(S.ap, 0.0), writes=[S])
                p.dma("sp", VTM.ap[:, 0:NCk, :], VCT.ap[tok0:tok0 + L, :].rearrange("(n q) f -> q n f", q=128),
                      reads=[VCT], writes=[VTM])
                for i in range(NCk):
                    for d_ in range(2):
                        n = i if d_ == 0 else NCk - 1 - i
                        kt = KTc[cnt["k"] % 3]
                        kx = KX[cnt["k"] % 3]
                        cnt["k"] += 1
                        p.dma("sp", kt.ap, KCT.ap[tok0 + n * 128:tok0 + (n + 1) * 128, :], reads=[KCT], writes=[kt])
                        p.op("pool", lambda e, kt=kt, kx=kx, d_=d_: e.tensor_tensor(out=kx.ap, in0=kt.ap, in1=RKDt.ap[:, d_, :], op=ALU.mult),
                             reads=[kt, RKDt], writes=[kx])
                        p.op("act", lambda e, d_=d_, n=n: e.copy(out=Sbf.ap[:, d_, n, :], in_=S.ap[:, d_].rearrange("q h e -> q (h e)")),
                             reads=[S], writes=[Sbf])
                        pt = ps()
                        for h in range(4):
                            hs = slice(h * 128, (h + 1) * 128)
                            p.op("pe", lambda e, pt=pt, hs=hs, kx=kx, n=n: e.matmul(pt.ap[:, hs], lhsT=kx.ap[:, hs], rhs=VTM.ap[:, n, hs],
                                                                                    start=True, stop=True), reads=[kx, VTM], writes=[pt])
                        for h in range(4):
                            hs = slice(h * 128, (h + 1) * 128)
                            gd = (GAM[h] if d_ == 0 else GAMB[h]) ** 128
                            p.op("dve", lambda e, pt=pt, hs=hs, h=h, d_=d_, gd=gd: e.scalar_tensor_tensor(
                                out=S.ap[:, d_, h, :], in0=S.ap[:, d_, h, :], scalar=float(gd), in1=pt.ap[:, hs],
                                op0=ALU.mult, op1=ALU.add), reads=[S, pt], writes=[S])
                if ci == 0:
                    for d_ in range(2):
                        p.dma("pool", ns_ret.ap[si, l, d_].rearrange("h q e -> q h e"), S.ap[:, d_], reads=[S], writes=[ns_ret])
                for span0 in range(0, L, 512):
                    w = min(512, L - span0)
                    qf, kf, gf, ost = QF[cnt["sp"] % 2], KF[cnt["sp"] % 2], GF[cnt["sp"] % 2], OST[cnt["sp"] % 2]
                    cnt["sp"] += 1
                    a = tok0 + span0
                    p.dma("sp", qf.ap[:, :, :w], qv[:, :, a:a + w], reads=[QC], writes=[qf])
                    p.dma("sp", kf.ap[:, :, :w], kv[:, :, a:a + w], reads=[KC], writes=[kf])
                    p.dma("sp", gf.ap[:, :, :w], gv[:, :, a:a + w], reads=[GC], writes=[gf])
                    pend_c = []
                    for cc in range(w // 128):
                        n = span0 // 128 + cc
                        cs = slice(cc * 128, (cc + 1) * 128)
                        k2 = cnt["ck"] % 3
                        cnt["ck"] += 1
                        ptt, qxf, qxb, sq, rs, ot = PT[k2], QXF[k2], QXB[k2], SQ[k2], RS[k2], OT[k2]
                        pts = ps()
                        for h in range(4):
                            hs = slice(h * 128, (h + 1) * 128)
                            p.op("pe", lambda e, pts=pts, hs=hs, h=h, cs=cs: e.matmul(pts.ap[:, hs], lhsT=kf.ap[:, h, cs], rhs=qf.ap[:, h, cs],
                                                                                      start=True, stop=True), reads=[kf, qf], writes=[pts])
                        p.op("dve", lambda e, pts=pts, ptt=ptt: e.tensor_tensor(out=ptt.ap, in0=pts.ap, in1=RDTt.ap.rearrange("q h c -> q (h c)"),
                                                                                op=ALU.mult), reads=[pts, RDTt], writes=[ptt])
                        p.op("pool", lambda e, qxf=qxf, cs=cs: e.tensor_tensor(out=qxf.ap, in0=qf.ap[:, :, cs], in1=RQDt.ap[:, 0], op=ALU.mult),
                             reads=[qf, RQDt], writes=[qxf])
                        p.op("pool", lambda e, qxb=qxb, cs=cs: e.tensor_tensor(out=qxb.ap, in0=qf.ap[:, :, cs], in1=RQDt.ap[:, 1], op=ALU.mult),
                             reads=[qf, RQDt], writes=[qxb])
                        pto = ps()
                        for h in range(4):
                            hs = slice(h * 128, (h + 1) * 128)
                            p.op("pe", lambda e, pto=pto, hs=hs, h=h, n=n, qxf=qxf: e.matmul(
                                pto.ap[:, hs], lhsT=Sbf.ap[:, 0, n, hs], rhs=qxf.ap[:, h, :], start=True, stop=False),
                                reads=[Sbf, qxf], writes=[pto])
                            p.op("pe", lambda e, pto=pto, hs=hs, h=h, n=n, qxb=qxb: e.matmul(
                                pto.ap[:, hs], lhsT=Sbf.ap[:, 1, n, hs], rhs=qxb.ap[:, h, :], start=False, stop=False),
                                reads=[Sbf, qxb], writes=[pto])
                            p.op("pe", lambda e, pto=pto, hs=hs, h=h, n=n, ptt=ptt: e.matmul(
                                pto.ap[:, hs], lhsT=VTM.ap[:, n, hs], rhs=ptt.ap[:, hs], start=False, stop=True),
                                reads=[VTM, ptt], writes=[pto])
                        p.op("act", lambda e, pto=pto, sq=sq: e.activation(out=sq.ap, in_=pto.ap, func=AF.Square), reads=[pto], writes=[sq])

                        def post_c(pto=pto, sq=sq, rs=rs, ot=ot, cs=cs, gf=gf, ost=ost):
                            _post_c(pto, sq, rs, ot, cs, gf, ost)
                        if pend_c:
                            pend_c.pop(0)()
                        pend_c.append(post_c)
                    while pend_c:
                        pend_c.pop(0)()
                    p.dma("pool", ov[:, :, a:a + w], ost.ap[:, :, :w], reads=[ost], writes=[OC])
        p.barrier()

    def _post_c(pto, sq, rs, ot, cs, gf, ost):
        if True:
            if True:
                if True:
                    if True:
                        ptn = ps()
                        p.op("pe", lambda e, ptn=ptn, sq=sq: e.matmul(ptn.ap, lhsT=onesb.ap, rhs=sq.ap, start=True, stop=True),
                             reads=[onesb, sq], writes=[ptn])
                        p.op("act", lambda e, ptn=ptn, rs=rs: e.activation(out=rs.ap, in_=ptn.ap, func=AF.Ln, scale=1.0 / 128, bias=epsc.ap[:, 0:1]),
                             reads=[ptn, epsc], writes=[rs])
                        p.op("act", lambda e, rs=rs: e.activation(out=rs.ap, in_=rs.ap, func=AF.Exp, scale=-0.5), reads=[rs], writes=[rs])
                        p.op("dve", lambda e, pto=pto, rs=rs, ot=ot: e.tensor_tensor(out=ot.ap.rearrange("q h c -> q (h c)"), in0=pto.ap, in1=rs.ap,
                                                                                     op=ALU.mult), reads=[pto, rs], writes=[ot])
                        p.op("pool", lambda e, ot=ot, cs=cs: e.tensor_tensor(out=ost.ap[:, :, cs], in0=ot.ap, in1=gf.ap[:, :, cs], op=ALU.mult),
                             reads=[ot, gf], writes=[ost])

    OAF = dscr("oaf", [512, TTOT], F32)
    OAB = dscr("oab", [512, TTOT], F32)

    def mix_A(l):
        with ExitStack() as es:
            c = Ctx(p, es)
            UT = c.sb([128, 2, 128], F32, "UT")
            NM = c.sb([128, 4, 128], F32, "NM")
            SEL = c.sb([4, 4, 128], F32, "SEL")
            NLM = c.sb([128, 2, 7, 128], BF16, "NLM")
            NLMf = c.sb([128, 2, 7, 128], F32, "NLMf")
            p.dma("sp", NLMf.ap, c_nlm, writes=[NLMf])
            p.op("dve", lambda e: e.tensor_copy(out=NLM.ap, in_=NLMf.ap), reads=[NLMf], writes=[NLM])
            na = c.sb([128, 1], F32, "na")
            p.dma("sp", UT.ap, c_ut, writes=[UT])
            p.dma("sp", NM.ap, c_nm, writes=[NM])
            p.dma("sp", SEL.ap, c_sel, writes=[SEL])
            p.dma("sp", na.ap, norm_a[l].rearrange("(q o) -> q o", o=1), writes=[na])
            S = c.sb([128, 2, 4, 128], F32, "Sd")
            Sbf = c.sb([128, 2, 512], BF16, "Sdbf")

            KI = 6

            cg = None
            if l == 0 and LATE_CAST:
                cg = cast_gen(c, [1], CW=1024, nbuf=2, lq="actq")
            esm = ExitStack()
            cur = {"c": Ctx(p, esm)}

            def rot(shape, dt, name, n=KI):
                return [cur["c"].sb(shape, dt, name) for _ in range(n)]
            GQ, GK = rot([128, 4, 6 * 128], BF16, "GQ", 2), rot([128, 4, 6 * 128], BF16, "GK", 2)
            GKT, GVT = rot([128, 6, 512], BF16, "GKT", 2), rot([128, 6, 512], BF16, "GVT", 2)
            GG, GB = rot([128, 6, 8], F32, "GG", 2), rot([128, 6, 8], F32, "GB", 2)
            gcol, ngcol, bgc, negb, kgs, egl, glc = (rot([128, 4], F32, nm_) for nm_ in ("gcol", "ngcol", "bgc", "negb", "kgs", "egl", "glc"))
            gcr = rot([4, 128], F32, "gcr")
            E_, ET_, eR_ = rot([128, 4, 128], F32, "E_"), rot([128, 4, 128], F32, "ET_"), rot([128, 4, 128], F32, "eR_", 3)
            XT_, Xm_, AT_, QG_, P_ = (rot([128, 4, 128], BF16, nm_) for nm_ in ("XT_", "Xm_", "AT_", "QG_", "P_"))
            Mm_, Zt_, TMb_ = (rot([128, 4, 128], BF16, nm_) for nm_ in ("Mm_", "Zt_", "TMb_"))
            KBG_, VB_, KG_, NWT_, VN_ = (rot([128, 4, 128], BF16, nm_) for nm_ in ("KBG_", "VB_", "KG_", "NWT_", "VN_"))
            OSTa = rot([128, 4, 128], F32, "OSTa", 3)
            cnt = {"i": 0, "ev": 0}

            def evac(fn_act, fn_dve, reads, writes):
                cnt["ev"] += 1
                if cnt["ev"] % 2 == 0:
                    p.op("act", fn_act, reads=reads, writes=writes)
                else:
                    p.op("dve", fn_dve, reads=reads, writes=writes)

            def f2(t_):
                return t_.ap.rearrange("q h c -> q (h c)")

            qv = QA.ap.rearrange("(h q) t -> q h t", q=128)
            kv = KA.ap.rearrange("(h q) t -> q h t", q=128)
            ofv = [OAF.ap.rearrange("(h q) t -> q h t", q=128), OAB.ap.rearrange("(h q) t -> q h t", q=128)]
            odst = [OAF, OAB]
            for si, (tok0, L, ci) in enumerate(SEQS):
                NCk = L // 128
                if ci == 1:
                    for d_ in range(2):
                        p.dma("sp", S.ap[:, d_], st_delta[l][d_].rearrange("h q e -> q h e"), writes=[S])
                else:
                    p.op("pool", lambda e: e.memset(S.ap, 0.0), writes=[S])
                p.op("act", lambda e: e.copy(out=Sbf.ap, in_=S.ap.rearrange("q d h e -> q d (h e)")), reads=[S], writes=[Sbf])
                items = []
                for i in range(NCk):
                    for d_ in range(2):
                        items.append((d_, i if d_ == 0 else NCk - 1 - i))

                def load_group(grp_, gs):
                    slots = {}
                    for dd, s0 in ((0, 0), (1, 3)):
                        ns = sorted(n_ for (d2, n_) in grp_ if d2 == dd)
                        if not ns:
                            continue
                        a0 = tok0 + ns[0] * 128
                        w = len(ns) * 128
                        for n_ in ns:
                            slots[(dd, n_)] = s0 + (n_ - ns[0])
                        p.dma("sp", GQ[gs].ap[:, :, s0 * 128:s0 * 128 + w], qv[:, :, a0:a0 + w], reads=[QA], writes=[GQ[gs]])
                        p.dma("sp", GK[gs].ap[:, :, s0 * 128:s0 * 128 + w], kv[:, :, a0:a0 + w], reads=[KA], writes=[GK[gs]])
                        p.dma("sp", GKT[gs].ap[:, s0:s0 + len(ns), :], KAT.ap[a0:a0 + w, :].rearrange("(n q) f -> q n f", q=128),
                              reads=[KAT], writes=[GKT[gs]])
                        p.dma("sp", GVT[gs].ap[:, s0:s0 + len(ns), :], VAT.ap[a0:a0 + w, :].rearrange("(n q) f -> q n f", q=128),
                              reads=[VAT], writes=[GVT[gs]])
                        p.dma("sp", GG[gs].ap[:, s0:s0 + len(ns), :], GA.ap[a0:a0 + w, :].rearrange("(n q) j -> q n j", q=128),
                              reads=[GA], writes=[GG[gs]])
                        p.dma("sp", GB[gs].ap[:, s0:s0 + len(ns), :], BA.ap[a0:a0 + w, :].rearrange("(n q) j -> q n j", q=128),
                              reads=[BA], writes=[GB[gs]])
                    return slots

                def prep_gen(d_, n, r2, gs, slot):
                    a = tok0 + n * 128
                    qf = T(GQ[gs].ap[:, :, slot * 128:(slot + 1) * 128], GQ[gs].key)
                    kf = T(GK[gs].ap[:, :, slot * 128:(slot + 1) * 128], GK[gs].key)
                    kt = T(GKT[gs].ap[:, slot, :].rearrange("q (h c) -> q h c", h=4), GKT[gs].key)
                    vt = T(GVT[gs].ap[:, slot, :].rearrange("q (h c) -> q h c", h=4), GVT[gs].key)
                    gg = T(GG[gs].ap[:, slot, :], GG[gs].key)
                    bb = T(GB[gs].ap[:, slot, :], GB[gs].key)
                    gco, ngc, bg, nb_, kg_s, eg, gl = gcol[r2], ngcol[r2], bgc[r2], negb[r2], kgs[r2], egl[r2], glc[r2]
                    gr = gcr[r2]
                    E, ET, eR = E_[r2], ET_[r2], eR_[r2 % 3]
                    XT, Xm, AT, QG, P = XT_[r2], Xm_[r2], AT_[r2], QG_[r2], P_[r2]
                    KBG, VB, KG, NWT, VN = KBG_[r2], VB_[r2], KG_[r2], NWT_[r2], VN_[r2]
                    ds_ = slice(d_ * 4, d_ * 4 + 4)
                    yield
                    pg = ps()
                    p.op("pe", lambda e, pg=pg, gg=gg, ds_=ds_, d_=d_: e.matmul(pg.ap[:, 0:4], lhsT=UT.ap[:, d_, :], rhs=gg.ap[:, ds_],
                                                                                start=True, stop=True), reads=[UT, gg], writes=[pg])
                    p.op("pe", lambda e, pg=pg, gg=gg, ds_=ds_: e.matmul(pg.ap[:, 4:8], lhsT=ones.ap, rhs=gg.ap[:, ds_],
                                                                         start=True, stop=True), reads=[ones, gg], writes=[pg])
                    p.op("dve", lambda e, pg=pg, gco=gco: e.tensor_copy(out=gco.ap, in_=pg.ap[:, 0:4]), reads=[pg], writes=[gco])
                    yield
                    p.op("dve", lambda e, pg=pg, ngc=ngc: e.tensor_scalar(out=ngc.ap, in0=pg.ap[:, 0:4], scalar1=-1.0, scalar2=None, op0=ALU.mult),
                         reads=[pg], writes=[ngc])
                    p.op("dve", lambda e, pg=pg, gl=gl: e.tensor_copy(out=gl.ap, in_=pg.ap[:, 4:8]), reads=[pg], writes=[gl])
                    p.op("act", lambda e, gco=gco, bg=bg: e.activation(out=bg.ap, in_=gco.ap, func=AF.Exp), reads=[gco], writes=[bg])
                    p.op("dve", lambda e, bg=bg, bb=bb, ds_=ds_: e.tensor_tensor(out=bg.ap, in0=bg.ap, in1=bb.ap[:, ds_], op=ALU.mult),
                         reads=[bg, bb], writes=[bg])
                    p.op("dve", lambda e, nb_=nb_, bb=bb, ds_=ds_: e.tensor_scalar(out=nb_.ap, in0=bb.ap[:, ds_], scalar1=-1.0, scalar2=None, op0=ALU.mult),
                         reads=[bb], writes=[nb_])
                    p.op("dve", lambda e, kg_s=kg_s, gl=gl, gco=gco: e.tensor_tensor(out=kg_s.ap, in0=gl.ap, in1=gco.ap, op=ALU.subtract),
                         reads=[gl, gco], writes=[kg_s])
                    p.op("act", lambda e, kg_s=kg_s: e.activation(out=kg_s.ap, in_=kg_s.ap, func=AF.Exp), reads=[kg_s], writes=[kg_s])
                    p.op("act", lambda e, eg=eg, gl=gl: e.activation(out=eg.ap, in_=gl.ap, func=AF.Exp), reads=[gl], writes=[eg])
                    yield
                    ptr = ps()
                    p.op("pe", lambda e, ptr=ptr, gco=gco: e.transpose(out=ptr.ap[0:4, 0:128], in_=gco.ap, identity=ident.ap),
                         reads=[gco, ident], writes=[ptr])
                    p.op("dve", lambda e, ptr=ptr, gr=gr: e.tensor_copy(out=gr.ap, in_=ptr.ap[0:4, 0:128]), reads=[ptr], writes=[gr])
                    yield
                    pR = ps()
                    for h in range(4):
                        hs = slice(h * 128, (h + 1) * 128)
                        p.op("pe", lambda e, pR=pR, hs=hs, h=h, gr=gr: e.matmul(pR.ap[:, hs], lhsT=SEL.ap[:, h, :], rhs=gr.ap,
                                                                                start=True, stop=True), reads=[SEL, gr], writes=[pR])
                    pRv = pR.ap.rearrange("q (h c) -> q h c", h=4)
                    mi = 0 if d_ == 0 else 1
                    ms = 2 if d_ == 0 else 3
                    yield
                    p.op("dve", lambda e, pRv=pRv, E=E, mi=mi: e.tensor_tensor(
                        out=E.ap, in0=pRv, in1=NM.ap[:, mi, :].unsqueeze(1).broadcast_to([128, 4, 128]), op=ALU.add),
                        reads=[pR, NM], writes=[E])
                    p.op("dve", lambda e, pRv=pRv, ET=ET, ms=ms: e.scalar_tensor_tensor(
                        out=ET.ap, in0=pRv, scalar=-1.0, in1=NM.ap[:, ms, :].unsqueeze(1).broadcast_to([128, 4, 128]),
                        op0=ALU.mult, op1=ALU.add), reads=[pR, NM], writes=[ET])
                    for h in range(4):
                        p.op("act", lambda e, E=E, h=h, ngc=ngc: e.activation(out=E.ap[:, h, :], in_=E.ap[:, h, :], func=AF.Exp,
                                                                             bias=ngc.ap[:, h:h + 1]), reads=[E, ngc], writes=[E])
                        p.op("act", lambda e, ET=ET, h=h, gco=gco: e.activation(out=ET.ap[:, h, :], in_=ET.ap[:, h, :], func=AF.Exp,
                                                                               bias=gco.ap[:, h:h + 1]), reads=[ET, gco], writes=[ET])
                    p.op("act", lambda e, eR=eR, pRv=pRv: e.activation(out=eR.ap, in_=pRv, func=AF.Exp), reads=[pR], writes=[eR])
                    p.op("pool", lambda e, QG=QG, qf=qf, eR=eR: e.tensor_tensor(out=QG.ap, in0=qf.ap, in1=eR.ap, op=ALU.mult),
                         reads=[qf, eR], writes=[QG])
                    yield
                    pkk = ps()
                    for h in range(4):
                        hs = slice(h * 128, (h + 1) * 128)
                        p.op("pe", lambda e, pkk=pkk, hs=hs, h=h, kf=kf: e.matmul(pkk.ap[:, hs], lhsT=kf.ap[:, h, :], rhs=kf.ap[:, h, :],
                                                                                  start=True, stop=True), reads=[kf], writes=[pkk])
                    yield
                    for h in range(4):
                        hs = slice(h * 128, (h + 1) * 128)
                        p.op("dve", lambda e, pkk=pkk, hs=hs, h=h, XT=XT, ET=ET, nb_=nb_: e.scalar_tensor_tensor(
                            out=XT.ap[:, h, :], in0=pkk.ap[:, hs], scalar=nb_.ap[:, h:h + 1], in1=ET.ap[:, h, :],
                            op0=ALU.mult, op1=ALU.mult), reads=[pkk, nb_, ET], writes=[XT])
                    yield
                    pqk = ps()
                    for h in range(4):
                        hs = slice(h * 128, (h + 1) * 128)
                        p.op("pe", lambda e, pqk=pqk, hs=hs, h=h, kf=kf, qf=qf: e.matmul(pqk.ap[:, hs], lhsT=kf.ap[:, h, :], rhs=qf.ap[:, h, :],
                                                                                         start=True, stop=True), reads=[kf, qf], writes=[pqk])
                    yield
                    p.op("dve", lambda e, pqk=pqk, AT=AT, E=E: e.tensor_tensor(out=f2(AT), in0=pqk.ap, in1=f2(E), op=ALU.mult),
                         reads=[pqk, E], writes=[AT])
                    yield
                    ptx = ps()
                    ptxb = ptx.ap.bitcast(BF16)
                    for h in range(4):
                        hs = slice(h * 128, (h + 1) * 128)
                        p.op("pe", lambda e, ptxb=ptxb, hs=hs, h=h, XT=XT: e.transpose(out=ptxb[:, hs], in_=XT.ap[:, h, :], identity=identb.ap),
                             reads=[XT, identb], writes=[ptx])
                    p.op("act", lambda e, ptxb=ptxb, Xm=Xm: e.copy(out=f2(Xm), in_=ptxb[:, 0:512]), reads=[ptx], writes=[Xm])
                    yield
                    Mm, Zt, TMb = Mm_[r2], Zt_[r2], TMb_[r2]
                    p.op("dve", lambda e, P=P, Xm=Xm, d_=d_: e.tensor_tensor(
                        out=P.ap, in0=Xm.ap, in1=NLM.ap[:, d_, 0, :].unsqueeze(1).broadcast_to([128, 4, 128]), op=ALU.mult),
                        reads=[Xm, NLM], writes=[P])
                    p.op("dve", lambda e, P=P: e.tensor_tensor(out=P.ap, in0=P.ap, in1=identb.ap.unsqueeze(1).broadcast_to([128, 4, 128]),
                                                                op=ALU.add), reads=[P, identb], writes=[P])
                    p.op("dve", lambda e, Mm=Mm, XT=XT, d_=d_: e.tensor_tensor(
                        out=Mm.ap, in0=XT.ap, in1=NLM.ap[:, 1 - d_, 0, :].unsqueeze(1).broadcast_to([128, 4, 128]), op=ALU.mult),
                        reads=[XT, NLM], writes=[Mm])
                    p.op("dve", lambda e, Mm=Mm: e.tensor_tensor(out=Mm.ap, in0=Mm.ap, in1=identb.ap.unsqueeze(1).broadcast_to([128, 4, 128]),
                                                                  op=ALU.add), reads=[Mm, identb], writes=[Mm])
                    yield
                    for lev in range(1, 7):
                        pz = ps()
                        for h in range(4):
                            hs = slice(h * 128, (h + 1) * 128)
                            p.op("pe", lambda e, pz=pz, hs=hs, h=h, XT=XT, P=P: e.matmul(pz.ap[:, hs], lhsT=XT.ap[:, h, :], rhs=P.ap[:, h, :],
                                                                                         start=True, stop=True), reads=[XT, P], writes=[pz])
                        yield
                        p.op("act", lambda e, pz=pz, Zt=Zt: e.copy(out=f2(Zt), in_=pz.ap), reads=[pz], writes=[Zt])
                        yield
                        pwq = ps()
                        for h in range(4):
                            hs = slice(h * 128, (h + 1) * 128)
                            p.op("pe", lambda e, pwq=pwq, hs=hs, h=h, Mm=Mm, Zt=Zt: e.matmul(pwq.ap[:, hs], lhsT=Mm.ap[:, h, :], rhs=Zt.ap[:, h, :],
                                                                                             start=True, stop=True), reads=[Mm, Zt], writes=[pwq])
                        yield
                        p.op("dve", lambda e, pwq=pwq, TMb=TMb, d_=d_, lev=lev: e.tensor_tensor(
                            out=TMb.ap, in0=pwq.ap.rearrange("q (h c) -> q h c", h=4),
                            in1=NLM.ap[:, d_, lev, :].unsqueeze(1).broadcast_to([128, 4, 128]), op=ALU.mult), reads=[pwq, NLM], writes=[TMb])
                        p.op("dve", lambda e, P=P, TMb=TMb: e.tensor_tensor(out=P.ap, in0=P.ap, in1=TMb.ap, op=ALU.add), reads=[P, TMb], writes=[P])
                        yield
                        if lev < 6:
                            ptm = ps()
                            ptmb = ptm.ap.bitcast(BF16)
                            for h in range(4):
                                hs = slice(h * 128, (h + 1) * 128)
                                p.op("pe", lambda e, ptmb=ptmb, hs=hs, h=h, P=P: e.transpose(out=ptmb[:, hs], in_=P.ap[:, h, :], identity=identb.ap),
                                     reads=[P, identb], writes=[ptm])
                            yield
                            p.op("act", lambda e, ptmb=ptmb, Mm=Mm: e.copy(out=f2(Mm), in_=ptmb[:, 0:512]), reads=[ptm], writes=[Mm])
                            yield
                    yield
                    p.op("pool", lambda e, KBG=KBG, kt=kt, bg=bg: e.tensor_tensor(
                        out=KBG.ap, in0=kt.ap, in1=bg.ap.unsqueeze(2).broadcast_to([128, 4, 128]), op=ALU.mult), reads=[kt, bg], writes=[KBG])
                    p.op("pool", lambda e, VB=VB, vt=vt, bb=bb, ds_=ds_: e.tensor_tensor(
                        out=VB.ap, in0=vt.ap, in1=bb.ap[:, ds_].unsqueeze(2).broadcast_to([128, 4, 128]), op=ALU.mult), reads=[vt, bb], writes=[VB])
                    p.op("pool", lambda e, KG=KG, kt=kt, kg_s=kg_s: e.tensor_tensor(
                        out=KG.ap, in0=kt.ap, in1=kg_s.ap.unsqueeze(2).broadcast_to([128, 4, 128]), op=ALU.mult), reads=[kt, kg_s], writes=[KG])
                    yield
                    pw = ps()
                    for h in range(4):
                        hs = slice(h * 128, (h + 1) * 128)
                        p.op("pe", lambda e, pw=pw, hs=hs, h=h, KBG=KBG, P=P: e.matmul(pw.ap[:, hs], lhsT=KBG.ap[:, h, :], rhs=P.ap[:, h, :],
                                                                                       start=True, stop=True), reads=[KBG, P], writes=[pw])
                    p.op("act", lambda e, pw=pw, NWT=NWT: e.mul(out=f2(NWT), in_=pw.ap, mul=-1.0), reads=[pw], writes=[NWT])

                def rec_gen(d_, n, r2):
                    a = tok0 + n * 128
                    gco, ngc, bg, nb_, kg_s, eg, gl = gcol[r2], ngcol[r2], bgc[r2], negb[r2], kgs[r2], egl[r2], glc[r2]
                    gr = gcr[r2]
                    E, ET, eR = E_[r2], ET_[r2], eR_[r2 % 3]
                    XT, Xm, AT, QG, P = XT_[r2], Xm_[r2], AT_[r2], QG_[r2], P_[r2]
                    KBG, VB, KG, NWT, VN = KBG_[r2], VB_[r2], KG_[r2], NWT_[r2], VN_[r2]
                    ds_ = slice(d_ * 4, d_ * 4 + 4)
                    pv = ps()
                    for h in range(4):
                        hs = slice(h * 128, (h + 1) * 128)
                        p.op("pe", lambda e, pv=pv, hs=hs, h=h, P=P, VB=VB: e.matmul(pv.ap[:, hs], lhsT=P.ap[:, h, :], rhs=VB.ap[:, h, :],
                                                                                     start=True, stop=False), reads=[P, VB], writes=[pv])
                        p.op("pe", lambda e, pv=pv, hs=hs, h=h, NWT=NWT, d_=d_: e.matmul(pv.ap[:, hs], lhsT=NWT.ap[:, h, :], rhs=Sbf.ap[:, d_, hs],
                                                                                         start=False, stop=True), reads=[NWT, Sbf], writes=[pv])
                    yield
                    p.op("act", lambda e, pv=pv, VN=VN: e.copy(out=f2(VN), in_=pv.ap), reads=[pv], writes=[VN])
                    yield
                    po = ps()
                    for h in range(4):
                        hs = slice(h * 128, (h + 1) * 128)
                        p.op("pe", lambda e, po=po, hs=hs, h=h, QG=QG, d_=d_: e.matmul(po.ap[:, hs], lhsT=Sbf.ap[:, d_, hs], rhs=QG.ap[:, h, :],
                                                                                       start=True, stop=False), reads=[Sbf, QG], writes=[po])
                        p.op("pe", lambda e, po=po, hs=hs, h=h, VN=VN, AT=AT: e.matmul(po.ap[:, hs], lhsT=VN.ap[:, h, :], rhs=AT.ap[:, h, :],
                                                                                       start=False, stop=True), reads=[VN, AT], writes=[po])
                    yield
                    ost = OSTa[r2 % 3]
                    p.op("dve", lambda e, po=po, ost=ost: e.tensor_copy(out=f2(ost), in_=po.ap), reads=[po], writes=[ost])
                    p.dma("pool", ofv[d_][:, :, a:a + 128], ost.ap, reads=[ost], writes=[odst[d_]])
                    pss = ps()
                    for h in range(4):
                        hs = slice(h * 128, (h + 1) * 128)
                        p.op("pe", lambda e, pss=pss, hs=hs, h=h, KG=KG, VN=VN: e.matmul(pss.ap[:, hs], lhsT=KG.ap[:, h, :], rhs=VN.ap[:, h, :],
                                                                                         start=True, stop=True), reads=[KG, VN], writes=[pss])
                    for h in range(4):
                        hs = slice(h * 128, (h + 1) * 128)
                        p.op("dve", lambda e, pss=pss, hs=hs, h=h, d_=d_, eg=eg: e.scalar_tensor_tensor(
                            out=S.ap[:, d_, h, :], in0=S.ap[:, d_, h, :], scalar=eg.ap[:, h:h + 1], in1=pss.ap[:, hs],
                            op0=ALU.mult, op1=ALU.add), reads=[S, eg, pss], writes=[S])
                    p.op("act", lambda e, d_=d_: e.copy(out=Sbf.ap[:, d_, :], in_=S.ap[:, d_].rearrange("q h e -> q (h e)")),
                         reads=[S], writes=[Sbf])

                def lockstep(gens, stagger=0):
                    gens = list(gens)
                    done = [False] * len(gens)
                    r_ = 0
                    while not all(done):
                        for j_, g_ in enumerate(gens):
                            if done[j_] or r_ < j_ * stagger:
                                continue
                            try:
                                next(g_)
                            except StopIteration:
                                done[j_] = True
                        r_ += 1

                groups = [items[g0:g0 + KI] for g0 in range(0, len(items), KI)]
                slotmaps = {0: load_group(groups[0], 0)}
                for gi_, grp in enumerate(groups):
                    if gi_ + 1 < len(groups):
                        slotmaps[gi_ + 1] = load_group(groups[gi_ + 1], (gi_ + 1) % 2)
                    sm_ = slotmaps.pop(gi_)
                    lockstep([prep_gen(d_, n, r_, gi_ % 2, sm_[(d_, n)]) for r_, (d_, n) in enumerate(grp)], stagger=dbg_opts.get("stagger", 6))
                    for q0 in range(0, len(grp), 2):
                        lockstep([rec_gen(d_, n, q0 + r_) for r_, (d_, n) in enumerate(grp[q0:q0 + 2])])
                    if cg is not None:
                        for _ in range(12):
                            if next(cg, "done") == "done":
                                cg = None
                                break
                if ci == 0:
                    for d_ in range(2):
                        p.dma("pool", ns_delta.ap[si, l, d_].rearrange("h q e -> q h e"), S.ap[:, d_], reads=[S], writes=[ns_delta])
            if cg is not None:
                for _ in cg:
                    pass
            p.barrier()
            esm.close()
            cur["c"] = c
            OFs = rot([128, 4, 512], F32, "OFs", 2)
            OBs = rot([128, 4, 512], F32, "OBs", 2)
            ZFs = rot([128, 4, 512], BF16, "ZFs", 2)
            OOs = rot([128, 4, 512], BF16, "OOs", 2)
            SQa = rot([128, 512], BF16, "SQa", 3)
            RSa = rot([128, 512], F32, "RSa", 3)
            zv = ZA.ap.rearrange("(h q) t -> q h t", q=128)
            oav = OA.ap.rearrange("(h q) t -> q h t", q=128)
            for bi, t0 in enumerate(range(0, TTOT, 512)):
                of_, ob_, zf, oo = OFs[bi % 2], OBs[bi % 2], ZFs[bi % 2], OOs[bi % 2]
                p.dma("sp", of_.ap, ofv[0][:, :, t0:t0 + 512], reads=[OAF], writes=[of_])
                p.dma("sp", ob_.ap, ofv[1][:, :, t0:t0 + 512], reads=[OAB], writes=[ob_])
                p.dma("sp", zf.ap, zv[:, :, t0:t0 + 512], reads=[ZA], writes=[zf])
                p.op("pool", lambda e, of_=of_, ob_=ob_: e.tensor_tensor(out=of_.ap, in0=of_.ap, in1=ob_.ap, op=ALU.add), reads=[of_, ob_], writes=[of_])
                pend_a = []
                for h in range(4):
                    k3 = (bi * 4 + h) % 3
                    sq, rs = SQa[k3], RSa[k3]
                    p.op("act", lambda e, sq=sq, of_=of_, h=h: e.activation(out=sq.ap, in_=of_.ap[:, h, :], func=AF.Square), reads=[of_], writes=[sq])
                    ptn = ps()
                    p.op("pe", lambda e, ptn=ptn, sq=sq: e.matmul(ptn.ap, lhsT=onesb.ap, rhs=sq.ap, start=True, stop=True), reads=[onesb, sq], writes=[ptn])

                    def post_a(ptn=ptn, rs=rs, of_=of_, zf=zf, oo=oo, h=h):
                        p.op("act", lambda e: e.activation(out=rs.ap, in_=ptn.ap, func=AF.Ln, scale=1.0 / 128, bias=epsc.ap[:, 0:1]),
                             reads=[ptn, epsc], writes=[rs])
                        p.op("act", lambda e: e.activation(out=rs.ap, in_=rs.ap, func=AF.Exp, scale=-0.5), reads=[rs], writes=[rs])
                        p.op("dve", lambda e: e.scalar_tensor_tensor(
                            out=rs.ap, in0=of_.ap[:, h, :], scalar=na.ap[:, 0:1], in1=rs.ap, op0=ALU.mult, op1=ALU.mult), reads=[of_, na, rs], writes=[rs])
                        p.op("pool", lambda e: e.tensor_tensor(out=oo.ap[:, h, :], in0=rs.ap, in1=zf.ap[:, h, :], op=ALU.mult),
                             reads=[rs, zf], writes=[oo])
                    if pend_a:
                        pend_a.pop(0)()
                    pend_a.append(post_a)
                while pend_a:
                    pend_a.pop(0)()
                p.dma("pool", oav[:, :, t0:t0 + 512], oo.ap, reads=[oo], writes=[OA])
        p.barrier()

    YSCR = dscr("yscr", [8, 16, 32, TTOT // 8], F32)
    MAGIC = 12582912.0
    TWO_PI_LO = 6.2831845
    NCP = NSEQ_P * LP // 8
    NCS = LS // 8

    def mix_B(l):
        with ExitStack() as es:
            c = Ctx(p, es)
            Tg = c.sb([128, 32, 128], BF16, "Tg")
            Pm = [[c.sb([128, 32, 128], BF16, "Pm%d%d" % (d_, v_)) for v_ in range(2)] for d_ in range(2)]
            Qm = [[c.sb([128, 32, 128], BF16, "Qm%d%d" % (d_, v_)) for v_ in range(2)] for d_ in range(2)]
            RHO = c.sb([128, 2, 32], F32, "RHO")
            F8 = c.sb([128, 2, 32], F32, "F8")
            X0 = c.sb([128, 2, 32], F32, "X0")
            KV513 = c.sb([128, 513], F32, "KV513")
            hpi = c.sb([128, 1], F32, "hpi")
            p.dma("sp", KV513.ap, c_kv513, writes=[KV513])
            p.op("dve", lambda e: e.memset(hpi.ap, math.pi / 2), writes=[hpi])

            def rnd_frac(cx, t_, tmp_):
                p.op("dve", lambda e: e.tensor_scalar(out=tmp_.ap, in0=t_.ap, scalar1=MAGIC, scalar2=None, op0=ALU.add), reads=[t_], writes=[tmp_])
                p.op("dve", lambda e: e.tensor_scalar(out=tmp_.ap, in0=tmp_.ap, scalar1=-MAGIC, scalar2=None, op0=ALU.add), reads=[tmp_], writes=[tmp_])
                p.op("dve", lambda e: e.tensor_tensor(out=t_.ap, in0=t_.ap, in1=tmp_.ap, op=ALU.subtract), reads=[t_, tmp_], writes=[t_])

            def sincos_turns(t_, tmp_, sin_o, cos_o):
                rnd_frac(None, t_, tmp_)
                p.op("act", lambda e: e.activation(out=sin_o.ap, in_=t_.ap, func=AF.Sin, scale=TWO_PI_LO), reads=[t_], writes=[sin_o])
                p.op("dve", lambda e: e.scalar_tensor_tensor(out=tmp_.ap, in0=t_.ap, scalar=-1.0, in1=t_.ap, op0=ALU.mult, op1=ALU.max), reads=[t_], writes=[tmp_])
                p.op("act", lambda e: e.activation(out=cos_o.ap, in_=tmp_.ap, func=AF.Sin, scale=-TWO_PI_LO, bias=hpi.ap[:, 0:1]),
                     reads=[tmp_, hpi], writes=[cos_o])

            with ExitStack() as es2:
                t = Ctx(p, es2)
                LR = t.sb([128, 2, 32], F32, "LR")
                LI = t.sb([128, 2, 32], F32, "LI")
                DT = t.sb([128, 2, 32], F32, "DT")
                AR = t.sb([128, 2, 32], F32, "AR")
                TH = t.sb([128, 2, 32], F32, "TH")
                for half in range(2):
                    hs_ = slice(half * 64, half * 64 + 64)
                    for d_ in range(2):
                        p.dma("sp", LR.ap[hs_, d_, :], lam_re[l][d_].rearrange("g q -> q g"), writes=[LR])
                        p.dma("sp", LI.ap[hs_, d_, :], lam_im[l][d_].rearrange("g q -> q g"), writes=[LI])
                        p.dma("sp", X0.ap[hs_, d_, :], (st_re if half == 0 else st_im)[l][d_].rearrange("g q -> q g"), writes=[X0])
                p.dma("sp", DT.ap.rearrange("q d g -> q (d g)"), log_dt[l].rearrange("d g -> (d g)").partition_broadcast(128), writes=[DT])
                p.op("act", lambda e: e.activation(out=DT.ap, in_=DT.ap, func=AF.Exp), reads=[DT], writes=[DT])
                p.op("dve", lambda e: e.tensor_tensor(out=AR.ap, in0=LR.ap, in1=DT.ap, op=ALU.mult), reads=[LR, DT], writes=[AR])
                p.op("dve", lambda e: e.tensor_tensor(out=TH.ap, in0=LI.ap, in1=DT.ap, op=ALU.mult), reads=[LI, DT], writes=[TH])
                KVt = t.sb([128, 16], F32, "KVt")
                KVm = t.sb([128, 16], F32, "KVm")
                p.dma("sp", KVt.ap, c_kv16[0].partition_broadcast(128), writes=[KVt])
                p.dma("sp", KVm.ap, c_kv16[1].partition_broadcast(128), writes=[KVm])
                PWR = t.sb([128, 16, 64], F32, "PWR")
                PWI = t.sb([128, 16, 64], F32, "PWI")
                F8t = t.sb([128, 2, 32], F32, "F8t")
                PAIRS = {}
                for nm_, sh_ in (("BReIm", [128, 2, 32, 16]), ("BImRe", [128, 2, 32, 16]), ("CReIm", [128, 32, 16]), ("CImRe", [128, 32, 16])):
                    PAIRS[nm_] = (t.sb(sh_, F32, nm_ + "a"), t.sb(sh_, F32, nm_ + "b"))
                esA = ExitStack()
                tA = Ctx(p, esA)
                MAG = tA.sb([128, 16, 64], F32, "MAG")
                TT = tA.sb([128, 16, 64], F32, "TT")
                TT2 = tA.sb([128, 16, 64], F32, "TT2")
                thb = TH.ap.rearrange("q d g -> q (d g)").unsqueeze(1).broadcast_to([128, 16, 64])
                arb = AR.ap.rearrange("q d g -> q (d g)").unsqueeze(1).broadcast_to([128, 16, 64])
                p.op("dve", lambda e: e.tensor_tensor(out=TT.ap, in0=thb, in1=KVt.ap.unsqueeze(2).broadcast_to([128, 16, 64]), op=ALU.mult),
                     reads=[TH, KVt], writes=[TT])
                p.op("dve", lambda e: e.tensor_tensor(out=MAG.ap, in0=arb, in1=KVm.ap.unsqueeze(2).broadcast_to([128, 16, 64]), op=ALU.mult),
                     reads=[AR, KVm], writes=[MAG])
                p.op("act", lambda e: e.activation(out=MAG.ap, in_=MAG.ap, func=AF.Exp), reads=[MAG], writes=[MAG])
                sincos_turns(TT, TT2, PWI, PWR)
                p.op("dve", lambda e: e.tensor_tensor(out=PWR.ap, in0=PWR.ap, in1=MAG.ap, op=ALU.mult), reads=[PWR, MAG], writes=[PWR])
                p.op("dve", lambda e: e.tensor_tensor(out=PWI.ap, in0=PWI.ap, in1=MAG.ap, op=ALU.mult), reads=[PWI, MAG], writes=[PWI])

                def pw(k):
                    return k + 7
                p.op("dve", lambda e: e.tensor_copy(out=RHO.ap.rearrange("q d g -> q (d g)"), in_=MAG.ap[:, pw(8), :]), reads=[MAG], writes=[RHO])
                p.op("dve", lambda e: e.tensor_scalar(out=F8.ap, in0=TH.ap, scalar1=8.0 / (2 * math.pi), scalar2=None, op0=ALU.mult),
                     reads=[TH], writes=[F8])
                rnd_frac(None, F8, F8t)
                p.barrier()
                esA.close()
                esB = ExitStack()
                tB = Ctx(p, esB)
                A1R = PWR.ap[:, pw(1), :]
                A1I = PWI.ap[:, pw(1), :]
                ZR = tB.sb([128, 64], F32, "ZR")
                ZI = tB.sb([128, 64], F32, "ZI")
                DEN = tB.sb([128, 64], F32, "DEN")
                TMPa = tB.sb([128, 64], F32, "TMPa")
                TMPb = tB.sb([128, 64], F32, "TMPb")
                lr2 = LR.ap.rearrange("q d g -> q (d g)")
                li2 = LI.ap.rearrange("q d g -> q (d g)")
                p.op("dve", lambda e: e.tensor_tensor(out=DEN.ap, in0=lr2, in1=lr2, op=ALU.mult), reads=[LR], writes=[DEN])
                p.op("dve", lambda e: e.tensor_tensor(out=TMPa.ap, in0=li2, in1=li2, op=ALU.mult), reads=[LI], writes=[TMPa])
                p.op("dve", lambda e: e.tensor_tensor(out=DEN.ap, in0=DEN.ap, in1=TMPa.ap, op=ALU.add), reads=[DEN, TMPa], writes=[DEN])
                p.op("dve", lambda e: e.reciprocal(out=DEN.ap, in_=DEN.ap), reads=[DEN], writes=[DEN])
                p.op("dve", lambda e: e.tensor_scalar(out=TMPa.ap, in0=A1R, scalar1=-1.0, scalar2=None, op0=ALU.add), reads=[PWR], writes=[TMPa])
                p.op("dve", lambda e: e.tensor_tensor(out=ZR.ap, in0=TMPa.ap, in1=lr2, op=ALU.mult), reads=[TMPa, LR], writes=[ZR])
                p.op("dve", lambda e: e.tensor_tensor(out=TMPb.ap, in0=A1I, in1=li2, op=ALU.mult), reads=[PWI, LI], writes=[TMPb])
                p.op("dve", lambda e: e.tensor_tensor(out=ZR.ap, in0=ZR.ap, in1=TMPb.ap, op=ALU.add), reads=[ZR, TMPb], writes=[ZR])
                p.op("dve", lambda e: e.tensor_tensor(out=ZR.ap, in0=ZR.ap, in1=DEN.ap, op=ALU.mult), reads=[ZR, DEN], writes=[ZR])
                p.op("dve", lambda e: e.tensor_tensor(out=ZI.ap, in0=A1I, in1=lr2, op=ALU.mult), reads=[PWI, LR], writes=[ZI])
                p.op("dve", lambda e: e.tensor_tensor(out=TMPb.ap, in0=TMPa.ap, in1=li2, op=ALU.mult), reads=[TMPa, LI], writes=[TMPb])
                p.op("dve", lambda e: e.tensor_tensor(out=ZI.ap, in0=ZI.ap, in1=TMPb.ap, op=ALU.subtract), reads=[ZI, TMPb], writes=[ZI])
                p.op("dve", lambda e: e.tensor_tensor(out=ZI.ap, in0=ZI.ap, in1=DEN.ap, op=ALU.mult), reads=[ZI, DEN], writes=[ZI])
                BR = tB.sb([128, 32, 16], F32, "BR")
                BI = tB.sb([128, 32, 16], F32, "BI")
                CR = tB.sb([128, 32, 16], F32, "CR")
                CI = tB.sb([128, 32, 16], F32, "CI")
                for half in range(2):
                    hs_ = slice(half * 64, half * 64 + 64)
                    p.dma("sp", BR.ap[hs_], b_re[l].rearrange("g q c -> q g c"), writes=[BR])
                    p.dma("sp", BI.ap[hs_], b_im[l].rearrange("g q c -> q g c"), writes=[BI])
                csrc = tB.sb([128, 4, 128], F32, "csrc")
                for (cin, cout) in ((c_re, CR), (c_im, CI)):
                    cv_ = cin[l].rearrange("g c q -> (g c) q").rearrange("(t r) q -> r t q", r=128)
                    p.dma("sp", csrc.ap[:, :, 0:64], cv_, writes=[csrc])
                    p.dma("sp", csrc.ap[:, :, 64:128], cv_, writes=[csrc])
                    pt = ps()
                    for tt in range(4):
                        p.op("pe", lambda e, pt=pt, tt=tt: e.transpose(out=pt.ap[:, tt * 128:(tt + 1) * 128], in_=csrc.ap[:, tt, :], identity=ident.ap),
                             reads=[csrc, ident], writes=[pt])
                    p.op("dve", lambda e, pt=pt, cout=cout: e.tensor_copy(out=cout.ap.rearrange("q g c -> q (g c)"), in_=pt.ap), reads=[pt], writes=[cout])
                ZBR = tB.sb([128, 2, 32, 16], F32, "ZBR")
                ZBI = tB.sb([128, 2, 32, 16], F32, "ZBI")
                TZ = tB.sb([128, 2, 32, 16], F32, "TZ")
                zrb = ZR.ap.rearrange("q (d g) -> q d g", d=2).unsqueeze(3).broadcast_to([128, 2, 32, 16])
                zib = ZI.ap.rearrange("q (d g) -> q d g", d=2).unsqueeze(3).broadcast_to([128, 2, 32, 16])
                brb = BR.ap.unsqueeze(1).broadcast_to([128, 2, 32, 16])
                bib = BI.ap.unsqueeze(1).broadcast_to([128, 2, 32, 16])
                p.op("dve", lambda e: e.tensor_tensor(out=ZBR.ap, in0=zrb, in1=brb, op=ALU.mult), reads=[ZR, BR], writes=[ZBR])
                p.op("dve", lambda e: e.tensor_tensor(out=TZ.ap, in0=zib, in1=bib, op=ALU.mult), reads=[ZI, BI], writes=[TZ])
                p.op("dve", lambda e: e.tensor_tensor(out=ZBR.ap, in0=ZBR.ap, in1=TZ.ap, op=ALU.subtract), reads=[ZBR, TZ], writes=[ZBR])
                p.op("dve", lambda e: e.tensor_tensor(out=ZBI.ap, in0=zrb, in1=bib, op=ALU.mult), reads=[ZR, BI], writes=[ZBI])
                p.op("dve", lambda e: e.tensor_tensor(out=TZ.ap, in0=zib, in1=brb, op=ALU.mult), reads=[ZI, BR], writes=[TZ])
                p.op("dve", lambda e: e.tensor_tensor(out=ZBI.ap, in0=ZBI.ap, in1=TZ.ap, op=ALU.add), reads=[ZBI, TZ], writes=[ZBI])

                def mk_pair(name, top_a, sa_top, bot_a, sa_bot, top_b, sb_top, bot_b, sb_bot, shape):
                    Ma, Mb = PAIRS[name]
                    for (M_, top, s_top, bot, s_bot) in ((Ma, top_a, sa_top, bot_a, sa_bot), (Mb, top_b, sb_top, bot_b, sb_bot)):
                        p.op("act", lambda e, M_=M_, top=top, s_top=s_top: e.mul(out=M_.ap[0:64], in_=top.ap[0:64], mul=float(s_top)), reads=[top], writes=[M_])
                        p.op("act", lambda e, M_=M_, bot=bot, s_bot=s_bot: e.mul(out=M_.ap[64:128], in_=bot.ap[64:128], mul=float(s_bot)), reads=[bot], writes=[M_])
                    return Ma, Mb
                shB = [128, 2, 32, 16]
                shC = [128, 32, 16]
                B_ReIm = mk_pair("BReIm", ZBR, 1, ZBI, 1, ZBI, -1, ZBR, 1, shB)
                B_ImmRe = mk_pair("BImRe", ZBI, 1, ZBR, -1, ZBR, 1, ZBI, 1, shB)
                C_RemIm = mk_pair("CReIm", CR, 1, CI, -1, CI, -1, CR, -1, shC)
                C_mImmRe = mk_pair("CImRe", CI, -1, CR, -1, CR, -1, CI, 1, shC)

                p.barrier()
                esB.close()
                GT1 = t.sb([128, 16, 8, 16], F32, "GT1")
                GT2 = t.sb([128, 16, 8, 16], F32, "GT2")

                def factor(out_t, pair, d_, ks, is_b):
                    Ma, Mb = pair
                    k0, kstep = ks
                    for gh in (0, 16):
                        def pv(PW, gh=gh):
                            v = PW.ap[:, :, d_ * 32 + gh:d_ * 32 + gh + 16]
                            i0 = pw(k0)
                            if kstep == 1:
                                v = v[:, i0:i0 + 8, :]
                            else:
                                v = v[:, i0 - 7:i0 + 1, :][:, ::-1, :]
                            return v.rearrange("q k g -> q g k").unsqueeze(3).broadcast_to([128, 16, 8, 16])
                        ma = (Ma.ap[:, d_] if is_b else Ma.ap)[:, gh:gh + 16].unsqueeze(2).broadcast_to([128, 16, 8, 16])
                        mb = (Mb.ap[:, d_] if is_b else Mb.ap)[:, gh:gh + 16].unsqueeze(2).broadcast_to([128, 16, 8, 16])
                        p.op("dve", lambda e, pv=pv, ma=ma: e.tensor_tensor(out=GT1.ap, in0=pv(PWR), in1=ma, op=ALU.mult), reads=[PWR, Ma], writes=[GT1])
                        p.op("pool", lambda e, pv=pv, mb=mb: e.tensor_tensor(out=GT2.ap, in0=pv(PWI), in1=mb, op=ALU.mult), reads=[PWI, Mb], writes=[GT2])
                        p.op("dve", lambda e, gh=gh: e.tensor_tensor(out=out_t.ap[:, gh:gh + 16, :].rearrange("q g (t c) -> q g t c", c=16),
                                                                    in0=GT1.ap, in1=GT2.ap, op=ALU.add), reads=[GT1, GT2], writes=[out_t])

                factor(Qm[0][0], C_RemIm, 0, (1, 1), False)
                factor(Qm[0][1], C_mImmRe, 0, (1, 1), False)
                factor(Qm[1][0], C_RemIm, 1, (8, -1), False)
                factor(Qm[1][1], C_mImmRe, 1, (8, -1), False)
                esT = ExitStack()
                tT = Ctx(p, esT)
                Lf = tT.sb([128, 32, 128], BF16, "Lf")
                Rf = tT.sb([128, 32, 128], BF16, "Rf")
                Rb = tT.sb([128, 32, 128], BF16, "Rb")
                Lb = tT.sb([128, 32, 128], BF16, "Lb")
                factor(Lf, B_ReIm, 0, (0, -1), True)
                factor(Rf, C_RemIm, 0, (0, 1), False)
                factor(Rb, C_RemIm, 1, (0, -1), False)
                factor(Lb, B_ReIm, 1, (0, 1), True)
                MFB = tT.sb([128, 2, 128], F32, "MFB")
                dcol = tT.sb([128, 32], F32, "dcol")
                p.dma("sp", MFB.ap, c_mfb, writes=[MFB])
                for s_ in range(8):
                    p.dma("sp", dcol.ap[s_ * 16:(s_ + 1) * 16, :], ssm_d[l].rearrange("(g c) -> c g", c=16), writes=[dcol])
                T1 = tT.sb([128, 4, 128], F32, "T1t")
                T2 = tT.sb([128, 4, 128], F32, "T2t")
                for g4 in range(0, 32, 4):
                    pf = ps()
                    pb = ps()
                    for j in range(4):
                        gg = g4 + j
                        js = slice(j * 128, (j + 1) * 128)
                        p.op("pe", lambda e, pf=pf, js=js, gg=gg: e.matmul(pf.ap[:, js], lhsT=Lf.ap[:, gg, :], rhs=Rf.ap[:, gg, :], start=True, stop=True),
                             reads=[Lf, Rf], writes=[pf])
                        p.op("pe", lambda e, pb=pb, js=js, gg=gg: e.matmul(pb.ap[:, js], lhsT=Lb.ap[:, gg, :], rhs=Rb.ap[:, gg, :], start=True, stop=True),
                             reads=[Lb, Rb], writes=[pb])
                    p.op("dve", lambda e, pf=pf: e.tensor_tensor(out=T1.ap, in0=pf.ap.rearrange("q (g c) -> q g c", g=4),
                                                                 in1=MFB.ap[:, 0, :].unsqueeze(1).broadcast_to([128, 4, 128]), op=ALU.mult),
                         reads=[pf, MFB], writes=[T1])
                    p.op("dve", lambda e, pb=pb: e.tensor_tensor(out=T2.ap, in0=pb.ap.rearrange("q (g c) -> q g c", g=4),
                                                                 in1=MFB.ap[:, 1, :].unsqueeze(1).broadcast_to([128, 4, 128]), op=ALU.mult),
                         reads=[pb, MFB], writes=[T2])
                    p.op("pool", lambda e: e.tensor_tensor(out=T1.ap, in0=T1.ap, in1=T2.ap, op=ALU.add), reads=[T1, T2], writes=[T1])
                    for j in range(4):
                        gg = g4 + j
                        p.op("dve", lambda e, j=j, gg=gg: e.scalar_tensor_tensor(out=Tg.ap[:, gg, :], in0=ident.ap, scalar=dcol.ap[:, gg:gg + 1],
                                                                                 in1=T1.ap[:, j, :], op0=ALU.mult, op1=ALU.add),
                             reads=[ident, dcol, T1], writes=[Tg])
                p.barrier()
                esT.close()
                PTr = [t.sb([128, 32, 128], BF16, "PTr") for _ in range(2)]
                pi_ = 0
                for d_, v_, pair, ks in ((0, 0, B_ReIm, (7, -1)), (0, 1, B_ImmRe, (7, -1)), (1, 0, B_ReIm, (0, 1)), (1, 1, B_ImmRe, (0, 1))):
                    ptt_ = PTr[pi_ % 2]
                    pi_ += 1
                    factor(ptt_, pair, d_, ks, True)
                    for g4 in range(0, 32, 4):
                        pt = ps()
                        ptb = pt.ap.bitcast(BF16)
                        for j in range(4):
                            p.op("pe", lambda e, ptb=ptb, j=j, g4=g4, ptt_=ptt_: e.transpose(
                                out=ptb[:, j * 128:(j + 1) * 128], in_=ptt_.ap[:, g4 + j, :], identity=identb.ap),
                                reads=[ptt_, identb], writes=[pt])
                        p.op("act", lambda e, ptb=ptb, g4=g4, d_=d_, v_=v_: e.copy(
                            out=Pm[d_][v_].ap[:, g4:g4 + 4, :].rearrange("q g c -> q (g c)"), in_=ptb[:, 0:512]), reads=[pt], writes=[Pm[d_][v_]])
                p.barrier()

            Vp = c.sb([128, 32, NCP], BF16, "Vp")
            Vs = c.sb([128, 32, NCS], BF16, "Vs")
            ubv = UB.ap.rearrange("s c g n -> (s c) g n")
            p.dma("sp", Vp.ap, ubv[:, :, 0:NCP], reads=[UB], writes=[Vp])
            p.dma("sp", Vs.ap, ubv[:, :, NCP:NCP + NCS], reads=[UB], writes=[Vs])
            YSTp = c.sb([128, 32, NCP], F32, "YSTp")
            TAB = [c.sb([128, 2, 513], F32, "TAB") for _ in range(4)]
            TBt = [c.sb([128, 513], F32, "TBt") for _ in range(2)]
            TBu = [c.sb([128, 513], F32, "TBu") for _ in range(2)]
            M1 = [c.sb([128, 512], F32, "M1b") for _ in range(2)]
            M2 = [c.sb([128, 512], F32, "M2b") for _ in range(2)]
            Wx = [c.sb([128, 513], F32, "Wx") for _ in range(2)]
            Wxp = [c.sb([128, 4, 33], F32, "Wxp") for _ in range(2)]
            Ab = [c.sb([128, 512], BF16, "Ab") for _ in range(2)]
            Bb = [c.sb([128, 512], BF16, "Bb") for _ in range(2)]
            Abp = [c.sb([128, 4, 32], BF16, "Abp") for _ in range(2)]
            Bbp = [c.sb([128, 4, 32], BF16, "Bbp") for _ in range(2)]
            FINA = c.sb([128, 4, 2, 32], F32, "FINA")
            FINB = c.sb([128, 4, 2, 32], F32, "FINB")
            YSTs = [c.sb([128, NCS], F32, "YSTs") for _ in range(2)]
            yv = YSCR.ap.rearrange("s c g n -> (s c) g n")
            it = 0
            pstate["n"] = 4

            def lockstep_b(gens):
                alive = list(gens)
                while alive:
                    for gg_ in list(alive):
                        try:
                            next(gg_)
                        except StopIteration:
                            alive.remove(gg_)

            def tab_gen(g_, d_, r2):
                    tab, tbt, tbu, m1, m2, wx, wxp, ab, bb, abp, bbp = TAB[r2], TBt[d_], TBu[d_], M1[d_], M2[d_], Wx[d_], Wxp[d_], Ab[d_], Bb[d_], Abp[d_], Bbp[d_]
                    p.op("dve", lambda e, tbt=tbt, d_=d_, g_=g_: e.tensor_scalar(out=tbt.ap, in0=KV513.ap, scalar1=F8.ap[:, d_, g_:g_ + 1], scalar2=None,
                                                                                 op0=ALU.mult), reads=[KV513, F8], writes=[tbt])
                    p.op("dve", lambda e, tbt=tbt, tbu=tbu: e.tensor_scalar(out=tbu.ap, in0=tbt.ap, scalar1=MAGIC, scalar2=None, op0=ALU.add), reads=[tbt], writes=[tbu])
                    p.op("dve", lambda e, tbu=tbu: e.tensor_scalar(out=tbu.ap, in0=tbu.ap, scalar1=-MAGIC, scalar2=None, op0=ALU.add), reads=[tbu], writes=[tbu])
                    p.op("pool", lambda e, tbt=tbt, tbu=tbu: e.tensor_tensor(out=tbt.ap, in0=tbt.ap, in1=tbu.ap, op=ALU.subtract), reads=[tbt, tbu], writes=[tbt])
                    yield
                    p.op("act", lambda e, tab=tab, tbt=tbt: e.activation(out=tab.ap[:, 0, :], in_=tbt.ap, func=AF.Sin, scale=TWO_PI_LO), reads=[tbt], writes=[tab])
                    p.op("dve", lambda e, tbt=tbt, tbu=tbu: e.scalar_tensor_tensor(out=tbu.ap, in0=tbt.ap, scalar=-1.0, in1=tbt.ap, op0=ALU.mult, op1=ALU.max),
                         reads=[tbt], writes=[tbu])
                    p.op("act", lambda e, tab=tab, tbu=tbu: e.activation(out=tab.ap[:, 1, :], in_=tbu.ap, func=AF.Sin, scale=-TWO_PI_LO, bias=hpi.ap[:, 0:1]),
                         reads=[tbu, hpi], writes=[tab])

            def main_gen(g_, d_, r2, pyp, pys):
                    tab, tbt, tbu, m1, m2, wx, wxp, ab, bb, abp, bbp = TAB[r2], TBt[d_], TBu[d_], M1[d_], M2[d_], Wx[d_], Wxp[d_], Ab[d_], Bb[d_], Abp[d_], Bbp[d_]
                    last = (d_ == 1)
                    rho_b = RHO.ap[:, d_, g_:g_ + 1]
                    for grp in range(2):
                        V, ncols, py = (Vp, NCP, pyp) if grp == 0 else (Vs, NCS, pys)
                        px = ps()
                        pxs = ps()
                        p.op("pe", lambda e, px=px, V=V, ncols=ncols, d_=d_, g_=g_: e.matmul(px.ap[:, :ncols], lhsT=Pm[d_][0].ap[:, g_, :], rhs=V.ap[:, g_, :],
                                                                                             start=True, stop=True), reads=[Pm[d_][0], V], writes=[px])
                        p.op("pe", lambda e, pxs=pxs, V=V, ncols=ncols, d_=d_, g_=g_: e.matmul(pxs.ap[:, :ncols], lhsT=Pm[d_][1].ap[:, g_, :], rhs=V.ap[:, g_, :],
                                                                                               start=True, stop=True), reads=[Pm[d_][1], V], writes=[pxs])
                        yield
                        if grp == 0:
                            nseq, nc1 = 4, 32
                        else:
                            nseq, nc1 = 1, NCS
                        xv = px.ap[:, :ncols].rearrange("q (s n) -> q s n", s=nseq)
                        xsv = pxs.ap[:, :ncols].rearrange("q (s n) -> q s n", s=nseq)
                        if d_ == 1:
                            xv = xv[:, :, ::-1]
                            xsv = xsv[:, :, ::-1]
                        cosm = tab.ap[:, 1, 1:nc1 + 1].unsqueeze(1).broadcast_to([128, nseq, nc1])
                        sinm = tab.ap[:, 0, 1:nc1 + 1].unsqueeze(1).broadcast_to([128, nseq, nc1])
                        m1v = m1.ap[:, :ncols].rearrange("q (s n) -> q s n", s=nseq)
                        m2v = m2.ap[:, :ncols].rearrange("q (s n) -> q s n", s=nseq)
                        p.op("dve", lambda e, m1v=m1v, xv=xv, cosm=cosm: e.tensor_tensor(out=m1v, in0=xv, in1=cosm, op=ALU.mult), reads=[px, tab], writes=[m1])
                        p.op("dve", lambda e, m2v=m2v, xsv=xsv, sinm=sinm: e.tensor_tensor(out=m2v, in0=xsv, in1=sinm, op=ALU.mult), reads=[pxs, tab], writes=[m2])
                        p.op("pool", lambda e, m1v=m1v, m2v=m2v: e.tensor_tensor(out=m1v, in0=m1v, in1=m2v, op=ALU.add), reads=[m1, m2], writes=[m1])
                        yield
                        if grp == 0:
                            p.op("pool", lambda e, wxp=wxp: e.memset(wxp.ap[:, :, 0:1], 0.0), writes=[wxp])
                            for s_ in range(4):
                                p.op("dve", lambda e, wxp=wxp, m1v=m1v, s_=s_: e.tensor_tensor_scan(
                                    out=wxp.ap[:, s_, 1:33], data0=rho_b.broadcast_to([128, 32]), data1=m1v[:, s_, :], initial=0.0,
                                    op0=ALU.mult, op1=ALU.add), reads=[m1, RHO], writes=[wxp])
                            yield
                            wsrc = wxp.ap[:, :, 0:32]
                            if d_ == 1:
                                wsrc = wsrc[:, :, ::-1]
                                cd = tab.ap[:, 1, 0:32][:, ::-1].unsqueeze(1).broadcast_to([128, 4, 32])
                                sd = tab.ap[:, 0, 0:32][:, ::-1].unsqueeze(1).broadcast_to([128, 4, 32])
                            else:
                                cd = tab.ap[:, 1, 0:32].unsqueeze(1).broadcast_to([128, 4, 32])
                                sd = tab.ap[:, 0, 0:32].unsqueeze(1).broadcast_to([128, 4, 32])
                            p.op("dve", lambda e, abp=abp, wsrc=wsrc, cd=cd: e.tensor_tensor(out=abp.ap, in0=wsrc, in1=cd, op=ALU.mult), reads=[wxp, tab], writes=[abp])
                            p.op("pool", lambda e, bbp=bbp, wsrc=wsrc, sd=sd: e.tensor_tensor(out=bbp.ap, in0=wsrc, in1=sd, op=ALU.mult), reads=[wxp, tab], writes=[bbp])
                            p.op("dve", lambda e, wxp=wxp, tab=tab, d_=d_, g_=g_: e.tensor_scalar(
                                out=FINA.ap[:, :, d_, g_], in0=wxp.ap[:, :, 32], scalar1=tab.ap[:, 1, 32:33], scalar2=None, op0=ALU.mult),
                                reads=[wxp, tab], writes=[FINA])
                            p.op("dve", lambda e, wxp=wxp, tab=tab, d_=d_, g_=g_: e.tensor_scalar(
                                out=FINB.ap[:, :, d_, g_], in0=wxp.ap[:, :, 32], scalar1=tab.ap[:, 0, 32:33], scalar2=None, op0=ALU.mult),
                                reads=[wxp, tab], writes=[FINB])
                            arhs, brhs = abp.ap.rearrange("q s n -> q (s n)"), bbp.ap.rearrange("q s n -> q (s n)")
                        else:
                            p.op("pool", lambda e, wx=wx, d_=d_, g_=g_: e.tensor_copy(out=wx.ap[:, 0:1], in_=X0.ap[:, d_, g_:g_ + 1]), reads=[X0], writes=[wx])
                            p.op("dve", lambda e, wx=wx, m1=m1, d_=d_, g_=g_: e.tensor_tensor_scan(
                                out=wx.ap[:, 1:NCS + 1], data0=rho_b.broadcast_to([128, NCS]), data1=m1.ap[:, :NCS], initial=X0.ap[:, d_, g_:g_ + 1],
                                op0=ALU.mult, op1=ALU.add), reads=[m1, RHO, X0], writes=[wx])
                            yield
                            wsrc = wx.ap[:, 0:NCS]
                            cd = tab.ap[:, 1, 0:NCS]
                            sd = tab.ap[:, 0, 0:NCS]
                            if d_ == 1:
                                wsrc, cd, sd = wsrc[:, ::-1], cd[:, ::-1], sd[:, ::-1]
                            p.op("dve", lambda e, ab=ab, wsrc=wsrc, cd=cd: e.tensor_tensor(out=ab.ap, in0=wsrc, in1=cd, op=ALU.mult), reads=[wx, tab], writes=[ab])
                            p.op("pool", lambda e, bb=bb, wsrc=wsrc, sd=sd: e.tensor_tensor(out=bb.ap, in0=wsrc, in1=sd, op=ALU.mult), reads=[wx, tab], writes=[bb])
                            arhs, brhs = ab.ap, bb.ap
                        yield
                        p.op("pe", lambda e, py=py, ncols=ncols, arhs=arhs, d_=d_, g_=g_: e.matmul(py.ap[:, :ncols], lhsT=Qm[d_][0].ap[:, g_, :], rhs=arhs,
                                                                                                   start=False, stop=False),
                             reads=[Qm[d_][0], abp if grp == 0 else ab], writes=[py])
                        p.op("pe", lambda e, py=py, ncols=ncols, brhs=brhs, d_=d_, g_=g_, last=last: e.matmul(py.ap[:, :ncols], lhsT=Qm[d_][1].ap[:, g_, :], rhs=brhs,
                                                                                                              start=False, stop=last),
                             reads=[Qm[d_][1], bbp if grp == 0 else bb], writes=[py])

            lockstep_b([tab_gen(0, d_, d_) for d_ in range(2)])
            for g_ in range(32):
                pyp = psum[4 + 2 * (g_ % 2)]
                pys = psum[5 + 2 * (g_ % 2)]
                p.op("pe", lambda e, pyp=pyp, g_=g_: e.matmul(pyp.ap[:, :NCP], lhsT=Tg.ap[:, g_, :], rhs=Vp.ap[:, g_, :], start=True, stop=False),
                     reads=[Tg, Vp], writes=[pyp])
                p.op("pe", lambda e, pys=pys, g_=g_: e.matmul(pys.ap[:, :NCS], lhsT=Tg.ap[:, g_, :], rhs=Vs.ap[:, g_, :], start=True, stop=False),
                     reads=[Tg, Vs], writes=[pys])
                gens = [main_gen(g_, d_, (g_ % 2) * 2 + d_, pyp, pys) for d_ in range(2)]
                if g_ + 1 < 32:
                    gens += [tab_gen(g_ + 1, d_, ((g_ + 1) % 2) * 2 + d_) for d_ in range(2)]
                lockstep_b(gens)
                p.op("act", lambda e, pyp=pyp, g_=g_: e.copy(out=YSTp.ap[:, g_, :], in_=pyp.ap[:, :NCP]), reads=[pyp], writes=[YSTp])
                yst = YSTs[g_ % 2]
                p.op("act", lambda e, pys=pys, yst=yst: e.copy(out=yst.ap, in_=pys.ap[:, :NCS]), reads=[pys], writes=[yst])
                p.dma("pool", yv[:, g_, NCP:NCP + NCS], yst.ap, reads=[yst], writes=[YSCR])
            p.dma("pool", yv[:, :, 0:NCP], YSTp.ap, reads=[YSTp], writes=[YSCR])
            pstate["n"] = 8
            SWP = c.sb([128, 128], F32, "SWP")
            p.dma("sp", SWP.ap, c_swp, writes=[SWP])
            pfin = ps()
            fa = FINA.ap.rearrange("q s d g -> q (s d g)")
            fb = FINB.ap.rearrange("q s d g -> q (s d g)")
            p.op("pe", lambda e: e.matmul(pfin.ap[:, :256], lhsT=ident.ap, rhs=fa, start=True, stop=False), reads=[ident, FINA], writes=[pfin])
            p.op("pe", lambda e: e.matmul(pfin.ap[:, :256], lhsT=SWP.ap, rhs=fb, start=False, stop=True), reads=[SWP, FINB], writes=[pfin])
            XF = c.sb([128, 256], F32, "XF")
            p.op("dve", lambda e: e.tensor_copy(out=XF.ap, in_=pfin.ap[:, :256]), reads=[pfin], writes=[XF])
            pft = ps()
            for hh in range(2):
                p.op("pe", lambda e, hh=hh: e.transpose(out=pft.ap[:, hh * 128:(hh + 1) * 128], in_=XF.ap[:, hh * 128:(hh + 1) * 128], identity=ident.ap),
                     reads=[XF, ident], writes=[pft])
            XFT = c.sb([128, 2, 128], F32, "XFT")
            p.op("dve", lambda e: e.tensor_copy(out=XFT.ap.rearrange("q a b -> q (a b)"), in_=pft.ap[:, :256]), reads=[pft], writes=[XFT])
            for s_ in range(4):
                hh, s2 = s_ // 2, s_ % 2
                p.dma("pool", ns_re.ap[s_, l].rearrange("d g q -> (d g) q"), XFT.ap[s2 * 64:(s2 + 1) * 64, hh, 0:64], reads=[XFT], writes=[ns_re])
                p.dma("pool", ns_im.ap[s_, l].rearrange("d g q -> (d g) q"), XFT.ap[s2 * 64:(s2 + 1) * 64, hh, 64:128], reads=[XFT], writes=[ns_im])
            p.barrier()

        with ExitStack() as es:
            c = Ctx(p, es)
            WGL = c.sb([128, 4, 512], BF16, "WGL")
            bgl = c.sb([128, 4], F32, "bgl")
            p.dma("sp", WGL.ap, WB_glu.ap[l].rearrange("(k q) n -> q k n", q=128), reads=[WB_glu], writes=[WGL])
            p.dma("sp", bgl.ap, b_glu[l].rearrange("(j q) -> q j", q=128), writes=[bgl])
            YL = [c.sb([128, 4, 8, 64], F32, "YL") for _ in range(2)]
            YF = [c.sb([128, 4, 512], F32, "YF") for _ in range(2)]
            YQ = [c.sb([128, 4, 512], F32, "YQ") for _ in range(2)]
            YG = [c.sb([128, 4, 512], F32, "YG") for _ in range(2)]
            YGb = [c.sb([128, 4, 512], BF16, "YGb") for _ in range(2)]
            SG = [c.sb([128, 512], F32, "SGb") for _ in range(2)]
            YO = [c.sb([128, 4, 512], BF16, "YO") for _ in range(2)]
            ybv = YB.ap.rearrange("(k q) t -> q k t", q=128)
            K0 = 2.0 * math.sqrt(2.0 / math.pi)
            for bi, t0 in enumerate(range(0, TTOT, 512)):
                yl, yf, yq, yg, ygb, yo = YL[bi % 2], YF[bi % 2], YQ[bi % 2], YG[bi % 2], YGb[bi % 2], YO[bi % 2]
                n0 = t0 // 8
                for gt_ in range(4):
                    for gl in range(8):
                        p.dma("sp", yl.ap[gl * 16:(gl + 1) * 16, gt_, :, :],
                              YSCR.ap[:, :, gt_ * 8 + gl, n0:n0 + 64].rearrange("s c n -> c s n"), reads=[YSCR], writes=[yl])
                for gt_ in range(4):
                    evn = (gt_ % 2 == 0)
                    if evn:
                        p.op("act", lambda e, gt_=gt_: e.copy(out=yf.ap[:, gt_, :].rearrange("q (n s) -> q s n", s=8), in_=yl.ap[:, gt_, :, :]),
                             reads=[yl], writes=[yf])
                    else:
                        p.op("pool", lambda e, gt_=gt_: e.tensor_copy(out=yf.ap[:, gt_, :].rearrange("q (n s) -> q s n", s=8), in_=yl.ap[:, gt_, :, :]),
                             reads=[yl], writes=[yf])
                p.op("act", lambda e: e.activation(out=yq.ap, in_=yf.ap, func=AF.Square), reads=[yf], writes=[yq])
                p.op("dve", lambda e: e.tensor_scalar(out=yq.ap, in0=yq.ap, scalar1=0.044715, scalar2=1.0, op0=ALU.mult, op1=ALU.add), reads=[yq], writes=[yq])
                p.op("pool", lambda e: e.tensor_tensor(out=yq.ap, in0=yq.ap, in1=yf.ap, op=ALU.mult), reads=[yq, yf], writes=[yq])
                p.op("act", lambda e: e.activation(out=yq.ap, in_=yq.ap, func=AF.Sigmoid, scale=K0), reads=[yq], writes=[yq])
                p.op("dve", lambda e: e.tensor_tensor(out=yg.ap, in0=yq.ap, in1=yf.ap, op=ALU.mult), reads=[yq, yf], writes=[yg])
                p.op("pool", lambda e: e.tensor_copy(out=ygb.ap, in_=yg.ap), reads=[yg], writes=[ygb])
                for j in range(4):
                    pt = ps()
                    for k in range(4):
                        p.op("pe", lambda e, pt=pt, j=j, k=k: e.matmul(pt.ap, lhsT=WGL.ap[:, k, j * 128:(j + 1) * 128], rhs=ygb.ap[:, k, :],
                                                                       start=(k == 0), stop=(k == 3)), reads=[WGL, ygb], writes=[pt])
                    sg = SG[j % 2]
                    p.op("act", lambda e, pt=pt, sg=sg, j=j: e.activation(out=sg.ap, in_=pt.ap, func=AF.Sigmoid, bias=bgl.ap[:, j:j + 1]),
                         reads=[pt, bgl], writes=[sg])
                    p.op("dve", lambda e, sg=sg, j=j: e.tensor_tensor(out=yo.ap[:, j, :], in0=sg.ap, in1=yg.ap[:, j, :], op=ALU.mult), reads=[sg, yg], writes=[yo])
                p.dma("pool", ybv[:, :, t0:t0 + 512], yo.ap, reads=[yo], writes=[YB])
        p.barrier()

    def stage_mix_stub(l):
        p.dma("sp", OA.ap, ZA.ap, reads=[ZA], writes=[OA])
        p.dma("sp", YB.ap, GC.ap, reads=[GC], writes=[YB])
        p.dma("sp", OC.ap, QC.ap, reads=[QC], writes=[OC])
        p.barrier()

    nlayers = dbg_opts.get("nlayers", DEPTH)
    xi = 0
    for l in range(nlayers):
        es_l = ExitStack()
        lc = Ctx(p, es_l)
        mod = stage_mod(l, lc)
        stage_A(l, xi, mod)
        if dbg_opts.get("stub_mix"):
            stage_mix_stub(l)
        else:
            mixs = dbg_opts.get("mix", "ABC")
            if "C" in mixs:
                mix_C(l)
            if "A" in mixs:
                mix_A(l)
            if "B" in mixs:
                mix_B(l)
        if dbg_opts.get("stop_after_mix"):
            es_l.close()
            break
        stage_C1(l, xi, 1 - xi, mod)
        stage_C2(l, 1 - xi, xi, mod)
        es_l.close()
        p.barrier()
    stage_output(xi)
    p.finish()
    es0.close()
    print("n instructions", p.ninst)
    return nc


def host_consts():
    ident = np.eye(128, dtype=np.float32)
    t = np.arange(LS)
    rows = (t // 64).astype(np.float32)
    cols = (t % 64).astype(np.float32)
    freqs = (10000.0 ** (-np.arange(32, dtype=np.float32) / 32)).astype(np.float32)
    ang = np.concatenate([rows[None, :] * freqs[:, None], cols[None, :] * freqs[:, None]], axis=0)
    ang = ang.astype(np.float32)
    cos = np.cos(ang).astype(np.float32)
    sin = np.sin(ang).astype(np.float32)
    rope = np.stack([np.concatenate([cos, cos], 0), np.concatenate([sin, sin], 0)], 0).astype(np.float32)
    gam = 1.0 - 2.0 ** (-5.0 - np.arange(4))
    gamb = gam[::-1]
    C = 128
    ii = np.arange(C)
    dist = ii[None, :] - ii[:, None]
    rdt = np.zeros((C, 4, C), np.float64)
    for h in range(4):
        rdt[:, h, :] = np.where(dist > 0, gam[h] ** np.maximum(dist, 0), 0.0) + np.where(dist < 0, gamb[h] ** np.maximum(-dist, 0), 0.0) \
            + np.where(dist == 0, 2.0, 0.0)
    rqd = np.zeros((128, 2, 4, C), np.float64)
    rkd = np.zeros((C, 2, 4, 128), np.float64)
    for h in range(4):
        rqd[:, 0, h, :] = (gam[h] ** (ii + 1.0))[None, :]
        rqd[:, 1, h, :] = (gamb[h] ** (C - ii * 1.0))[None, :]
        rkd[:, 0, h, :] = (gam[h] ** (C - 1.0 - ii))[:, None]
        rkd[:, 1, h, :] = (gamb[h] ** (ii * 1.0))[:, None]
    ut = np.zeros((C, 2, C), np.float32)
    ut[:, 0, :] = (ii[:, None] <= ii[None, :])
    ut[:, 1, :] = (ii[:, None] >= ii[None, :])
    BIG = -1.0e6
    nm = np.zeros((C, 4, C), np.float32)
    nm[:, 0, :] = np.where(ii[:, None] <= ii[None, :], 0.0, BIG)
    nm[:, 1, :] = np.where(ii[:, None] >= ii[None, :], 0.0, BIG)
    nm[:, 2, :] = np.where(ii[:, None] > ii[None, :], 0.0, BIG)
    nm[:, 3, :] = np.where(ii[:, None] < ii[None, :], 0.0, BIG)
    sel = np.zeros((4, 4, 128), np.float32)
    for h in range(4):
        sel[h, h, :] = 1.0
    nlm = np.zeros((C, 2, 7, C), np.float32)
    for lev in range(7):
        b = 1 << lev
        mrow = ii[:, None]
        ccol = ii[None, :]
        mk = ((mrow // (2 * b)) == (ccol // (2 * b))) & ((mrow % (2 * b)) < b) & ((ccol % (2 * b)) >= b)
        nlm[:, 0, lev, :] = mk
        nlm[:, 1, lev, :] = mk.T
    kv513 = np.broadcast_to(np.arange(513, dtype=np.float32)[None, :], (128, 513)).copy()
    ks = np.arange(-7, 9, dtype=np.float64)
    kv16 = np.stack([ks / (2 * np.pi), ks], 0).astype(np.float32)
    sI = (ii // 16)[:, None]
    tI = (ii // 16)[None, :]
    mfb = np.stack([(tI >= sI), (sI >= tI)], 1).astype(np.float32)
    swp = np.zeros((128, 128), np.float32)
    for q in range(64):
        swp[64 + q, q] = -1.0
        swp[q, 64 + q] = 1.0
    extra = {"c_ut": ut, "c_nm": nm, "c_sel": sel, "c_nlm": nlm, "c_kv513": kv513, "c_kv16": kv16, "c_mfb": mfb, "c_swp": swp}
    return {**extra, "c_ident": ident, "c_rope": rope, "c_rdt": rdt.astype(np.float32), "c_rqd": rqd.astype(np.float32),
            "c_rkd": rkd.reshape(C, 2, 512).astype(np.float32)}


_CACHE = {}


def run_raw(inp, debug=(), dbg_opts=None):
    inp = {k: np.asarray(v) for k, v in inp.items()}
    key = tuple(sorted(debug))
    if key not in _CACHE:
        _CACHE[key] = build(debug, dbg_opts)
    nc = _CACHE[key]
    consts = host_consts()
    in_maps = []
    for core in range(8):
        b = core // 4
        m = {}
        m["xin"] = np.ascontiguousarray(np.concatenate(
            [inp["x_prompt"][core * 4:(core + 1) * 4].reshape(NSEQ_P * LP, D), inp["x_sample"][b]], axis=0))
        m["cond"] = np.ascontiguousarray(np.stack([inp["c_ctx"], inp["c"][b]], 0))
        m["st_delta"] = np.ascontiguousarray(inp["state_delta"][b])
        m["st_re"] = np.ascontiguousarray(inp["state_ssm_re"][b])
        m["st_im"] = np.ascontiguousarray(inp["state_ssm_im"][b])
        m["st_ret"] = np.ascontiguousarray(inp["state_ret"][b])
        for k in ("final_norm", "norm1", "norm2", "w_mod", "b_mod", "w_in", "w_conv_qkv", "norm_a", "w_br_a",
                  "ssm_lam_re", "ssm_lam_im", "ssm_log_dt", "ssm_b_re", "ssm_b_im", "ssm_c_re", "ssm_c_im",
                  "ssm_d", "w_glu", "b_glu", "w_br_b", "w_br_c", "w_o", "w_up", "w_conv_ffn", "b_conv_ffn",
                  "w_down"):
            m[k] = inp[k]
        m["a_log"] = np.ascontiguousarray(inp["a_log"].reshape(DEPTH, 8))
        m["dt_bias"] = np.ascontiguousarray(inp["dt_bias"].reshape(DEPTH, 8))
        m.update(consts)
        in_maps.append(m)
    res = run_bass_kernel_spmd(nc, in_maps, core_ids=list(range(8)))
    return res.results


def kernel(**inp):
    R = run_raw(inp)
    y_prompt = np.concatenate([R[c]["yout"][:NSEQ_P * LP].reshape(NSEQ_P, LP, D) for c in range(8)], 0)
    y_sample = np.stack([R[0]["yout"][NSEQ_P * LP:], R[4]["yout"][NSEQ_P * LP:]], 0)
    nsd = np.concatenate([R[c]["ns_delta"] for c in range(8)], 0)
    nsre = np.concatenate([R[c]["ns_re"] for c in range(8)], 0)
    nsim = np.concatenate([R[c]["ns_im"] for c in range(8)], 0)
    nsr = np.concatenate([R[c]["ns_ret"] for c in range(8)], 0)
    return (y_prompt.astype(np.float32), y_sample.astype(np.float32), nsd.astype(np.float32),
            nsre.astype(np.float32), nsim.astype(np.float32), nsr.astype(np.float32))
```
